# Optimizing a Trainium2 kernel written in Bass

```python
import jax, jax.numpy as jnp
from jax import lax
import numpy as np

D_MODEL = 2048
BATCH = 4
SEQ = 2048
DEPTH = 1

D_MIX = D_MODEL
D_CONV = D_MIX // 2
D_MLSTM = D_MIX - D_CONV
N_MLSTM_HEADS = 4
DV_HEAD = D_MLSTM // N_MLSTM_HEADS
DQK_HEAD = DV_HEAD // 2
D_QK = N_MLSTM_HEADS * DQK_HEAD
CONV_WIDTH = 3
CHUNK = 64
D_FF = 5632
D_PLE = 256
EPS = 1e-6
D_IN_PROJ = 3 * D_CONV + 2 * D_QK + 2 * D_MLSTM + 2 * N_MLSTM_HEADS

kernel_name = "hymba_style_shortconv_mlstm_convffn_ple"


def rmsnorm(x, g):
    xf = x.astype(jnp.float32)
    y = xf * lax.rsqrt(jnp.mean(xf * xf, axis=-1, keepdims=True) + EPS)
    return (y * g.astype(jnp.float32)).astype(x.dtype)


def causal_dwconv3(u, w):
    S = u.shape[1]
    up = jnp.pad(u, ((0, 0), (CONV_WIDTH - 1, 0), (0, 0)))
    y = up[:, 0:S] * w[0]
    for j in range(1, CONV_WIDTH):
        y = y + up[:, j:j + S] * w[j]
    return y


def mlstm_chunkwise(q, k, v, ig, fg):
    Bn, H, S, dk = q.shape
    dv = v.shape[-1]
    nc = S // CHUNK

    def to_chunks(a):
        a = a.astype(jnp.float32).reshape((Bn, H, nc, CHUNK) + a.shape[3:])
        return jnp.moveaxis(a, 2, 0)

    logf = jax.nn.log_sigmoid(fg.astype(jnp.float32))
    logi = ig.astype(jnp.float32)
    xs = (to_chunks(q), to_chunks(k), to_chunks(v), to_chunks(logf), to_chunks(logi))
    causal = jnp.tril(jnp.ones((CHUNK, CHUNK), dtype=bool))

    def step(carry, xs_c):
        C, n, m = carry
        qb, kb, vb, lf, li = xs_c
        b = jnp.cumsum(lf, axis=-1)
        dlog = b[..., :, None] - b[..., None, :] + li[..., None, :]
        dlog = jnp.where(causal, dlog, -jnp.inf)
        inter = b + m[..., None]
        m_t = jnp.maximum(inter, jnp.max(dlog, axis=-1))
        dw = jnp.exp(dlog - m_t[..., None])
        a_inter = jnp.exp(inter - m_t)
        s = jnp.einsum('bhtd,bhsd->bhts', qb, kb) * dw
        num = (a_inter[..., None] * jnp.einsum('bhvd,bhtd->bhtv', C, qb)
               + jnp.einsum('bhts,bhsv->bhtv', s, vb))
        den = a_inter * jnp.einsum('bhd,bhtd->bht', n, qb) + jnp.sum(s, axis=-1)
        den = jnp.maximum(jnp.abs(den), jnp.exp(-m_t))
        h = num / den[..., None]
        b_tot = b[..., -1]
        wlog = b_tot[..., None] - b + li
        m_new = jnp.maximum(b_tot + m, jnp.max(wlog, axis=-1))
        a_prev = jnp.exp(b_tot + m - m_new)
        w = jnp.exp(wlog - m_new[..., None])
        C_new = a_prev[..., None, None] * C + jnp.einsum('bhs,bhsv,bhsd->bhvd', w, vb, kb)
        n_new = a_prev[..., None] * n + jnp.einsum('bhs,bhsd->bhd', w, kb)
        return (C_new, n_new, m_new), h

    init = (jnp.zeros((Bn, H, dv, dk), jnp.float32),
            jnp.zeros((Bn, H, dk), jnp.float32),
            jnp.zeros((Bn, H), jnp.float32))
    _, hs = lax.scan(step, init, xs)
    return jnp.moveaxis(hs, 0, 2).reshape(Bn, H, S, dv)


def hybrid_mixer(h, w_in, b_igate, b_fgate, short_conv_w, mh_norm_g, w_out):
    Bn, S, _ = h.shape
    proj = h @ w_in
    idx = np.cumsum([D_CONV, D_CONV, D_CONV, D_QK, D_QK, D_MLSTM, D_MLSTM, N_MLSTM_HEADS]).tolist()
    gb, gc, u, q, k, v, o, ig, fg = jnp.split(proj, idx, axis=-1)
    y_conv = gb * causal_dwconv3(gc * u, short_conv_w)
    def heads(a, d):
        return a.reshape(Bn, S, N_MLSTM_HEADS, d).transpose(0, 2, 1, 3)
    qh = heads(q, DQK_HEAD) * (DQK_HEAD ** -0.5)
    kh = heads(k, DQK_HEAD)
    vh = heads(v, DV_HEAD)
    igh = (ig.astype(jnp.float32) + b_igate.astype(jnp.float32)).transpose(0, 2, 1)
    fgh = (fg.astype(jnp.float32) + b_fgate.astype(jnp.float32)).transpose(0, 2, 1)
    hm = mlstm_chunkwise(qh, kh, vh, igh, fgh).transpose(0, 2, 1, 3)
    hm = hm * lax.rsqrt(jnp.mean(hm * hm, axis=-1, keepdims=True) + EPS)
    hm = hm.reshape(Bn, S, D_MLSTM) * mh_norm_g.astype(jnp.float32)
    y_m = (jax.nn.sigmoid(o.astype(jnp.float32)) * hm).astype(h.dtype)
    return jnp.concatenate([y_conv, y_m], axis=-1) @ w_out


def conv_ffn(h, w_up, ffn_conv_w, ffn_conv_b, w_down):
    up = causal_dwconv3(h @ w_up, ffn_conv_w) + ffn_conv_b
    g, u = jnp.split(up, 2, axis=-1)
    return (jax.nn.silu(g) * u) @ w_down


def setup_inputs(seed: int = 0) -> dict:
    key = jax.random.key(seed)
    ks = jax.random.split(key, 20)
    f32 = jnp.float32
    nrm = lambda k, shape, s: jax.random.normal(k, shape, f32) * s
    H = N_MLSTM_HEADS
    return {
        "x": nrm(ks[0], (BATCH, SEQ, D_MODEL), 1.0),
        "p": nrm(ks[1], (DEPTH, BATCH, SEQ, D_PLE), 1.0),
        "norm_mix_g": 1.0 + nrm(ks[2], (DEPTH, D_MODEL), 0.1),
        "w_in": nrm(ks[3], (DEPTH, D_MODEL, D_IN_PROJ), D_MODEL ** -0.5),
        "b_igate": nrm(ks[4], (DEPTH, H), 0.1) - 1.0,
        "b_fgate": nrm(ks[5], (DEPTH, H), 0.5) + 3.0,
        "short_conv_w": nrm(ks[6], (DEPTH, CONV_WIDTH, D_CONV), CONV_WIDTH ** -0.5),
        "mh_norm_g": 1.0 + nrm(ks[7], (DEPTH, D_MLSTM), 0.1),
        "w_out": nrm(ks[8], (DEPTH, D_MIX, D_MODEL), D_MIX ** -0.5),
        "norm_ffn_g": 1.0 + nrm(ks[9], (DEPTH, D_MODEL), 0.1),
        "w_up": nrm(ks[10], (DEPTH, D_MODEL, 2 * D_FF), D_MODEL ** -0.5),
        "ffn_conv_w": nrm(ks[11], (DEPTH, CONV_WIDTH, 2 * D_FF), CONV_WIDTH ** -0.5),
        "ffn_conv_b": nrm(ks[12], (DEPTH, 2 * D_FF), 0.01),
        "w_down": nrm(ks[13], (DEPTH, D_FF, D_MODEL), D_FF ** -0.5),
        "norm_ple_g": 1.0 + nrm(ks[14], (DEPTH, D_MODEL), 0.1),
        "w_ple_gate": nrm(ks[15], (DEPTH, D_MODEL, D_MODEL), D_MODEL ** -0.5),
        "w_ple_proj": nrm(ks[16], (DEPTH, D_PLE, D_MODEL), D_PLE ** -0.5),
        "final_norm_g": 1.0 + nrm(ks[17], (D_MODEL,), 0.1),
    }


def reference(x, p, norm_mix_g, w_in, b_igate, b_fgate, short_conv_w, mh_norm_g, w_out,
              norm_ffn_g, w_up, ffn_conv_w, ffn_conv_b, w_down, norm_ple_g, w_ple_gate,
              w_ple_proj, final_norm_g):
    for i in range(DEPTH):
        h = rmsnorm(x, norm_mix_g[i])
        x = x + hybrid_mixer(h, w_in[i], b_igate[i], b_fgate[i], short_conv_w[i],
                             mh_norm_g[i], w_out[i])
        h = rmsnorm(x, norm_ffn_g[i])
        x = x + conv_ffn(h, w_up[i], ffn_conv_w[i], ffn_conv_b[i], w_down[i])
        h = rmsnorm(x, norm_ple_g[i])
        gate = jax.nn.sigmoid((h @ w_ple_gate[i]).astype(jnp.float32)).astype(x.dtype)
        x = x + gate * (p[i] @ w_ple_proj[i])
    return rmsnorm(x, final_norm_g)
```

```python
import contextlib
import numpy as np
import concourse.bass as bass
import concourse.mybir as mybir
from concourse.bass_utils import run_bass_kernel_spmd

F32 = mybir.dt.float32
BF16 = mybir.dt.bfloat16
AF = mybir.ActivationFunctionType
ALU = mybir.AluOpType

EPS = 1e-6
NS = 4
N_DMA_SEMS = 8

C_G1, C_G2, C_G3, C_GF = 0, 16, 32, 48
C_SCW = 64
C_FCW = 88
C_FCB = 352
C_BIF = 440
C_GMH = 448
C_TRI = 1472
C_ONE = 1600
C_IDN = 1728
NCST = 1856


class Res:
    __slots__ = ("name", "writer", "readers", "rdma")

    def __init__(self, name):
        self.name = name
        self.writer = None
        self.readers = {}
        self.rdma = []


class Op:
    __slots__ = ("eng", "fn", "deps", "signal", "sem", "val", "is_dma")

    def __init__(self, eng, fn, is_dma):
        self.eng = eng
        self.fn = fn
        self.deps = []
        self.signal = False
        self.sem = None
        self.val = 0
        self.is_dma = is_dma


ENGS = ("pe", "act", "dve", "pool", "sp")


class Prog:
    def __init__(self, nc):
        self.nc = nc
        self.ops = {e: [] for e in ENGS}

    def op(self, eng, fn, reads=(), writes=(), dma=False):
        o = Op(eng, fn, dma)
        deps = []
        for r in reads:
            if r.writer is not None:
                deps.append(r.writer)
        for w in writes:
            if w.writer is not None:
                deps.append(w.writer)
            deps.extend(w.readers.values())
            deps.extend(w.rdma)
        seen = set()
        for d in deps:
            if id(d) in seen or d is o:
                continue
            seen.add(id(d))
            if d.eng == "pe" and eng == "pe" and not d.is_dma:
                continue
            d.signal = True
            o.deps.append(d)
        for r in reads:
            if dma:
                r.rdma.append(o)
            else:
                r.readers[eng] = o
        for w in writes:
            w.writer = o
            w.readers = {}
            w.rdma = []
        self.ops[eng].append(o)
        return o

    def alias(self, new, olds):
        for r in olds:
            for d in ([r.writer] if r.writer is not None else []) + list(r.readers.values()) + r.rdma:
                new.rdma.append(d)

    def emit(self):
        nc = self.nc
        with contextlib.ExitStack() as st:
            esem = {e: st.enter_context(nc.semaphore("s_" + e)) for e in ENGS}
            dsem = {e: [st.enter_context(nc.semaphore("d_%s%d" % (e, i))) for i in range(N_DMA_SEMS)]
                    for e in ("sp", "act", "pool")}
            for e in ENGS:
                cnt = 0
                nd = 0
                for o in self.ops[e]:
                    if o.is_dma:
                        o.sem = dsem[e][nd % N_DMA_SEMS]
                        o.val = 16 * (nd // N_DMA_SEMS + 1)
                        nd += 1
                    elif o.signal:
                        cnt += 1
                        o.sem = esem[e]
                        o.val = cnt
            block = st.enter_context(nc.Block())

            def run(e, eng):
                seen = {}
                nd = 0
                for o in self.ops[e]:
                    waits = {}
                    for d in o.deps:
                        k = id(d.sem)
                        if k not in waits or waits[k][1] < d.val:
                            waits[k] = (d.sem, d.val)
                    if o.is_dma:
                        if nd >= N_DMA_SEMS:
                            k = id(o.sem)
                            v = o.val - 16
                            if k not in waits or waits[k][1] < v:
                                waits[k] = (o.sem, v)
                        nd += 1
                    for k, (s, v) in waits.items():
                        if seen.get(k, 0) >= v:
                            continue
                        seen[k] = v
                        eng.wait_ge(s, v)
                    ins = o.fn(eng)
                    if o.is_dma:
                        ins.then_inc(o.sem, 16)
                    elif o.signal:
                        ins.then_inc(o.sem, 1)

            @block.tensor
            def _(eng):
                run("pe", eng)

            @block.scalar
            def _(eng):
                run("act", eng)

            @block.vector
            def _(eng):
                run("dve", eng)

            @block.gpsimd
            def _(eng):
                run("pool", eng)

            @block.sync
            def _(eng):
                run("sp", eng)


def build_nc(debug=()):
    nc = bass.Bass("TRN2", target_bir_lowering=False)
    dt_in = lambda n, s: nc.dram_tensor(n, s, F32, kind="ExternalInput").ap()
    xsl = dt_in("xsl", [128, 16 * 2048])
    xr = dt_in("xr", [128, 16 * 1029])
    pT = dt_in("pT", [256, 1024])
    w_in_b = dt_in("w_in_b", [24, 128, 16 * 256])
    w_if = dt_in("w_if", [128, 16 * 8])
    w_out_b = dt_in("w_out_b", [8, 128, 16 * 256])
    w_up_b = dt_in("w_up_b", [44, 128, 16 * 256])
    w_down_b = dt_in("w_down_b", [8, 128, 44 * 256])
    w_pg_b = dt_in("w_pg_b", [8, 128, 16 * 256])
    w_pp = dt_in("w_pp", [256, 2048])
    cst_d = dt_in("cst", [128, NCST])
    outT = nc.dram_tensor("outT", [2048, 1024], F32, kind="ExternalOutput").ap()
    dbg_out = {}
    for name, shape in debug:
        dbg_out[name] = nc.dram_tensor("dbg_" + name, list(shape), F32, kind="ExternalOutput").ap()

    kview = lambda w: w.rearrange("(k p) n -> p k n", p=128)
    w_ppv, pTv, outTv = map(kview, (w_pp, pT, outT))
    blk = lambda wb, cb, kc=16: wb[cb].rearrange("p (k n) -> p k n", k=kc)
    slab_off = {}
    _o = 0
    for (c0_, n_, W_) in ((0, 896, 128), (896, 1152, 256)):
        for s_ in range((n_ + W_ - 1) // W_):
            w_ = min(W_, n_ - W_ * s_)
            slab_off[(c0_, s_)] = (_o, w_)
            _o += 16 * w_

    P = Prog(nc)
    sb = nc.alloc_sbuf_tensor

    cst = sb("cst_sb", [128, NCST], F32)
    cbf = sb("cbf", [128, 256], BF16)
    ones_bf = cbf[:, 0:128]
    ident_bf = cbf[:, 128:256]
    wsl = sb("wsl", [128, NS, 4096], BF16)
    hT = sb("hT", [128, 16, 1152], BF16)
    arena = sb("arena", [128, 16464], F32)
    yTt = sb("yT", [128, 16464], BF16)
    NE = 7296
    regE = sb("regE", [128, NE], F32)

    psA = nc.alloc_psum_tensor("psA", [128, 3, 512], F32)
    psB = nc.alloc_psum_tensor("psB", [128, 3, 512], F32)
    psC = nc.alloc_psum_tensor("psC", [128, 512], F32)
    psD = nc.alloc_psum_tensor("psD", [128, 1024], BF16)
    RpsA = [Res("psA%d" % i) for i in range(3)]
    RpsB = [Res("psB%d" % i) for i in range(3)]
    RpsC = Res("psC")
    RpsD = Res("psD")
    RpsDh = [Res("psDh0"), Res("psDh1")]
    banksA = [(psA[:, i, :], RpsA[i]) for i in range(3)]
    banksB = [(psB[:, i, :], RpsB[i]) for i in range(3)]
    pools = {"tm": banksA + banksB, "fm": [(psA, RpsA), (psB, RpsB)], "tmi": 0, "fmi": 0}

    def set_pools(tm, fm):
        pools["tm"] = tm
        pools["fm"] = fm

    Rcst = Res("cst")
    Rcbf = Res("cbf")
    RhT = [Res("hT%d" % k) for k in range(16)]

    ar_bf = arena[:, :].bitcast(BF16)
    slab = [arena[:, 0:4096].rearrange("p (k n) -> p k n", k=16),
            arena[:, 4096:8192].rearrange("p (k n) -> p k n", k=16)]
    Rslab = [Res("slab0"), Res("slab1")]
    slab_p = [arena[:, 2048 * i:2048 * (i + 1)].rearrange("p (k n) -> p k n", k=16) for i in range(4)]
    Rslab_p = [Res("slabp%d" % i) for i in range(4)]
    k_tm = ar_bf[:, 0:4608].rearrange("p (t n) -> p t n", t=9)
    vw = ar_bf[:, 4608:13824].rearrange("p (t h n) -> p t h n", t=9, h=4)
    kT = ar_bf[:, 13824:18432].rearrange("p (h n) -> p h n", h=4)
    qT = ar_bf[:, 18432:23040].rearrange("p (h n) -> p h n", h=4)
    og = ar_bf[:, 23040:32256].rearrange("p (t n) -> p t n", t=9)
    k_tm_p = ar_bf[:, 18432:22016].rearrange("p (t n) -> p t n", t=7)
    vw_p = ar_bf[:, 22016:29184].rearrange("p (t h n) -> p t h n", t=7, h=4)
    RqT = [Res("qT%d" % h) for h in range(4)]
    RkT = [Res("kT%d" % h) for h in range(4)]
    Rk_tm = [Res("k_tm%d" % t) for t in range(9)]
    Rvw = [Res("vw%d" % t) for t in range(9)]
    Rog = [Res("og%d" % t) for t in range(9)]
    Rk_tm_p = [Res("k_tm_p%d" % t) for t in range(7)]
    Rvw_p = [Res("vw_p%d" % t) for t in range(7)]
    resAB = Rk_tm + Rvw + RkT
    resC = RqT + Rog
    resCp = Rk_tm_p + Rvw_p
    xres = arena[:, 0:16464].rearrange("p (k n) -> p k n", k=16)
    Rxres = [Res("xres%d" % k) for k in range(16)]

    yT = yTt[:, :].rearrange("p (k n) -> p k n", k=16)
    RyT = [Res("yT%d" % k) for k in range(16)]
    hTp = yTt[:, 0:16 * 896].rearrange("p (k n) -> p k n", k=16)
    RhTp = [Res("hTp%d" % k) for k in range(16)]
    aT = yTt[:, 0:16 * 1026].rearrange("p (k n) -> p k n", k=16)
    RaT = [Res("aT%d" % k) for k in range(16)]
    ppw = yTt[:, 0:4096].rearrange("p (k n) -> p k n", k=2)
    Rppw = Res("ppw")
    pTb = yTt[:, 4096:6144].rearrange("p (k n) -> p k n", k=2)
    RpTb = Res("pTb")

    e_bf = regE[:, :].bitcast(BF16)
    E32 = regE[:, 0:1024].rearrange("p (h n) -> p h n", h=4)
    Cbf = e_bf[:, 2048:3072].rearrange("p (h n) -> p h n", h=4)
    Sm = e_bf[:, 3072:3584].rearrange("p (h n) -> p h n", h=4)
    ym = e_bf[:, 3584:4608]
    junk = regE[:, 2304:2560]
    so = [2560]

    def small(n):
        v = regE[:, so[0]:so[0] + n]
        so[0] += n
        return v

    class GateSet:
        def __init__(self, T, tag):
            self.T = T
            r3 = lambda v: v.rearrange("p (t n) -> p t n", t=T)
            self.gates = r3(small(8 * T))
            self.etmp = r3(small(4 * T))
            self.nlf = r3(small(4 * T))
            self.arg = r3(small(4 * T))
            self.wp = r3(small(4 * T))
            self.ebt = r3(small(4 * T))
            self.ebs = r3(small(4 * T))
            o = so[0]
            self.wpb = e_bf[:, 2 * o:2 * o + 4 * T].rearrange("p (t n) -> p t n", t=T)
            so[0] += 2 * T
            mk = lambda n: Res(n + tag)
            self.Rgates, self.Retmp, self.Rnlf, self.Rarg = mk("gates"), mk("etmp"), mk("nlf"), mk("arg")
            self.Rwp, self.Rwpb, self.Rebt, self.Rebs = mk("wp"), mk("wpb"), mk("ebt"), mk("ebs")

        def all_res(self):
            return [self.Rgates, self.Retmp, self.Rnlf, self.Rarg, self.Rwp, self.Rwpb, self.Rebt, self.Rebs]

    Gp = GateSet(7, "_p")
    Gx = GateSet(9, "_x")
    En = small(4)
    d1 = small(4); d2 = small(4); d3 = small(4); rec = small(4); cc = small(4)
    ss = small(4); t1 = small(4); t2 = small(4); lnr = small(4); rr = small(4); rowsc = small(4)
    nbf = e_bf[:, 2 * so[0]:2 * so[0] + 4]; so[0] += 2
    rs_col = small(8)
    Rrs_col = Res("rs_col")
    rs = small(512)
    assert so[0] <= 4100
    RE32 = [Res("E32_%d" % h) for h in range(4)]
    RCbf = [Res("Cbf_%d" % h) for h in range(4)]
    RSm = Res("Sm"); Rym = Res("ym"); Rjunk = Res("junk")
    REn = Res("En"); Rnbf = Res("nbf")
    Rsm = {n: Res(n) for n in "d1 d2 d3 rec cc ss t1 t2 lnr rr rowsc".split()}
    Rrs = Res("rs")
    mlstm_small_res = RE32 + RCbf + [RSm, Rym, Rjunk, REn, Rnbf, Rrs] + list(Rsm.values()) + Gp.all_res() + Gx.all_res()
    gcs = [regE[:, 4100:5135].rearrange("p (s n) -> p s n", s=3), regE[:, 5135:6170].rearrange("p (s n) -> p s n", s=3)]
    cacc = regE[:, 6170:7199].rearrange("p (s n) -> p s n", s=3)
    Rgcs = [Res("gcs0"), Res("gcs1")]
    Rcacc = Res("cacc")
    conv_res = Rgcs + [Rcacc]
    sqd = e_bf[:, 8200:12296].rearrange("p (k n) -> p k n", k=16)
    Rsqd = Res("sqd")
    upg = [regE[:, 0:1026].rearrange("p (s n) -> p s n", s=3), regE[:, 1026:2052].rearrange("p (s n) -> p s n", s=3)]
    upu = [regE[:, 2052:3078].rearrange("p (s n) -> p s n", s=3), regE[:, 3078:4104].rearrange("p (s n) -> p s n", s=3)]
    Rupg = [Res("upg0"), Res("upg1")]
    Rupu = [Res("upu0"), Res("upu1")]
    rs2 = regE[:, 4140:5169]
    acc = regE[:, 5169:6198]
    tmpq = regE[:, 6198:7227]
    Rrs2 = Res("rs2"); Racc = Res("acc"); Rtmpq = Res("tmpq")
    sgt = [regE[:, 0:512], regE[:, 512:1024]]
    ppt = [regE[:, 1024:1536], regE[:, 1536:2048]]
    Rsgt = [Res("sgt0"), Res("sgt1")]
    Rppt = [Res("ppt0"), Res("ppt1")]

    cs = lambda off, n=1: cst[:, off:off + n]
    triM = cst[:, C_TRI:C_TRI + 128]
    onesF = cst[:, C_ONE:C_ONE + 128]

    Rslot = [Res("slot%d" % i) for i in range(NS)]
    wq = []
    wstate = {"issued": 0}

    def wget(i):
        while wstate["issued"] < min(i + NS, len(wq)):
            j = wstate["issued"]
            src, kc, ncol = wq[j]
            dst = wsl[:, j % NS, 0:kc * ncol].rearrange("p (k n) -> p k n", k=kc)
            o = P.op("pool", lambda e, dst=dst, src=src: e.dma_start(out=dst, in_=src),
                     writes=[Rslot[j % NS]], dma=True)
            if j < NS and wstate.get("gate") is not None:
                g = wstate["gate"]
                g.signal = True
                o.deps.append(g)
            wstate["issued"] += 1
        src, kc, ncol = wq[i]
        return wsl[:, i % NS, 0:kc * ncol].rearrange("p (k n) -> p k n", k=kc), Rslot[i % NS]

    tasks = []

    def task(src, kc, ncol, fn):
        tasks.append(("w", (src, kc, ncol), fn))

    def call(fn):
        tasks.append(("c", None, fn))

    def run_tasks():
        for t in tasks:
            if t[0] == "w":
                wq.append(t[1])
        n = 0
        for t in tasks:
            if t[0] == "w":
                slot, rs_ = wget(n)
                n += 1
                t[2](slot, rs_)
            else:
                t[2]()

    def tap(name, ap, reads):
        if name in dbg_out:
            P.op("pool", lambda e: e.dma_start(out=dbg_out[name], in_=ap), reads=reads, writes=[Res("dbg")], dma=True)

    def rstd_from_psum(ps_ap, out_rs, Rps, Rout, div):
        P.op("act", lambda e: e.activation(out=out_rs, in_=ps_ap, func=AF.Ln, bias=EPS, scale=1.0 / div),
             reads=[Rps], writes=[Rout])
        P.op("act", lambda e: e.activation(out=out_rs, in_=out_rs, func=AF.Exp, scale=-0.5),
             reads=[Rout], writes=[Rout])

    def norm_slab(c0, s, n, goff, dst, Rdst, W=256):
        w = min(W, n - s * W)
        if W == 256:
            sl = slab[s % 2][:, :, 0:w]
            Rs = Rslab[s % 2]
        else:
            sl = slab_p[s % 4][:, :, 0:w]
            Rs = Rslab_p[s % 4]
        so_, _w = slab_off[(c0, s)]
        assert _w == w
        o = P.op("sp", lambda e: e.dma_start(out=sl, in_=xsl[:, so_:so_ + 16 * w].rearrange("p (k n) -> p k n", k=16)),
                 writes=[Rs], dma=True)
        if c0 == 0 and s == 3:
            wstate["gate"] = o
        sq = sqd[:, :, 0:w]
        P.op("act", lambda e: e.activation(out=sq, in_=sl, func=AF.Square), reads=[Rs], writes=[Rsqd])
        for k in range(16):
            P.op("pe", lambda e, k=k: e.matmul(psC[:, 0:w], lhsT=ones_bf, rhs=sqd[:, k, 0:w], start=(k == 0), stop=(k == 15)),
                 reads=[Rsqd, Rcbf], writes=[RpsC])
        rstd_from_psum(psC[:, 0:w], rs[:, 0:w], RpsC, Rrs, 2048.0)
        for k in range(16):
            if True:
                P.op("dve", lambda e, k=k: e.scalar_tensor_tensor(
                    out=dst[:, k, s * W:s * W + w], in0=sl[:, k, :], scalar=cs(goff + k), in1=rs[:, 0:w],
                    op0=ALU.mult, op1=ALU.mult), reads=[Rs, Rrs, Rcst], writes=[Rdst[k]])
            else:
                Rk = Res("slabk")
                P.op("pool", lambda e, k=k: e.tensor_scalar(out=sl[:, k, :], in0=sl[:, k, :], scalar1=cs(goff + k), scalar2=None, op0=ALU.mult),
                     reads=[Rs, Rcst], writes=[Rk])
                P.op("pool", lambda e, k=k: e.tensor_tensor(out=dst[:, k, s * 256:s * 256 + w], in0=sl[:, k, :], in1=rs[:, 0:w], op=ALU.mult),
                     reads=[Rs, Rk, Rrs], writes=[Rdst[k]])

    def prefix_slab(s):
        sl = slab_p[s % 4]
        Rs = Rslab_p[s % 4]
        so_, _w = slab_off[(0, s)]
        o = P.op("sp", lambda e: e.dma_start(out=sl, in_=xsl[:, so_:so_ + 16 * 128].rearrange("p (k n) -> p k n", k=16)),
                 writes=[Rs], dma=True)
        if s == 3:
            wstate["gate"] = o
        P.op("act", lambda e: e.activation(out=sqd[:, :, 0:128], in_=sl, func=AF.Square), reads=[Rs], writes=[Rsqd])
        for k in range(16):
            P.op("pe", lambda e, k=k: e.matmul(psC[:, 64 + s:65 + s], lhsT=sqd[:, k, 0:128], rhs=ones_bf[:, 0:1], start=(k == 0), stop=(k == 15)),
                 reads=[Rsqd, Rcbf], writes=[RpsC])
        P.op("dve", lambda e: e.tensor_tensor(out=hTp[:, :, 128 * s:128 * (s + 1)], in0=sl,
                                              in1=cst[:, C_G1:C_G1 + 16].unsqueeze(2).broadcast_to([128, 16, 128]), op=ALU.mult),
             reads=[Rs, Rcst], writes=RhTp)

    def prefix_rs():
        P.op("act", lambda e: e.activation(out=rs_col[:, 0:7], in_=psC[:, 64:71], func=AF.Ln, bias=EPS, scale=1.0 / 2048.0),
             reads=[RpsC], writes=[Rrs_col])
        P.op("act", lambda e: e.activation(out=rs_col[:, 0:7], in_=rs_col[:, 0:7], func=AF.Exp, scale=-0.5),
             reads=[Rrs_col], writes=[Rrs_col])

    def sumsq_hook(f, col0, w, first):
        if first:
            P.op("act", lambda e: e.activation(out=acc[:, 0:w], in_=xres[:, f, col0:col0 + w], func=AF.Square),
                 reads=[Rxres[f]], writes=[Racc])
        else:
            P.op("act", lambda e: e.activation(out=tmpq[:, 0:w], in_=xres[:, f, col0:col0 + w], func=AF.Square),
                 reads=[Rxres[f]], writes=[Rtmpq])
            P.op("dve", lambda e: e.tensor_tensor(out=acc[:, 0:w], in0=acc[:, 0:w], in1=tmpq[:, 0:w], op=ALU.add),
                 reads=[Racc, Rtmpq], writes=[Racc])

    def norm_finish_sliced(col0, slices, goff, dst, Rdst2, Rrs_list):
        for si, (s0, w) in enumerate(slices):
            P.op("pe", lambda e, s0=s0, w=w: e.matmul(psC[:, 0:w], lhsT=onesF, rhs=acc[:, s0:s0 + w], start=True, stop=True),
                 reads=[Racc, Rcst], writes=[RpsC])
            rstd_from_psum(psC[:, 0:w], rs2[:, s0:s0 + w], RpsC, Rrs_list[si], 2048.0)
            for k in range(16):
                P.op("dve", lambda e, k=k, s0=s0, w=w: e.scalar_tensor_tensor(
                    out=dst[:, k, s0:s0 + w], in0=xres[:, k, col0 + s0:col0 + s0 + w], scalar=cs(goff + k), in1=rs2[:, s0:s0 + w],
                    op0=ALU.mult, op1=ALU.mult), reads=[Rxres[k], Rrs_list[si], Rcst], writes=[Rdst2[k][si]])

    def norm_finish(col0, W, slices, goff, out_fn, out_res_fn, post=None):
        for (s0, w) in slices:
            P.op("pe", lambda e, s0=s0, w=w: e.matmul(psC[:, 0:w], lhsT=onesF, rhs=acc[:, s0:s0 + w], start=True, stop=True),
                 reads=[Racc, Rcst], writes=[RpsC])
            rstd_from_psum(psC[:, 0:w], rs2[:, s0:s0 + w], RpsC, Rrs2, 2048.0)
        for k in range(16):
            if True:
                P.op("dve", lambda e, k=k: e.scalar_tensor_tensor(
                    out=out_fn(k), in0=xres[:, k, col0:col0 + W], scalar=cs(goff + k), in1=rs2[:, 0:W],
                    op0=ALU.mult, op1=ALU.mult), reads=[Rxres[k], Rrs2, Rcst], writes=[out_res_fn(k)])
            else:
                P.op("pool", lambda e, k=k: e.tensor_scalar(out=tmpq[:, 0:W], in0=xres[:, k, col0:col0 + W], scalar1=cs(goff + k),
                                                            scalar2=None, op0=ALU.mult), reads=[Rxres[k], Rcst], writes=[Rtmpq])
                P.op("pool", lambda e, k=k: e.tensor_tensor(out=out_fn(k), in0=tmpq[:, 0:W], in1=rs2[:, 0:W], op=ALU.mult),
                     reads=[Rtmpq, Rrs2], writes=[out_res_fn(k)])
            if post is not None:
                post(k)

    def next_bank():
        tm = pools["tm"]
        b = tm[pools["tmi"] % len(tm)]
        pools["tmi"] += 1
        return b

    def next_group():
        fm = pools["fm"]
        g = fm[pools["fmi"] % len(fm)]
        pools["fmi"] += 1
        return g

    def tok_major_proj(slot, Rs, src, Rsrc, tile_list, ncol, evac):
        for t in tile_list:
            ps, Rp = next_bank()
            for k in range(16):
                P.op("pe", lambda e, k=k, t=t, ps=ps: e.matmul(ps[:, 0:ncol], lhsT=src[:, k, t * 128:(t + 1) * 128],
                                                                rhs=slot[:, k, 0:ncol], start=(k == 0), stop=(k == 15)),
                     reads=[Rsrc[k], Rs], writes=[Rp])
            evac(t, ps[:, 0:ncol], Rp)

    def feat_major_proj(slot, Rs, j, src, Rsrc, kc, col_slices, Rsrc_fn=None, k_outer=False):
        ps, Rb = next_group()
        if k_outer:
            order = [(si, k) for k in range(kc) for si in range(len(col_slices))]
        else:
            order = [(si, k) for si in range(len(col_slices)) for k in range(kc)]
        for si, k in order:
            c0, w = col_slices[si]
            rd = [Rs] + (Rsrc_fn(k, si) if Rsrc_fn is not None else [Rsrc[k]])
            P.op("pe", lambda e, k=k, si=si, c0=c0, w=w: e.matmul(
                ps[:, si, 0:w], lhsT=slot[:, k, j * 128:(j + 1) * 128], rhs=src[:, k, c0:c0 + w],
                start=(k == 0), stop=(k == kc - 1)),
                 reads=rd, writes=[Rb[si]])
        return ps, Rb

    def gate_math(G):
        T = G.T
        P.op("act", lambda e: e.activation(out=G.etmp, in_=G.gates[:, :, 4:8], func=AF.Exp, scale=-1.0),
             reads=[G.Rgates], writes=[G.Retmp])
        P.op("act", lambda e: e.activation(out=G.nlf, in_=G.etmp, func=AF.Ln, bias=1.0),
             reads=[G.Retmp], writes=[G.Rnlf])
        nlf_all = G.nlf.rearrange("p t n -> p (t n)")
        P.op("pe", lambda e: e.matmul(psC[:, 0:4 * T], lhsT=triM, rhs=nlf_all, start=True, stop=True),
             reads=[G.Rnlf, Rcst], writes=[RpsC])
        P.op("pe", lambda e: e.matmul(psC[:, 4 * T:8 * T], lhsT=onesF, rhs=nlf_all, start=True, stop=True),
             reads=[G.Rnlf, Rcst], writes=[RpsC])
        pcum = psC[:, 0:4 * T].rearrange("p (t n) -> p t n", t=T)
        ptot = psC[:, 4 * T:8 * T].rearrange("p (t n) -> p t n", t=T)
        P.op("dve", lambda e: e.tensor_tensor(out=G.arg, in0=pcum, in1=G.gates[:, :, 0:4], op=ALU.add),
             reads=[RpsC, G.Rgates], writes=[G.Rarg])
        P.op("act", lambda e: e.activation(out=G.wp, in_=G.arg, func=AF.Exp), reads=[G.Rarg], writes=[G.Rwp])
        P.op("dve", lambda e: e.tensor_copy(out=G.wpb, in_=G.wp), reads=[G.Rwp], writes=[G.Rwpb])
        P.op("act", lambda e: e.activation(out=G.ebt, in_=ptot, func=AF.Exp, scale=-1.0),
             reads=[RpsC], writes=[G.Rebt])
        P.op("act", lambda e: e.activation(out=G.ebs, in_=pcum, func=AF.Exp, scale=-1.0),
             reads=[RpsC], writes=[G.Rebs])

    def if_task(G, src, Rsrc, rsc=None):
        T = G.T

        def consume(slot, Rs):
            for t in range(T):
                for k in range(16):
                    P.op("pe", lambda e, k=k, t=t: e.matmul(psC[:, t * 8:t * 8 + 8], lhsT=src[:, k, t * 128:(t + 1) * 128],
                                                             rhs=slot[:, k, 0:8], start=(k == 0), stop=(k == 15)),
                         reads=[Rsrc[k], Rs], writes=[RpsC])
            pc = psC[:, 0:8 * T].rearrange("p (t n) -> p t n", t=T)
            bias_b = cst[:, C_BIF:C_BIF + 8].unsqueeze(1).broadcast_to([128, T, 8])
            if rsc is None:
                P.op("dve", lambda e: e.tensor_tensor(out=G.gates, in0=pc, in1=bias_b, op=ALU.add),
                     reads=[RpsC, Rcst], writes=[G.Rgates])
            else:
                P.op("dve", lambda e: e.tensor_tensor(out=G.gates, in0=pc, in1=rsc[:, 0:T].unsqueeze(2).broadcast_to([128, T, 8]), op=ALU.mult),
                     reads=[RpsC, Rrs_col], writes=[G.Rgates])
                P.op("dve", lambda e: e.tensor_tensor(out=G.gates, in0=G.gates, in1=bias_b, op=ALU.add),
                     reads=[G.Rgates, Rcst], writes=[G.Rgates])
            gate_math(G)
        task(w_if[:, :].rearrange("p (k n) -> p k n", k=16), 16, 8, consume)

    def v_task(G, src, Rsrc, vwb, Rvwb, h, rsc=None):
        def consume(slot, Rs):
            def evac(t, ps, Rp):
                if rsc is None:
                    P.op("dve", lambda e: e.tensor_scalar(out=vwb[:, t, h, :], in0=ps, scalar1=G.wp[:, t, h:h + 1], scalar2=None,
                                                          op0=ALU.mult), reads=[Rp, G.Rwp], writes=[Rvwb[t]])
                else:
                    P.op("dve", lambda e: e.tensor_scalar(out=vwb[:, t, h, :], in0=ps, scalar1=rsc[:, t:t + 1], scalar2=G.wp[:, t, h:h + 1],
                                                          op0=ALU.mult, op1=ALU.mult), reads=[Rp, G.Rwp, Rrs_col], writes=[Rvwb[t]])
            tok_major_proj(slot, Rs, src, Rsrc, range(G.T), 256, evac)
        task(blk(w_in_b, 16 + h), 16, 256, consume)

    qk_slices = [(0, 384), (384, 384), (768, 384)]

    def k_task(G, src, Rsrc, kb, Rkb, j, with_kT, rsc=None):
        def consume(slot, Rs):
            def evac(t, ps, Rp):
                if rsc is None:
                    P.op("act", lambda e: e.activation(out=kb[:, t, 256 * j:256 * (j + 1)], in_=ps, func=AF.Copy),
                         reads=[Rp], writes=[Rkb[t]])
                else:
                    P.op("act", lambda e: e.activation(out=kb[:, t, 256 * j:256 * (j + 1)], in_=ps, func=AF.Identity, scale=rsc[:, t:t + 1]),
                         reads=[Rp, Rrs_col], writes=[Rkb[t]])
            tok_major_proj(slot, Rs, src, Rsrc, range(G.T), 256, evac)
            if with_kT and j == 1:
                tb = [(psD[:, 0:512], RpsD), (psC[:, :].bitcast(BF16)[:, 0:512], RpsC)]
                for t in range(G.T):
                    pt, Rpt = tb[t % 2]
                    for h in range(4):
                        P.op("pe", lambda e, t=t, h=h, pt=pt: e.transpose(pt[:, h * 128:(h + 1) * 128], kb[:, t, h * 128:(h + 1) * 128], ident_bf),
                             reads=[Rkb[t], Rcbf], writes=[Rpt])
                    P.op("act", lambda e, t=t, pt=pt: e.activation(out=kT[:, :, t * 128:(t + 1) * 128],
                                                                   in_=pt.rearrange("p (h n) -> p h n", h=4), func=AF.Copy),
                         reads=[Rpt], writes=RkT)
        task(blk(w_in_b, 14 + j), 16, 256, consume)

    def q_task(j):
        def consume(slot, Rs):
            for jj in range(2):
                h = 2 * j + jj
                ps, Rb = feat_major_proj(slot, Rs, jj, hT, RhT, 16, qk_slices)
                P.op("act", lambda e, ps=ps, h=h: e.activation(out=qT[:, h, :].rearrange("p (s n) -> p s n", s=3),
                                                                in_=ps[:, :, 0:384], func=AF.Identity, scale=float(128 ** -0.5)),
                     reads=Rb, writes=[RqT[h]])
        task(blk(w_in_b, 12 + j), 16, 256, consume)

    def o_task(j):
        def consume(slot, Rs):
            def evac(t, ps, Rp):
                P.op("act", lambda e: e.activation(out=junk[:, 0:256], in_=ps, func=AF.Sigmoid), reads=[Rp], writes=[Rjunk])
                P.op("dve", lambda e: e.tensor_tensor(out=og[:, t, 256 * j:256 * (j + 1)], in0=junk[:, 0:256],
                                                      in1=cst[:, C_GMH + 256 * j:C_GMH + 256 * (j + 1)], op=ALU.mult),
                     reads=[Rjunk, Rcst], writes=[Rog[t]])
            tok_major_proj(slot, Rs, hT, RhT, range(9), 256, evac)
        task(blk(w_in_b, 20 + j), 16, 256, consume)

    psNum = [psA[:, 0, :].rearrange("p (h n) -> p h n", h=2), psA[:, 1, :].rearrange("p (h n) -> p h n", h=2)]
    psDen = psA[:, 2, 0:4]
    psNU = psA[:, 2, 4:8]
    mstate = {"first": True}

    def prev_ebt(G, T):
        if T > 0:
            return G.ebt[:, T - 1, :], G.Rebt
        return Gp.ebt[:, 6, :], Gp.Rebt

    def mlstm_part1(G, T):
        cols = slice(T * 128, (T + 1) * 128)
        for h in range(4):
            P.op("pe", lambda e, h=h: e.matmul(psC[:, h * 128:(h + 1) * 128], lhsT=kT[:, h, cols], rhs=qT[:, h, cols],
                                               start=True, stop=True), reads=[RkT[h], RqT[h]], writes=[RpsC])
        P.op("dve", lambda e: e.tensor_tensor(out=Sm[:, :, :], in0=psC[:, 0:512].rearrange("p (h n) -> p h n", h=4),
                                              in1=triM.unsqueeze(1).broadcast_to([128, 4, 128]), op=ALU.mult),
             reads=[RpsC, Rcst], writes=[RSm])
        for h in range(4):
            pn = psNum[h // 2][:, h % 2, :]
            P.op("pe", lambda e, h=h, pn=pn: e.matmul(pn, lhsT=Sm[:, h, :], rhs=vw[:, T, h, :], start=True, stop=False),
                 reads=[RSm, Rvw[T]], writes=[RpsA[h // 2]])
            P.op("pe", lambda e, h=h, pn=pn: e.matmul(pn, lhsT=qT[:, h, cols], rhs=Cbf[:, h, :], start=False, stop=True),
                 reads=[RqT[h], RCbf[h]], writes=[RpsA[h // 2]])
            P.op("pe", lambda e, h=h: e.matmul(psDen[:, h:h + 1], lhsT=Sm[:, h, :], rhs=G.wpb[:, T, h:h + 1], start=True, stop=False),
                 reads=[RSm, G.Rwpb], writes=[RpsA[2]])
            P.op("pe", lambda e, h=h: e.matmul(psDen[:, h:h + 1], lhsT=qT[:, h, cols], rhs=nbf[:, h:h + 1], start=False, stop=True),
                 reads=[RqT[h], Rnbf], writes=[RpsA[2]])
        S = Rsm
        P.op("dve", lambda e: e.tensor_tensor(out=d1, in0=psDen, in1=G.ebs[:, T, :], op=ALU.mult), reads=[RpsA[2], G.Rebs], writes=[S["d1"]])
        P.op("dve", lambda e: e.scalar_tensor_tensor(out=d2, in0=d1, scalar=-1.0, in1=d1, op0=ALU.mult, op1=ALU.max),
             reads=[S["d1"]], writes=[S["d2"]])
        P.op("dve", lambda e: e.tensor_scalar(out=d3, in0=d2, scalar1=1.0, scalar2=None, op0=ALU.max), reads=[S["d2"]], writes=[S["d3"]])
        P.op("dve", lambda e: e.reciprocal(out=rec, in_=d3), reads=[S["d3"]], writes=[S["rec"]])
        P.op("dve", lambda e: e.tensor_tensor(out=cc, in0=rec, in1=G.ebs[:, T, :], op=ALU.mult), reads=[S["rec"], G.Rebs], writes=[S["cc"]])
        P.op("dve", lambda e: e.memset(ss, 0.0), writes=[S["ss"]])
        for h in range(4):
            pn = psNum[h // 2][:, h % 2, :]
            P.op("act", lambda e, h=h, pn=pn: e.activation(out=junk[:, 0:256], in_=pn, func=AF.Square, scale=cc[:, h:h + 1],
                                                           accum_out=ss[:, h:h + 1]),
                 reads=[RpsA[h // 2], S["ss"], S["cc"]], writes=[Rjunk, S["ss"]])
        P.op("act", lambda e: e.activation(out=lnr, in_=ss, func=AF.Ln, bias=EPS, scale=1.0 / 256.0), reads=[S["ss"]], writes=[S["lnr"]])
        P.op("act", lambda e: e.activation(out=rr, in_=lnr, func=AF.Exp, scale=-0.5), reads=[S["lnr"]], writes=[S["rr"]])
        P.op("dve", lambda e: e.tensor_tensor(out=rowsc, in0=cc, in1=rr, op=ALU.mult), reads=[S["cc"], S["rr"]], writes=[S["rowsc"]])
        for h in range(4):
            pn = psNum[h // 2][:, h % 2, :]
            P.op("dve", lambda e, h=h, pn=pn: e.scalar_tensor_tensor(
                out=ym[:, h * 256:(h + 1) * 256], in0=pn, scalar=rowsc[:, h:h + 1], in1=og[:, T, h * 256:(h + 1) * 256],
                op0=ALU.mult, op1=ALU.mult), reads=[RpsA[h // 2], S["rowsc"], Rog[T]], writes=[Rym])

    def mlstm_transposes(T):
        for j in range(8):
            P.op("pe", lambda e, j=j: e.transpose(psD[:, j * 128:(j + 1) * 128], ym[:, j * 128:(j + 1) * 128], ident_bf),
                 reads=[Rym, Rcbf], writes=[RpsD])
        if T == 0:
            i0, n, src0 = 0, 5, 123
        else:
            i0, n, src0 = 5 + 128 * (T - 1), 128, 0
        P.op("act", lambda e: e.activation(out=yT[:, 8:16, i0:i0 + n],
                                           in_=psD[:, :].rearrange("p (j n) -> p j n", j=8)[:, :, src0:src0 + n], func=AF.Copy),
             reads=[RpsD], writes=RyT[8:16])

    def mlstm_update(G, T, kb, Rkb, vwb, Rvwb, need_c):
        first = mstate["first"]
        for h in range(4):
            pu = psNum[h // 2][:, h % 2, :]
            P.op("pe", lambda e, h=h, pu=pu: e.matmul(pu, lhsT=kb[:, T, h * 128:(h + 1) * 128], rhs=vwb[:, T, h, :], start=True, stop=True),
                 reads=[Rkb[T], Rvwb[T]], writes=[RpsA[h // 2]])
            P.op("pe", lambda e, h=h: e.matmul(psNU[:, h:h + 1], lhsT=kb[:, T, h * 128:(h + 1) * 128], rhs=G.wpb[:, T, h:h + 1], start=True, stop=True),
                 reads=[Rkb[T], G.Rwpb], writes=[RpsA[2]])
        pe4, Rpe = prev_ebt(G, T)
        for h in range(4):
            pu = psNum[h // 2][:, h % 2, :]
            if first:
                P.op("dve", lambda e, h=h, pu=pu: e.tensor_copy(out=E32[:, h, :], in_=pu), reads=[RpsA[h // 2]], writes=[RE32[h]])
            else:
                P.op("dve", lambda e, h=h, pu=pu: e.scalar_tensor_tensor(out=E32[:, h, :], in0=E32[:, h, :], scalar=pe4[:, h:h + 1], in1=pu,
                                                                           op0=ALU.mult, op1=ALU.add),
                     reads=[RE32[h], RpsA[h // 2], Rpe], writes=[RE32[h]])
            if need_c:
                P.op("act", lambda e, h=h: e.activation(out=Cbf[:, h, :], in_=E32[:, h, :], func=AF.Identity, scale=G.ebt[:, T, h:h + 1]),
                     reads=[RE32[h], G.Rebt], writes=[RCbf[h]])
        if first:
            P.op("dve", lambda e: e.tensor_copy(out=En, in_=psNU), reads=[RpsA[2]], writes=[REn])
        else:
            P.op("dve", lambda e: e.tensor_tensor(out=En, in0=En, in1=pe4, op=ALU.mult), reads=[REn, Rpe], writes=[REn])
            P.op("dve", lambda e: e.tensor_tensor(out=En, in0=En, in1=psNU, op=ALU.add), reads=[REn, RpsA[2]], writes=[REn])
        if need_c:
            P.op("dve", lambda e: e.tensor_tensor(out=nbf, in0=En, in1=G.ebt[:, T, :], op=ALU.mult), reads=[REn, G.Rebt], writes=[Rnbf])
        mstate["first"] = False

    cg_slices = [(121 + 343 * s, 345) for s in range(3)]

    def conv_chunk(kind, b, j, slot, Rs):
        ps, Rb = feat_major_proj(slot, Rs, j, hT, RhT, 16, cg_slices)
        if kind == "gc":
            for si in range(3):
                P.op("act", lambda e, si=si: e.activation(out=gcs[j][:, si, :], in_=ps[:, si, 0:345], func=AF.Copy),
                     reads=[Rb[si]], writes=[Rgcs[j]])
        elif kind == "u":
            for si in range(3):
                P.op("dve", lambda e, si=si: e.tensor_tensor(out=gcs[j][:, si, :], in0=ps[:, si, 0:345], in1=gcs[j][:, si, :], op=ALU.mult),
                     reads=[Rb[si], Rgcs[j]], writes=[Rgcs[j]])
        else:
            fc = 2 * b + j
            z = gcs[j]
            P.op("act", lambda e: e.activation(out=cacc[:, :, :], in_=z[:, :, 0:343], func=AF.Identity,
                                               scale=cs(C_SCW + 3 * fc + 0)), reads=[Rgcs[j], Rcst], writes=[Rcacc])
            P.op("dve", lambda e: e.scalar_tensor_tensor(out=cacc[:, :, :], in0=z[:, :, 1:344], scalar=cs(C_SCW + 3 * fc + 1),
                                                         in1=cacc[:, :, :], op0=ALU.mult, op1=ALU.add),
                 reads=[Rgcs[j], Rcacc, Rcst], writes=[Rcacc])
            P.op("dve", lambda e: e.scalar_tensor_tensor(out=cacc[:, :, :], in0=z[:, :, 2:345], scalar=cs(C_SCW + 3 * fc + 2),
                                                         in1=cacc[:, :, :], op0=ALU.mult, op1=ALU.add),
                 reads=[Rgcs[j], Rcacc, Rcst], writes=[Rcacc])
            for si in range(3):
                P.op("dve", lambda e, si=si: e.tensor_tensor(out=yT[:, fc, 343 * si:343 * (si + 1)],
                                                             in0=ps[:, si, 2:345], in1=cacc[:, si, :], op=ALU.mult),
                     reads=[Rb[si], Rcacc], writes=[RyT[fc]])

    def conv_task(kind, b, split):
        st = {}
        col = {"gc": 4, "u": 8, "gb": 0}[kind] + b

        def consume(slot, Rs):
            st["slot"], st["Rs"] = slot, Rs
            conv_chunk(kind, b, 0, slot, Rs)
            if not split:
                conv_chunk(kind, b, 1, slot, Rs)
        task(blk(w_in_b, col), 16, 256, consume)
        if split:
            return lambda: conv_chunk(kind, b, 1, st["slot"], st["Rs"])
        return None

    def setup():
        P.op("sp", lambda e: e.dma_start(out=cst[:, :], in_=cst_d[:, :]), writes=[Rcst], dma=True)
        P.op("dve", lambda e: e.tensor_copy(out=cbf[:, 0:128], in_=cst[:, C_ONE:C_ONE + 128]), reads=[Rcst], writes=[Rcbf])
        P.op("dve", lambda e: e.tensor_copy(out=cbf[:, 128:256], in_=cst[:, C_IDN:C_IDN + 128]), reads=[Rcst], writes=[Rcbf])
    call(setup)
    for s in range(7):
        call(lambda s=s: prefix_slab(s))
    call(prefix_rs)

    def ext_norm_alias():
        for r in Rslab:
            P.alias(r, Rslab_p)
    call(ext_norm_alias)
    ext_slabs = [lambda s=s: norm_slab(896, s, 1152, C_G1, hT, RhT) for s in range(5)]
    if_task(Gp, hTp, RhTp, rs_col)
    for h in range(4):
        v_task(Gp, hTp, RhTp, vw_p, Rvw_p, h, rs_col)
        call(ext_slabs[h])
    k_task(Gp, hTp, RhTp, k_tm_p, Rk_tm_p, 0, False, rs_col)
    call(ext_slabs[4])
    k_task(Gp, hTp, RhTp, k_tm_p, Rk_tm_p, 1, False, rs_col)
    call(lambda: tap("hT", hT[:, 0, 0:1152], RhT))

    def phase_c_begin():
        for r in resAB:
            P.alias(r, Rslab)
        set_pools(banksB, [(psB, RpsB)])
    call(phase_c_begin)
    sweep = [lambda T=T: mlstm_update(Gp, T, k_tm_p, Rk_tm_p, vw_p, Rvw_p, T == 6) for T in range(7)]
    call(sweep[0])
    if_task(Gx, hT, RhT)
    for h in range(4):
        call(sweep[1 + h])
        v_task(Gx, hT, RhT, vw, Rvw, h)
    for j in range(2):
        call(sweep[5 + j])
        k_task(Gx, hT, RhT, k_tm, Rk_tm, j, True)

    def phase_d_begin():
        for r in resC:
            P.alias(r, resCp)
        set_pools(banksA + banksB, [(psA, RpsA), (psB, RpsB)])
    call(phase_d_begin)
    for j in range(2):
        q_task(j)
    for j in range(4):
        o_task(j)

    def phase_e_begin():
        for r in conv_res:
            P.alias(r, [Rsqd])
        for r in RyT:
            P.alias(r, RhTp)
        set_pools(banksB, [(psB, RpsB)])
    call(phase_e_begin)
    conv_list = []
    for b in range(4):
        conv_list += [("gc", b), ("u", b), ("gb", b)]
    ci = 0
    for T in range(9):
        call(lambda T=T: mlstm_part1(Gx, T))
        second = conv_task(conv_list[ci][0], conv_list[ci][1], True); ci += 1
        if T < 8:
            call(lambda T=T: mlstm_update(Gx, T, k_tm, Rk_tm, vw, Rvw, True))
        call(lambda T=T: mlstm_transposes(T))
        call(second)
        if T < 3:
            conv_task(conv_list[ci][0], conv_list[ci][1], False); ci += 1
    assert ci == 12
    call(lambda: tap("yT", yTt[:, 0:16464], RyT))

    def xreload():
        set_pools(banksA + banksB, [(psA, RpsA), (psB, RpsB)])
        for k in range(16):
            P.alias(Rxres[k], resAB + resC + Rslab)
        for k4 in range(4):
            P.op("sp", lambda e, k4=k4: e.dma_start(out=xres[:, 4 * k4:4 * k4 + 4, :],
                                                    in_=xr[:, 4116 * k4:4116 * (k4 + 1)].rearrange("p (k n) -> p k n", k=4)),
                 writes=Rxres[4 * k4:4 * k4 + 4], dma=True)
        for r in (Racc, Rtmpq, Rrs2):
            P.alias(r, conv_res + mlstm_small_res)
    call(xreload)
    wo_slices = [(343 * s, 343) for s in range(3)]
    n2_slices = [(0, 345), (345, 342), (687, 342)]
    RhT2 = [[Res("hT2_%d_%d" % (k, si)) for si in range(3)] for k in range(16)]
    RhT3 = [[Res("hT3_%d_%d" % (k, si)) for si in range(2)] for k in range(16)]
    Rrs2_3 = [Res("rs2_%d" % si) for si in range(3)]
    ff_src = lambda k, si: [RhT2[k][si]] + ([RhT2[k][si - 1]] if si > 0 else [])

    def wout_task(b):
        def consume(slot, Rs):
            for j in range(2):
                f = 2 * b + j
                ps, Rb = feat_major_proj(slot, Rs, j, yT, RyT, 16, wo_slices)
                P.op("dve", lambda e, ps=ps, f=f: e.tensor_tensor(out=xres[:, f, :].rearrange("p (s n) -> p s n", s=3),
                                                                  in0=ps[:, :, 0:343], in1=xres[:, f, :].rearrange("p (s n) -> p s n", s=3),
                                                                  op=ALU.add), reads=Rb + [Rxres[f]], writes=[Rxres[f]])
                sumsq_hook(f, 0, 1029, f == 0)
        task(blk(w_out_b, b), 16, 256, consume)
    for b in range(8):
        wout_task(b)

    def norm2():
        tap("x1", arena[:, 0:16464], Rxres)
        for k in range(16):
            for si in range(3):
                P.alias(RhT2[k][si], [RhT[k]])
        for si in range(3):
            P.alias(Rrs2_3[si], [Rrs2])
        norm_finish_sliced(0, n2_slices, C_G2, hT, RhT2, Rrs2_3)
        for r in RaT:
            P.alias(r, RyT)
        for r in Rupg + Rupu:
            P.alias(r, conv_res + mlstm_small_res)
    call(norm2)

    ff_slices = [(1 + 342 * s, 344) for s in range(3)]
    groups = [(0, 16), (16, 16), (32, 12)]

    def conv_evac(ps, Rb, dst, Rdst, cidx):
        w0, w1, w2 = (cs(C_FCW + 3 * cidx + i) for i in range(3))
        P.op("act", lambda e: e.activation(out=dst[:, :, :], in_=ps[:, :, 2:344], func=AF.Identity, bias=cs(C_FCB + cidx), scale=w2),
             reads=Rb + [Rcst], writes=[Rdst])
        P.op("dve", lambda e: e.scalar_tensor_tensor(out=dst[:, :, :], in0=ps[:, :, 1:343], scalar=w1, in1=dst[:, :, :], op0=ALU.mult, op1=ALU.add),
             reads=Rb + [Rdst, Rcst], writes=[Rdst])
        P.op("dve", lambda e: e.scalar_tensor_tensor(out=dst[:, :, :], in0=ps[:, :, 0:342], scalar=w0, in1=dst[:, :, :], op0=ALU.mult, op1=ALU.add),
             reads=Rb + [Rdst, Rcst], writes=[Rdst])

    def ffg_task(jj):
        def consume(slot, Rs):
            for j in range(2):
                c = 2 * jj + j
                ps, Rb = feat_major_proj(slot, Rs, j, hT, RhT, 16, ff_slices, ff_src)
                conv_evac(ps, Rb, upg[j], Rupg[j], c)
                P.op("act", lambda e, j=j: e.activation(out=upg[j][:, :, :], in_=upg[j][:, :, :], func=AF.Silu),
                     reads=[Rupg[j]], writes=[Rupg[j]])
        task(blk(w_up_b, jj), 16, 256, consume)

    def ffu_task(jj, g0):
        def consume(slot, Rs):
            for j in range(2):
                c = 2 * jj + j
                ps, Rb = feat_major_proj(slot, Rs, j, hT, RhT, 16, ff_slices, ff_src)
                conv_evac(ps, Rb, upu[j], Rupu[j], 44 + c)
                P.op("dve", lambda e, j=j, c=c: e.tensor_tensor(out=aT[:, c - g0, :].rearrange("p (s n) -> p s n", s=3),
                                                                 in0=upg[j][:, :, :], in1=upu[j][:, :, :], op=ALU.mult),
                     reads=[Rupg[j], Rupu[j]], writes=[RaT[c - g0]])
        task(blk(w_up_b, 22 + jj), 16, 256, consume)

    wd_slices = [(2, 512), (514, 512)]

    Rtl = Res("tbl")

    def table_preload():
        P.op("act", lambda e: e.activation(out=regE[:, 4110:4111], in_=cst[:, C_ONE:C_ONE + 1], func=AF.Ln), reads=[Rcst], writes=[Rtl])

    def wdown_task(g0, gn, cb, last):
        def consume(slot, Rs):
            if last and cb == 0:
                table_preload()
            for j in range(2):
                f = 2 * cb + j
                ps, Rb = feat_major_proj(slot, Rs, j, aT, RaT[0:gn], gn, wd_slices, k_outer=True)
                P.op("dve", lambda e, ps=ps, f=f: e.tensor_tensor(out=xres[:, f, 5:1029].rearrange("p (s n) -> p s n", s=2),
                                                                  in0=ps[:, 0:2, 0:512], in1=xres[:, f, 5:1029].rearrange("p (s n) -> p s n", s=2),
                                                                  op=ALU.add), reads=Rb[0:2] + [Rxres[f]], writes=[Rxres[f]])
                if last:
                    sumsq_hook(f, 5, 1024, f == 0)
        task(blk(w_down_b, cb, 44)[:, g0:g0 + gn, :], gn, 256, consume)

    for gi, (g0, gn) in enumerate(groups):
        for jj in range(g0 // 2, (g0 + gn) // 2):
            ffg_task(jj)
            ffu_task(jj, g0)
        for cb in range(8):
            wdown_task(g0, gn, cb, gi == len(groups) - 1)

    own_slices = [(0, 512), (512, 512)]

    def norm3():
        tap("x2", arena[:, 0:16464], Rxres)
        for k in range(16):
            for si in range(2):
                P.alias(RhT3[k][si], RhT2[k])
        for si in range(2):
            P.alias(Rrs2_3[si], Rrs2_3)
        norm_finish_sliced(5, own_slices, C_G3, hT, RhT3, Rrs2_3)
        P.alias(Rppw, RaT)
        P.alias(RpTb, RaT)
        P.op("pool", lambda e: e.dma_start(out=ppw, in_=w_ppv[:, :, :]), writes=[Rppw], dma=True)
        P.op("pool", lambda e: e.dma_start(out=pTb, in_=pTv[:, :, :]), writes=[RpTb], dma=True)
        for r in Rsgt + Rppt:
            P.alias(r, Rupg + Rupu)
    call(norm3)
    pairs = [((psA[:, 0, :], RpsA[0]), (psA[:, 1, :], RpsA[1])),
             ((psA[:, 2, :], RpsA[2]), (psB[:, 0, :], RpsB[0])),
             ((psB[:, 1, :], RpsB[1]), (psB[:, 2, :], RpsB[2]))]
    pr = {"i": 0}
    Rx2 = [[Res("x2_%d_%d" % (f, sl_)) for sl_ in range(2)] for f in range(16)]
    Racc2 = [Res("acc2_0"), Res("acc2_1")]
    Rrs2s = [Res("rs2s0"), Res("rs2s1")]

    def ple_alias():
        for f in range(16):
            for sl_ in range(2):
                P.alias(Rx2[f][sl_], [Rxres[f]])
        for sl_ in range(2):
            P.alias(Racc2[sl_], [Racc])
            P.alias(Rrs2s[sl_], [Rrs2] + Rrs2_3)
    call(ple_alias)

    def pg_task(b, s):
        def consume(slot, Rs):
            for j in range(2):
                f = 2 * b + j
                (pg_, Rg), (pp_, Rp) = pairs[pr["i"] % 3]
                bi = pr["i"] % 2
                pr["i"] += 1
                xs = xres[:, f, 5 + 512 * s:5 + 512 * (s + 1)]
                for k in range(16):
                    P.op("pe", lambda e, k=k, pg_=pg_, j=j: e.matmul(pg_[:, 0:512], lhsT=slot[:, k, j * 128:(j + 1) * 128],
                                                                     rhs=hT[:, k, 512 * s:512 * (s + 1)], start=(k == 0), stop=(k == 15)),
                         reads=[Rs, RhT3[k][s]], writes=[Rg])
                for k in range(2):
                    P.op("pe", lambda e, k=k, pp_=pp_, f=f: e.matmul(pp_[:, 0:512], lhsT=ppw[:, k, f * 128:(f + 1) * 128],
                                                                     rhs=pTb[:, k, 512 * s:512 * (s + 1)], start=(k == 0), stop=(k == 1)),
                         reads=[Rppw, RpTb], writes=[Rp])
                P.op("act", lambda e, pg_=pg_, bi=bi: e.activation(out=sgt[bi], in_=pg_[:, 0:512], func=AF.Sigmoid), reads=[Rg], writes=[Rsgt[bi]])
                P.op("dve", lambda e, pp_=pp_, bi=bi: e.tensor_tensor(out=ppt[bi], in0=pp_[:, 0:512], in1=sgt[bi], op=ALU.mult),
                     reads=[Rp, Rsgt[bi]], writes=[Rppt[bi]])
                P.op("dve", lambda e, bi=bi, xs=xs: e.tensor_tensor(out=xs, in0=xs, in1=ppt[bi], op=ALU.add),
                     reads=[Rppt[bi], Rx2[f][s]], writes=[Rx2[f][s]])
                a_ = acc[:, 512 * s:512 * (s + 1)]
                if f == 0:
                    P.op("act", lambda e, xs=xs, a_=a_: e.activation(out=a_, in_=xs, func=AF.Square), reads=[Rx2[f][s]], writes=[Racc2[s]])
                else:
                    P.op("act", lambda e, xs=xs: e.activation(out=tmpq[:, 0:512], in_=xs, func=AF.Square), reads=[Rx2[f][s]], writes=[Rtmpq])
                    P.op("dve", lambda e, a_=a_: e.tensor_tensor(out=a_, in0=a_, in1=tmpq[:, 0:512], op=ALU.add),
                         reads=[Racc2[s], Rtmpq], writes=[Racc2[s]])
        task(blk(w_pg_b, b), 16, 256, consume)

    def final_rs(s):
        P.op("pe", lambda e: e.matmul(psC[:, 0:512], lhsT=onesF, rhs=acc[:, 512 * s:512 * (s + 1)], start=True, stop=True),
             reads=[Racc2[s], Rcst], writes=[RpsC])
        rstd_from_psum(psC[:, 0:512], rs2[:, 512 * s:512 * (s + 1)], RpsC, Rrs2s[s], 2048.0)

    def final_ops(s, ks):
        for k in ks:
            xs = xres[:, k, 5 + 512 * s:5 + 512 * (s + 1)]
            P.op("dve", lambda e, k=k, xs=xs: e.scalar_tensor_tensor(out=xs, in0=xs, scalar=cs(C_GF + k), in1=rs2[:, 512 * s:512 * (s + 1)],
                                                                      op0=ALU.mult, op1=ALU.mult),
                 reads=[Rx2[k][s], Rrs2s[s], Rcst], writes=[Rx2[k][s]])
            P.op("sp", lambda e, k=k, xs=xs: e.dma_start(out=outTv[:, k, 512 * s:512 * (s + 1)], in_=xs),
                 reads=[Rx2[k][s]], writes=[Res("o")], dma=True)

    for b in range(8):
        pg_task(b, 0)
    call(lambda: final_rs(0))
    for b in range(8):
        pg_task(b, 1)
        call(lambda b=b: final_ops(0, [2 * b, 2 * b + 1]))
    call(table_preload)
    call(lambda: final_rs(1))
    call(lambda: final_ops(1, range(16)))

    def finish():
        Rfin = Res("fin")
        for o in P.ops["sp"]:
            if o.is_dma:
                Rfin.rdma.append(o)
        P.op("sp", lambda e: e.nop(), writes=[Rfin])
    call(finish)

    run_tasks()
    P.emit()
    return nc


def _consts(norm_mix_g, norm_ffn_g, norm_ple_g, final_norm_g, short_conv_w, ffn_conv_w, ffn_conv_b,
            b_igate, b_fgate, mh_norm_g):
    c = np.zeros((128, NCST), np.float32)
    pc = lambda v: np.ascontiguousarray(np.asarray(v, np.float32).reshape(-1, 128).T)
    c[:, C_G1:C_G1 + 16] = pc(norm_mix_g)
    c[:, C_G2:C_G2 + 16] = pc(norm_ffn_g)
    c[:, C_G3:C_G3 + 16] = pc(norm_ple_g)
    c[:, C_GF:C_GF + 16] = pc(final_norm_g)
    scw = np.asarray(short_conv_w, np.float32).reshape(3, 8, 128)
    c[:, C_SCW:C_SCW + 24] = scw.transpose(2, 1, 0).reshape(128, 24)
    fcw = np.asarray(ffn_conv_w, np.float32).reshape(3, 88, 128)
    c[:, C_FCW:C_FCW + 264] = fcw.transpose(2, 1, 0).reshape(128, 264)
    c[:, C_FCB:C_FCB + 88] = pc(ffn_conv_b)
    c[:, C_BIF:C_BIF + 4] = np.asarray(b_igate, np.float32).reshape(1, 4)
    c[:, C_BIF + 4:C_BIF + 8] = np.asarray(b_fgate, np.float32).reshape(1, 4)
    c[:, C_GMH:C_GMH + 1024] = np.asarray(mh_norm_g, np.float32).reshape(1, 1024)
    c[:, C_TRI:C_TRI + 128] = np.triu(np.ones((128, 128), np.float32))
    c[:, C_ONE:C_ONE + 128] = 1.0
    c[:, C_IDN:C_IDN + 128] = np.eye(128, dtype=np.float32)
    return c


def make_in_maps(x, p, norm_mix_g, w_in, b_igate, b_fgate, short_conv_w, mh_norm_g, w_out,
                 norm_ffn_g, w_up, ffn_conv_w, ffn_conv_b, w_down, norm_ple_g, w_pg, w_pp, final_norm_g):
    x = np.asarray(x, np.float32)
    p = np.asarray(p, np.float32)
    cst = _consts(norm_mix_g[0], norm_ffn_g[0], norm_ple_g[0], final_norm_g, short_conv_w[0], ffn_conv_w[0],
                  ffn_conv_b[0], b_igate[0], b_fgate[0], mh_norm_g[0])
    def blocked(w, nb):
        w = np.asarray(w, np.float32)
        kc = w.shape[0] // 128
        return np.ascontiguousarray(w.reshape(kc, 128, nb, 256).transpose(2, 1, 0, 3)).reshape(nb, 128, kc * 256)
    w_in0 = np.asarray(w_in[0], np.float32)
    shared = {
        "w_in_b": blocked(w_in0[:, 0:6144], 24),
        "w_if": np.ascontiguousarray(w_in0[:, 6144:6152].reshape(16, 128, 8).transpose(1, 0, 2)).reshape(128, 128),
        "w_out_b": blocked(w_out[0], 8),
        "w_up_b": blocked(w_up[0], 44),
        "w_down_b": blocked(w_down[0], 8),
        "w_pg_b": blocked(w_pg[0], 8),
        "w_pp": np.ascontiguousarray(np.asarray(w_pp[0], np.float32)),
        "cst": cst,
    }
    maps = []
    for c in range(8):
        b, half = c // 2, c % 2
        xT = np.zeros((2048, 2048), np.float32)
        if half == 1:
            xT[:, 0:1024] = x[b, 0:1024].T
        xT[:, 1024:2048] = x[b, half * 1024:(half + 1) * 1024].T
        pT = np.ascontiguousarray(p[0, b, half * 1024:(half + 1) * 1024].T)
        m = dict(shared)
        x4 = xT.reshape(16, 128, 2048)
        parts = []
        for (c0_, n_, W_) in ((0, 896, 128), (896, 1152, 256)):
            for s_ in range((n_ + W_ - 1) // W_):
                w_ = min(W_, n_ - W_ * s_)
                a0 = c0_ + W_ * s_
                parts.append(x4[:, :, a0:a0 + w_].transpose(1, 0, 2).reshape(128, 16 * w_))
        m["xsl"] = np.ascontiguousarray(np.concatenate(parts, axis=1))
        m["xr"] = np.ascontiguousarray(x4[:, :, 1019:2048].transpose(1, 0, 2)).reshape(128, 16 * 1029)
        m["pT"] = pT
        maps.append(m)
    return maps


def kernel(x, p, norm_mix_g, w_in, b_igate, b_fgate, short_conv_w, mh_norm_g, w_out,
           norm_ffn_g, w_up, ffn_conv_w, ffn_conv_b, w_down, norm_ple_g, w_ple_gate,
           w_ple_proj, final_norm_g):
    maps = make_in_maps(x, p, norm_mix_g, w_in, b_igate, b_fgate, short_conv_w, mh_norm_g, w_out,
                        norm_ffn_g, w_up, ffn_conv_w, ffn_conv_b, w_down, norm_ple_g, w_ple_gate,
                        w_ple_proj, final_norm_g)
    nc = build_nc()
    res = run_bass_kernel_spmd(nc, maps, core_ids=list(range(8)))
    out = np.empty((4, 2048, 2048), np.float32)
    for c in range(8):
        b, half = c // 2, c % 2
        out[b, half * 1024:(half + 1) * 1024, :] = res.results[c]["outT"].T
    return out
```

```python
import contextlib
import numpy as np
import concourse.bass as bass
import concourse.mybir as mybir
from concourse.bass_utils import run_bass_kernel_spmd

F32 = mybir.dt.float32
BF16 = mybir.dt.bfloat16
AF = mybir.ActivationFunctionType
ALU = mybir.AluOpType

EPS = 1e-6
NS = 4
N_DMA_SEMS = 8

C_G1, C_G2, C_G3, C_GF = 0, 16, 32, 48
C_SCW = 64
C_FCW = 88
C_FCB = 352
C_BIF = 440
C_GMH = 448
C_TRI = 1472
C_ONE = 1600
C_IDN = 1728
NCST = 1856


class Res:
    __slots__ = ("name", "writer", "readers", "rdma")

    def __init__(self, name):
        self.name = name
        self.writer = None
        self.readers = {}
        self.rdma = []


class Op:
    __slots__ = ("eng", "fn", "deps", "signal", "sem", "val", "is_dma")

    def __init__(self, eng, fn, is_dma):
        self.eng = eng
        self.fn = fn
        self.deps = []
        self.signal = False
        self.sem = None
        self.val = 0
        self.is_dma = is_dma


ENGS = ("pe", "act", "dve", "pool", "sp")


class Prog:
    def __init__(self, nc):
        self.nc = nc
        self.ops = {e: [] for e in ENGS}

    def op(self, eng, fn, reads=(), writes=(), dma=False):
        o = Op(eng, fn, dma)
        deps = []
        for r in reads:
            if r.writer is not None:
                deps.append(r.writer)
        for w in writes:
            if w.writer is not None:
                deps.append(w.writer)
            deps.extend(w.readers.values())
            deps.extend(w.rdma)
        seen = set()
        for d in deps:
            if id(d) in seen or d is o:
                continue
            seen.add(id(d))
            if d.eng == "pe" and eng == "pe" and not d.is_dma:
                continue
            d.signal = True
            o.deps.append(d)
        for r in reads:
            if dma:
                r.rdma.append(o)
            else:
                r.readers[eng] = o
        for w in writes:
            w.writer = o
            w.readers = {}
            w.rdma = []
        self.ops[eng].append(o)
        return o

    def alias(self, new, olds):
        for r in olds:
            for d in ([r.writer] if r.writer is not None else []) + list(r.readers.values()) + r.rdma:
                new.rdma.append(d)

    def emit(self):
        nc = self.nc
        with contextlib.ExitStack() as st:
            esem = {e: st.enter_context(nc.semaphore("s_" + e)) for e in ENGS}
            dsem = {e: [st.enter_context(nc.semaphore("d_%s%d" % (e, i))) for i in range(N_DMA_SEMS)]
                    for e in ("sp", "act", "pool")}
            for e in ENGS:
                cnt = 0
                nd = 0
                for o in self.ops[e]:
                    if o.is_dma:
                        o.sem = dsem[e][nd % N_DMA_SEMS]
                        o.val = 16 * (nd // N_DMA_SEMS + 1)
                        nd += 1
                    elif o.signal:
                        cnt += 1
                        o.sem = esem[e]
                        o.val = cnt
            block = st.enter_context(nc.Block())

            def run(e, eng):
                seen = {}
                nd = 0
                for o in self.ops[e]:
                    waits = {}
                    for d in o.deps:
                        k = id(d.sem)
                        if k not in waits or waits[k][1] < d.val:
                            waits[k] = (d.sem, d.val)
                    if o.is_dma:
                        if nd >= N_DMA_SEMS:
                            k = id(o.sem)
                            v = o.val - 16
                            if k not in waits or waits[k][1] < v:
                                waits[k] = (o.sem, v)
                        nd += 1
                    for k, (s, v) in waits.items():
                        if seen.get(k, 0) >= v:
                            continue
                        seen[k] = v
                        eng.wait_ge(s, v)
                    ins = o.fn(eng)
                    if o.is_dma:
                        ins.then_inc(o.sem, 16)
                    elif o.signal:
                        ins.then_inc(o.sem, 1)

            @block.tensor
            def _(eng):
                run("pe", eng)

            @block.scalar
            def _(eng):
                run("act", eng)

            @block.vector
            def _(eng):
                run("dve", eng)

            @block.gpsimd
            def _(eng):
                run("pool", eng)

            @block.sync
            def _(eng):
                run("sp", eng)


def build_nc(debug=()):
    nc = bass.Bass("TRN2", target_bir_lowering=False)
    dt_in = lambda n, s: nc.dram_tensor(n, s, F32, kind="ExternalInput").ap()
    xsl = dt_in("xsl", [128, 16 * 2048])
    xr = dt_in("xr", [128, 16 * 1029])
    pT = dt_in("pT", [256, 1024])
    w_in_b = dt_in("w_in_b", [24, 128, 16 * 256])
    w_if = dt_in("w_if", [128, 16 * 8])
    w_out_b = dt_in("w_out_b", [8, 128, 16 * 256])
    w_up_b = dt_in("w_up_b", [44, 128, 16 * 256])
    w_down_b = dt_in("w_down_b", [8, 128, 44 * 256])
    w_pg_b = dt_in("w_pg_b", [8, 128, 16 * 256])
    w_pp = dt_in("w_pp", [256, 2048])
    cst_d = dt_in("cst", [128, NCST])
    outT = nc.dram_tensor("outT", [2048, 1024], F32, kind="ExternalOutput").ap()
    dbg_out = {}
    for name, shape in debug:
        dbg_out[name] = nc.dram_tensor("dbg_" + name, list(shape), F32, kind="ExternalOutput").ap()

    kview = lambda w: w.rearrange("(k p) n -> p k n", p=128)
    w_ppv, pTv, outTv = map(kview, (w_pp, pT, outT))
    blk = lambda wb, cb, kc=16: wb[cb].rearrange("p (k n) -> p k n", k=kc)
    slab_off = {}
    _o = 0
    for (c0_, n_, W_) in ((0, 896, 128), (896, 1152, 256)):
        for s_ in range((n_ + W_ - 1) // W_):
            w_ = min(W_, n_ - W_ * s_)
            slab_off[(c0_, s_)] = (_o, w_)
            _o += 16 * w_

    P = Prog(nc)
    sb = nc.alloc_sbuf_tensor

    cst = sb("cst_sb", [128, NCST], F32)
    cbf = sb("cbf", [128, 256], BF16)
    ones_bf = cbf[:, 0:128]
    ident_bf = cbf[:, 128:256]
    wsl = sb("wsl", [128, NS, 4096], BF16)
    hT = sb("hT", [128, 16, 1152], BF16)
    arena = sb("arena", [128, 16464], F32)
    yTt = sb("yT", [128, 16464], BF16)
    NE = 7296
    regE = sb("regE", [128, NE], F32)

    psA = nc.alloc_psum_tensor("psA", [128, 3, 512], F32)
    psB = nc.alloc_psum_tensor("psB", [128, 3, 512], F32)
    psC = nc.alloc_psum_tensor("psC", [128, 512], F32)
    psD = nc.alloc_psum_tensor("psD", [128, 1024], BF16)
    RpsA = [Res("psA%d" % i) for i in range(3)]
    RpsB = [Res("psB%d" % i) for i in range(3)]
    RpsC = Res("psC")
    RpsD = Res("psD")
    RpsDh = [Res("psDh0"), Res("psDh1")]
    banksA = [(psA[:, i, :], RpsA[i]) for i in range(3)]
    banksB = [(psB[:, i, :], RpsB[i]) for i in range(3)]
    pools = {"tm": banksA + banksB, "fm": [(psA, RpsA), (psB, RpsB)], "tmi": 0, "fmi": 0}

    def set_pools(tm, fm):
        pools["tm"] = tm
        pools["fm"] = fm

    Rcst = Res("cst")
    Rcbf = Res("cbf")
    RhT = [Res("hT%d" % k) for k in range(16)]

    ar_bf = arena[:, :].bitcast(BF16)
    slab = [arena[:, 0:4096].rearrange("p (k n) -> p k n", k=16),
            arena[:, 4096:8192].rearrange("p (k n) -> p k n", k=16)]
    Rslab = [Res("slab0"), Res("slab1")]
    slab_p = [arena[:, 2048 * i:2048 * (i + 1)].rearrange("p (k n) -> p k n", k=16) for i in range(4)]
    Rslab_p = [Res("slabp%d" % i) for i in range(4)]
    k_tm = ar_bf[:, 0:4608].rearrange("p (t n) -> p t n", t=9)
    vw = ar_bf[:, 4608:13824].rearrange("p (t h n) -> p t h n", t=9, h=4)
    kT = ar_bf[:, 13824:18432].rearrange("p (h n) -> p h n", h=4)
    qT = ar_bf[:, 18432:23040].rearrange("p (h n) -> p h n", h=4)
    og = ar_bf[:, 23040:32256].rearrange("p (t n) -> p t n", t=9)
    k_tm_p = ar_bf[:, 18432:22016].rearrange("p (t n) -> p t n", t=7)
    vw_p = ar_bf[:, 22016:29184].rearrange("p (t h n) -> p t h n", t=7, h=4)
    RqT = [Res("qT%d" % h) for h in range(4)]
    RkT = [Res("kT%d" % h) for h in range(4)]
    Rk_tm = [Res("k_tm%d" % t) for t in range(9)]
    Rvw = [Res("vw%d" % t) for t in range(9)]
    Rog = [Res("og%d" % t) for t in range(9)]
    Rk_tm_p = [Res("k_tm_p%d" % t) for t in range(7)]
    Rvw_p = [Res("vw_p%d" % t) for t in range(7)]
    resAB = Rk_tm + Rvw + RkT
    resC = RqT + Rog
    resCp = Rk_tm_p + Rvw_p
    xres = arena[:, 0:16464].rearrange("p (k n) -> p k n", k=16)
    Rxres = [Res("xres%d" % k) for k in range(16)]

    yT = yTt[:, :].rearrange("p (k n) -> p k n", k=16)
    RyT = [Res("yT%d" % k) for k in range(16)]
    hTp = yTt[:, 0:16 * 896].rearrange("p (k n) -> p k n", k=16)
    RhTp = [Res("hTp%d" % k) for k in range(16)]
    aT = yTt[:, 0:16 * 1026].rearrange("p (k n) -> p k n", k=16)
    RaT = [Res("aT%d" % k) for k in range(16)]
    ppw = yTt[:, 0:4096].rearrange("p (k n) -> p k n", k=2)
    Rppw = Res("ppw")
    pTb = yTt[:, 4096:6144].rearrange("p (k n) -> p k n", k=2)
    RpTb = Res("pTb")

    e_bf = regE[:, :].bitcast(BF16)
    E32 = regE[:, 0:1024].rearrange("p (h n) -> p h n", h=4)
    Cbf = e_bf[:, 2048:3072].rearrange("p (h n) -> p h n", h=4)
    Sm = e_bf[:, 3072:3584].rearrange("p (h n) -> p h n", h=4)
    ym = e_bf[:, 3584:4608]
    junk = regE[:, 2304:2560]
    so = [2560]

    def small(n):
        v = regE[:, so[0]:so[0] + n]
        so[0] += n
        return v

    class GateSet:
        def __init__(self, T, tag):
            self.T = T
            r3 = lambda v: v.rearrange("p (t n) -> p t n", t=T)
            self.gates = r3(small(8 * T))
            self.etmp = r3(small(4 * T))
            self.nlf = r3(small(4 * T))
            self.arg = r3(small(4 * T))
            self.wp = r3(small(4 * T))
            self.ebt = r3(small(4 * T))
            self.ebs = r3(small(4 * T))
            o = so[0]
            self.wpb = e_bf[:, 2 * o:2 * o + 4 * T].rearrange("p (t n) -> p t n", t=T)
            so[0] += 2 * T
            mk = lambda n: Res(n + tag)
            self.Rgates, self.Retmp, self.Rnlf, self.Rarg = mk("gates"), mk("etmp"), mk("nlf"), mk("arg")
            self.Rwp, self.Rwpb, self.Rebt, self.Rebs = mk("wp"), mk("wpb"), mk("ebt"), mk("ebs")

        def all_res(self):
            return [self.Rgates, self.Retmp, self.Rnlf, self.Rarg, self.Rwp, self.Rwpb, self.Rebt, self.Rebs]

    Gp = GateSet(7, "_p")
    Gx = GateSet(9, "_x")
    En = small(4)
    d1 = small(4); d2 = small(4); d3 = small(4); rec = small(4); cc = small(4)
    ss = small(4); t1 = small(4); t2 = small(4); lnr = small(4); rr = small(4); rowsc = small(4)
    nbf = e_bf[:, 2 * so[0]:2 * so[0] + 4]; so[0] += 2
    rs_col = small(8)
    Rrs_col = Res("rs_col")
    rs = small(512)
    assert so[0] <= 4100
    RE32 = [Res("E32_%d" % h) for h in range(4)]
    RCbf = [Res("Cbf_%d" % h) for h in range(4)]
    RSm = Res("Sm"); Rym = Res("ym"); Rjunk = Res("junk")
    REn = Res("En"); Rnbf = Res("nbf")
    Rsm = {n: Res(n) for n in "d1 d2 d3 rec cc ss t1 t2 lnr rr rowsc".split()}
    Rrs = Res("rs")
    mlstm_small_res = RE32 + RCbf + [RSm, Rym, Rjunk, REn, Rnbf, Rrs] + list(Rsm.values()) + Gp.all_res() + Gx.all_res()
    gcs = [regE[:, 4100:5135].rearrange("p (s n) -> p s n", s=3), regE[:, 5135:6170].rearrange("p (s n) -> p s n", s=3)]
    cacc = regE[:, 6170:7199].rearrange("p (s n) -> p s n", s=3)
    Rgcs = [Res("gcs0"), Res("gcs1")]
    Rcacc = Res("cacc")
    conv_res = Rgcs + [Rcacc]
    sqd = e_bf[:, 8200:12296].rearrange("p (k n) -> p k n", k=16)
    Rsqd = Res("sqd")
    upg = [regE[:, 0:1026].rearrange("p (s n) -> p s n", s=3), regE[:, 1026:2052].rearrange("p (s n) -> p s n", s=3)]
    upu = [regE[:, 2052:3078].rearrange("p (s n) -> p s n", s=3), regE[:, 3078:4104].rearrange("p (s n) -> p s n", s=3)]
    Rupg = [Res("upg0"), Res("upg1")]
    Rupu = [Res("upu0"), Res("upu1")]
    rs2 = regE[:, 4140:5169]
    acc = regE[:, 5169:6198]
    tmpq = regE[:, 6198:7227]
    Rrs2 = Res("rs2"); Racc = Res("acc"); Rtmpq = Res("tmpq")
    sgt = [regE[:, 0:512], regE[:, 512:1024]]
    ppt = [regE[:, 1024:1536], regE[:, 1536:2048]]
    Rsgt = [Res("sgt0"), Res("sgt1")]
    Rppt = [Res("ppt0"), Res("ppt1")]

    cs = lambda off, n=1: cst[:, off:off + n]
    triM = cst[:, C_TRI:C_TRI + 128]
    onesF = cst[:, C_ONE:C_ONE + 128]

    Rslot = [Res("slot%d" % i) for i in range(NS)]
    wq = []
    wstate = {"issued": 0}

    def wget(i):
        while wstate["issued"] < min(i + NS, len(wq)):
            j = wstate["issued"]
            src, kc, ncol = wq[j]
            dst = wsl[:, j % NS, 0:kc * ncol].rearrange("p (k n) -> p k n", k=kc)
            o = P.op("pool", lambda e, dst=dst, src=src: e.dma_start(out=dst, in_=src),
                     writes=[Rslot[j % NS]], dma=True)
            if j < NS and wstate.get("gate") is not None:
                g = wstate["gate"]
                g.signal = True
                o.deps.append(g)
            wstate["issued"] += 1
        src, kc, ncol = wq[i]
        return wsl[:, i % NS, 0:kc * ncol].rearrange("p (k n) -> p k n", k=kc), Rslot[i % NS]

    tasks = []

    def task(src, kc, ncol, fn):
        tasks.append(("w", (src, kc, ncol), fn))

    def call(fn):
        tasks.append(("c", None, fn))

    def run_tasks():
        for t in tasks:
            if t[0] == "w":
                wq.append(t[1])
        n = 0
        for t in tasks:
            if t[0] == "w":
                slot, rs_ = wget(n)
                n += 1
                t[2](slot, rs_)
            else:
                t[2]()

    def tap(name, ap, reads):
        if name in dbg_out:
            P.op("pool", lambda e: e.dma_start(out=dbg_out[name], in_=ap), reads=reads, writes=[Res("dbg")], dma=True)

    def rstd_from_psum(ps_ap, out_rs, Rps, Rout, div):
        P.op("act", lambda e: e.activation(out=out_rs, in_=ps_ap, func=AF.Ln, bias=EPS, scale=1.0 / div),
             reads=[Rps], writes=[Rout])
        P.op("act", lambda e: e.activation(out=out_rs, in_=out_rs, func=AF.Exp, scale=-0.5),
             reads=[Rout], writes=[Rout])

    def norm_slab(c0, s, n, goff, dst, Rdst, W=256):
        w = min(W, n - s * W)
        if W == 256:
            sl = slab[s % 2][:, :, 0:w]
            Rs = Rslab[s % 2]
        else:
            sl = slab_p[s % 4][:, :, 0:w]
            Rs = Rslab_p[s % 4]
        so_, _w = slab_off[(c0, s)]
        assert _w == w
        o = P.op("sp", lambda e: e.dma_start(out=sl, in_=xsl[:, so_:so_ + 16 * w].rearrange("p (k n) -> p k n", k=16)),
                 writes=[Rs], dma=True)
        if c0 == 0 and s == 3:
            wstate["gate"] = o
        sq = sqd[:, :, 0:w]
        P.op("act", lambda e: e.activation(out=sq, in_=sl, func=AF.Square), reads=[Rs], writes=[Rsqd])
        for k in range(16):
            P.op("pe", lambda e, k=k: e.matmul(psC[:, 0:w], lhsT=ones_bf, rhs=sqd[:, k, 0:w], start=(k == 0), stop=(k == 15)),
                 reads=[Rsqd, Rcbf], writes=[RpsC])
        rstd_from_psum(psC[:, 0:w], rs[:, 0:w], RpsC, Rrs, 2048.0)
        for k in range(16):
            if True:
                P.op("dve", lambda e, k=k: e.scalar_tensor_tensor(
                    out=dst[:, k, s * W:s * W + w], in0=sl[:, k, :], scalar=cs(goff + k), in1=rs[:, 0:w],
                    op0=ALU.mult, op1=ALU.mult), reads=[Rs, Rrs, Rcst], writes=[Rdst[k]])
            else:
                Rk = Res("slabk")
                P.op("pool", lambda e, k=k: e.tensor_scalar(out=sl[:, k, :], in0=sl[:, k, :], scalar1=cs(goff + k), scalar2=None, op0=ALU.mult),
                     reads=[Rs, Rcst], writes=[Rk])
                P.op("pool", lambda e, k=k: e.tensor_tensor(out=dst[:, k, s * 256:s * 256 + w], in0=sl[:, k, :], in1=rs[:, 0:w], op=ALU.mult),
                     reads=[Rs, Rk, Rrs], writes=[Rdst[k]])

    def prefix_slab(s):
        sl = slab_p[s % 4]
        Rs = Rslab_p[s % 4]
        so_, _w = slab_off[(0, s)]
        o = P.op("sp", lambda e: e.dma_start(out=sl, in_=xsl[:, so_:so_ + 16 * 128].rearrange("p (k n) -> p k n", k=16)),
                 writes=[Rs], dma=True)
        if s == 3:
            wstate["gate"] = o
        P.op("act", lambda e: e.activation(out=sqd[:, :, 0:128], in_=sl, func=AF.Square), reads=[Rs], writes=[Rsqd])
        for k in range(16):
            P.op("pe", lambda e, k=k: e.matmul(psC[:, 64 + s:65 + s], lhsT=sqd[:, k, 0:128], rhs=ones_bf[:, 0:1], start=(k == 0), stop=(k == 15)),
                 reads=[Rsqd, Rcbf], writes=[RpsC])
        P.op("dve", lambda e: e.tensor_tensor(out=hTp[:, :, 128 * s:128 * (s + 1)], in0=sl,
                                              in1=cst[:, C_G1:C_G1 + 16].unsqueeze(2).broadcast_to([128, 16, 128]), op=ALU.mult),
             reads=[Rs, Rcst], writes=RhTp)

    def prefix_rs():
        P.op("act", lambda e: e.activation(out=rs_col[:, 0:7], in_=psC[:, 64:71], func=AF.Ln, bias=EPS, scale=1.0 / 2048.0),
             reads=[RpsC], writes=[Rrs_col])
        P.op("act", lambda e: e.activation(out=rs_col[:, 0:7], in_=rs_col[:, 0:7], func=AF.Exp, scale=-0.5),
             reads=[Rrs_col], writes=[Rrs_col])

    def sumsq_hook(f, col0, w, first):
        if first:
            P.op("act", lambda e: e.activation(out=acc[:, 0:w], in_=xres[:, f, col0:col0 + w], func=AF.Square),
                 reads=[Rxres[f]], writes=[Racc])
        else:
            P.op("act", lambda e: e.activation(out=tmpq[:, 0:w], in_=xres[:, f, col0:col0 + w], func=AF.Square),
                 reads=[Rxres[f]], writes=[Rtmpq])
            P.op("dve", lambda e: e.tensor_tensor(out=acc[:, 0:w], in0=acc[:, 0:w], in1=tmpq[:, 0:w], op=ALU.add),
                 reads=[Racc, Rtmpq], writes=[Racc])

    def norm_finish_sliced(col0, slices, goff, dst, Rdst2, Rrs_list):
        for si, (s0, w) in enumerate(slices):
            P.op("pe", lambda e, s0=s0, w=w: e.matmul(psC[:, 0:w], lhsT=onesF, rhs=acc[:, s0:s0 + w], start=True, stop=True),
                 reads=[Racc, Rcst], writes=[RpsC])
            rstd_from_psum(psC[:, 0:w], rs2[:, s0:s0 + w], RpsC, Rrs_list[si], 2048.0)
            for k in range(16):
                P.op("dve", lambda e, k=k, s0=s0, w=w: e.scalar_tensor_tensor(
                    out=dst[:, k, s0:s0 + w], in0=xres[:, k, col0 + s0:col0 + s0 + w], scalar=cs(goff + k), in1=rs2[:, s0:s0 + w],
                    op0=ALU.mult, op1=ALU.mult), reads=[Rxres[k], Rrs_list[si], Rcst], writes=[Rdst2[k][si]])

    def norm_finish(col0, W, slices, goff, out_fn, out_res_fn, post=None):
        for (s0, w) in slices:
            P.op("pe", lambda e, s0=s0, w=w: e.matmul(psC[:, 0:w], lhsT=onesF, rhs=acc[:, s0:s0 + w], start=True, stop=True),
                 reads=[Racc, Rcst], writes=[RpsC])
            rstd_from_psum(psC[:, 0:w], rs2[:, s0:s0 + w], RpsC, Rrs2, 2048.0)
        for k in range(16):
            if True:
                P.op("dve", lambda e, k=k: e.scalar_tensor_tensor(
                    out=out_fn(k), in0=xres[:, k, col0:col0 + W], scalar=cs(goff + k), in1=rs2[:, 0:W],
                    op0=ALU.mult, op1=ALU.mult), reads=[Rxres[k], Rrs2, Rcst], writes=[out_res_fn(k)])
            else:
                P.op("pool", lambda e, k=k: e.tensor_scalar(out=tmpq[:, 0:W], in0=xres[:, k, col0:col0 + W], scalar1=cs(goff + k),
                                                            scalar2=None, op0=ALU.mult), reads=[Rxres[k], Rcst], writes=[Rtmpq])
                P.op("pool", lambda e, k=k: e.tensor_tensor(out=out_fn(k), in0=tmpq[:, 0:W], in1=rs2[:, 0:W], op=ALU.mult),
                     reads=[Rtmpq, Rrs2], writes=[out_res_fn(k)])
            if post is not None:
                post(k)

    def next_bank():
        tm = pools["tm"]
        b = tm[pools["tmi"] % len(tm)]
        pools["tmi"] += 1
        return b

    def next_group():
        fm = pools["fm"]
        g = fm[pools["fmi"] % len(fm)]
        pools["fmi"] += 1
        return g

    def tok_major_proj(slot, Rs, src, Rsrc, tile_list, ncol, evac):
        for t in tile_list:
            ps, Rp = next_bank()
            for k in range(16):
                P.op("pe", lambda e, k=k, t=t, ps=ps: e.matmul(ps[:, 0:ncol], lhsT=src[:, k, t * 128:(t + 1) * 128],
                                                                rhs=slot[:, k, 0:ncol], start=(k == 0), stop=(k == 15)),
                     reads=[Rsrc[k], Rs], writes=[Rp])
            evac(t, ps[:, 0:ncol], Rp)

    def feat_major_proj(slot, Rs, j, src, Rsrc, kc, col_slices, Rsrc_fn=None, k_outer=False):
        ps, Rb = next_group()
        if k_outer:
            order = [(si, k) for k in range(kc) for si in range(len(col_slices))]
        else:
            order = [(si, k) for si in range(len(col_slices)) for k in range(kc)]
        for si, k in order:
            c0, w = col_slices[si]
            rd = [Rs] + (Rsrc_fn(k, si) if Rsrc_fn is not None else [Rsrc[k]])
            P.op("pe", lambda e, k=k, si=si, c0=c0, w=w: e.matmul(
                ps[:, si, 0:w], lhsT=slot[:, k, j * 128:(j + 1) * 128], rhs=src[:, k, c0:c0 + w],
                start=(k == 0), stop=(k == kc - 1)),
                 reads=rd, writes=[Rb[si]])
        return ps, Rb

    def gate_math(G):
        T = G.T
        P.op("act", lambda e: e.activation(out=G.etmp, in_=G.gates[:, :, 4:8], func=AF.Exp, scale=-1.0),
             reads=[G.Rgates], writes=[G.Retmp])
        P.op("act", lambda e: e.activation(out=G.nlf, in_=G.etmp, func=AF.Ln, bias=1.0),
             reads=[G.Retmp], writes=[G.Rnlf])
        nlf_all = G.nlf.rearrange("p t n -> p (t n)")
        P.op("pe", lambda e: e.matmul(psC[:, 0:4 * T], lhsT=triM, rhs=nlf_all, start=True, stop=True),
             reads=[G.Rnlf, Rcst], writes=[RpsC])
        P.op("pe", lambda e: e.matmul(psC[:, 4 * T:8 * T], lhsT=onesF, rhs=nlf_all, start=True, stop=True),
             reads=[G.Rnlf, Rcst], writes=[RpsC])
        pcum = psC[:, 0:4 * T].rearrange("p (t n) -> p t n", t=T)
        ptot = psC[:, 4 * T:8 * T].rearrange("p (t n) -> p t n", t=T)
        P.op("dve", lambda e: e.tensor_tensor(out=G.arg, in0=pcum, in1=G.gates[:, :, 0:4], op=ALU.add),
             reads=[RpsC, G.Rgates], writes=[G.Rarg])
        P.op("act", lambda e: e.activation(out=G.wp, in_=G.arg, func=AF.Exp), reads=[G.Rarg], writes=[G.Rwp])
        P.op("dve", lambda e: e.tensor_copy(out=G.wpb, in_=G.wp), reads=[G.Rwp], writes=[G.Rwpb])
        P.op("act", lambda e: e.activation(out=G.ebt, in_=ptot, func=AF.Exp, scale=-1.0),
             reads=[RpsC], writes=[G.Rebt])
        P.op("act", lambda e: e.activation(out=G.ebs, in_=pcum, func=AF.Exp, scale=-1.0),
             reads=[RpsC], writes=[G.Rebs])

    def if_task(G, src, Rsrc, rsc=None):
        T = G.T

        def consume(slot, Rs):
            for t in range(T):
                for k in range(16):
                    P.op("pe", lambda e, k=k, t=t: e.matmul(psC[:, t * 8:t * 8 + 8], lhsT=src[:, k, t * 128:(t + 1) * 128],
                                                             rhs=slot[:, k, 0:8], start=(k == 0), stop=(k == 15)),
                         reads=[Rsrc[k], Rs], writes=[RpsC])
            pc = psC[:, 0:8 * T].rearrange("p (t n) -> p t n", t=T)
            bias_b = cst[:, C_BIF:C_BIF + 8].unsqueeze(1).broadcast_to([128, T, 8])
            if rsc is None:
                P.op("dve", lambda e: e.tensor_tensor(out=G.gates, in0=pc, in1=bias_b, op=ALU.add),
                     reads=[RpsC, Rcst], writes=[G.Rgates])
            else:
                P.op("dve", lambda e: e.tensor_tensor(out=G.gates, in0=pc, in1=rsc[:, 0:T].unsqueeze(2).broadcast_to([128, T, 8]), op=ALU.mult),
                     reads=[RpsC, Rrs_col], writes=[G.Rgates])
                P.op("dve", lambda e: e.tensor_tensor(out=G.gates, in0=G.gates, in1=bias_b, op=ALU.add),
                     reads=[G.Rgates, Rcst], writes=[G.Rgates])
            gate_math(G)
        task(w_if[:, :].rearrange("p (k n) -> p k n", k=16), 16, 8, consume)

    def v_task(G, src, Rsrc, vwb, Rvwb, h, rsc=None):
        def consume(slot, Rs):
            def evac(t, ps, Rp):
                if rsc is None:
                    P.op("dve", lambda e: e.tensor_scalar(out=vwb[:, t, h, :], in0=ps, scalar1=G.wp[:, t, h:h + 1], scalar2=None,
                                                          op0=ALU.mult), reads=[Rp, G.Rwp], writes=[Rvwb[t]])
                else:
                    P.op("dve", lambda e: e.tensor_scalar(out=vwb[:, t, h, :], in0=ps, scalar1=rsc[:, t:t + 1], scalar2=G.wp[:, t, h:h + 1],
                                                          op0=ALU.mult, op1=ALU.mult), reads=[Rp, G.Rwp, Rrs_col], writes=[Rvwb[t]])
            tok_major_proj(slot, Rs, src, Rsrc, range(G.T), 256, evac)
        task(blk(w_in_b, 16 + h), 16, 256, consume)

    qk_slices = [(0, 384), (384, 384), (768, 384)]

    def k_task(G, src, Rsrc, kb, Rkb, j, with_kT, rsc=None):
        def consume(slot, Rs):
            def evac(t, ps, Rp):
                if rsc is None:
                    P.op("act", lambda e: e.activation(out=kb[:, t, 256 * j:256 * (j + 1)], in_=ps, func=AF.Copy),
                         reads=[Rp], writes=[Rkb[t]])
                else:
                    P.op("act", lambda e: e.activation(out=kb[:, t, 256 * j:256 * (j + 1)], in_=ps, func=AF.Identity, scale=rsc[:, t:t + 1]),
                         reads=[Rp, Rrs_col], writes=[Rkb[t]])
            tok_major_proj(slot, Rs, src, Rsrc, range(G.T), 256, evac)
            if with_kT and j == 1:
                tb = [(psD[:, 0:512], RpsD), (psC[:, :].bitcast(BF16)[:, 0:512], RpsC)]
                for t in range(G.T):
                    pt, Rpt = tb[t % 2]
                    for h in range(4):
                        P.op("pe", lambda e, t=t, h=h, pt=pt: e.transpose(pt[:, h * 128:(h + 1) * 128], kb[:, t, h * 128:(h + 1) * 128], ident_bf),
                             reads=[Rkb[t], Rcbf], writes=[Rpt])
                    P.op("act", lambda e, t=t, pt=pt: e.activation(out=kT[:, :, t * 128:(t + 1) * 128],
                                                                   in_=pt.rearrange("p (h n) -> p h n", h=4), func=AF.Copy),
                         reads=[Rpt], writes=RkT)
        task(blk(w_in_b, 14 + j), 16, 256, consume)

    def q_task(j):
        def consume(slot, Rs):
            for jj in range(2):
                h = 2 * j + jj
                ps, Rb = feat_major_proj(slot, Rs, jj, hT, RhT, 16, qk_slices)
                P.op("act", lambda e, ps=ps, h=h: e.activation(out=qT[:, h, :].rearrange("p (s n) -> p s n", s=3),
                                                                in_=ps[:, :, 0:384], func=AF.Identity, scale=float(128 ** -0.5)),
                     reads=Rb, writes=[RqT[h]])
        task(blk(w_in_b, 12 + j), 16, 256, consume)

    def o_task(j):
        def consume(slot, Rs):
            def evac(t, ps, Rp):
                P.op("act", lambda e: e.activation(out=junk[:, 0:256], in_=ps, func=AF.Sigmoid), reads=[Rp], writes=[Rjunk])
                P.op("dve", lambda e: e.tensor_tensor(out=og[:, t, 256 * j:256 * (j + 1)], in0=junk[:, 0:256],
                                                      in1=cst[:, C_GMH + 256 * j:C_GMH + 256 * (j + 1)], op=ALU.mult),
                     reads=[Rjunk, Rcst], writes=[Rog[t]])
            tok_major_proj(slot, Rs, hT, RhT, range(9), 256, evac)
        task(blk(w_in_b, 20 + j), 16, 256, consume)

    psNum = [psA[:, 0, :].rearrange("p (h n) -> p h n", h=2), psA[:, 1, :].rearrange("p (h n) -> p h n", h=2)]
    psDen = psA[:, 2, 0:4]
    psNU = psA[:, 2, 4:8]
    mstate = {"first": True}

    def prev_ebt(G, T):
        if T > 0:
            return G.ebt[:, T - 1, :], G.Rebt
        return Gp.ebt[:, 6, :], Gp.Rebt

    def mlstm_part1(G, T):
        cols = slice(T * 128, (T + 1) * 128)
        for h in range(4):
            P.op("pe", lambda e, h=h: e.matmul(psC[:, h * 128:(h + 1) * 128], lhsT=kT[:, h, cols], rhs=qT[:, h, cols],
                                               start=True, stop=True), reads=[RkT[h], RqT[h]], writes=[RpsC])
        P.op("dve", lambda e: e.tensor_tensor(out=Sm[:, :, :], in0=psC[:, 0:512].rearrange("p (h n) -> p h n", h=4),
                                              in1=triM.unsqueeze(1).broadcast_to([128, 4, 128]), op=ALU.mult),
             reads=[RpsC, Rcst], writes=[RSm])
        for h in range(4):
            pn = psNum[h // 2][:, h % 2, :]
            P.op("pe", lambda e, h=h, pn=pn: e.matmul(pn, lhsT=Sm[:, h, :], rhs=vw[:, T, h, :], start=True, stop=False),
                 reads=[RSm, Rvw[T]], writes=[RpsA[h // 2]])
            P.op("pe", lambda e, h=h, pn=pn: e.matmul(pn, lhsT=qT[:, h, cols], rhs=Cbf[:, h, :], start=False, stop=True),
                 reads=[RqT[h], RCbf[h]], writes=[RpsA[h // 2]])
            P.op("pe", lambda e, h=h: e.matmul(psDen[:, h:h + 1], lhsT=Sm[:, h, :], rhs=G.wpb[:, T, h:h + 1], start=True, stop=False),
                 reads=[RSm, G.Rwpb], writes=[RpsA[2]])
            P.op("pe", lambda e, h=h: e.matmul(psDen[:, h:h + 1], lhsT=qT[:, h, cols], rhs=nbf[:, h:h + 1], start=False, stop=True),
                 reads=[RqT[h], Rnbf], writes=[RpsA[2]])
        S = Rsm
        P.op("dve", lambda e: e.tensor_tensor(out=d1, in0=psDen, in1=G.ebs[:, T, :], op=ALU.mult), reads=[RpsA[2], G.Rebs], writes=[S["d1"]])
        P.op("dve", lambda e: e.scalar_tensor_tensor(out=d2, in0=d1, scalar=-1.0, in1=d1, op0=ALU.mult, op1=ALU.max),
             reads=[S["d1"]], writes=[S["d2"]])
        P.op("dve", lambda e: e.tensor_scalar(out=d3, in0=d2, scalar1=1.0, scalar2=None, op0=ALU.max), reads=[S["d2"]], writes=[S["d3"]])
        P.op("dve", lambda e: e.reciprocal(out=rec, in_=d3), reads=[S["d3"]], writes=[S["rec"]])
        P.op("dve", lambda e: e.tensor_tensor(out=cc, in0=rec, in1=G.ebs[:, T, :], op=ALU.mult), reads=[S["rec"], G.Rebs], writes=[S["cc"]])
        P.op("dve", lambda e: e.memset(ss, 0.0), writes=[S["ss"]])
        for h in range(4):
            pn = psNum[h // 2][:, h % 2, :]
            P.op("act", lambda e, h=h, pn=pn: e.activation(out=junk[:, 0:256], in_=pn, func=AF.Square, scale=cc[:, h:h + 1],
                                                           accum_out=ss[:, h:h + 1]),
                 reads=[RpsA[h // 2], S["ss"], S["cc"]], writes=[Rjunk, S["ss"]])
        P.op("act", lambda e: e.activation(out=lnr, in_=ss, func=AF.Ln, bias=EPS, scale=1.0 / 256.0), reads=[S["ss"]], writes=[S["lnr"]])
        P.op("act", lambda e: e.activation(out=rr, in_=lnr, func=AF.Exp, scale=-0.5), reads=[S["lnr"]], writes=[S["rr"]])
        P.op("dve", lambda e: e.tensor_tensor(out=rowsc, in0=cc, in1=rr, op=ALU.mult), reads=[S["cc"], S["rr"]], writes=[S["rowsc"]])
        for h in range(4):
            pn = psNum[h // 2][:, h % 2, :]
            P.op("dve", lambda e, h=h, pn=pn: e.scalar_tensor_tensor(
                out=ym[:, h * 256:(h + 1) * 256], in0=pn, scalar=rowsc[:, h:h + 1], in1=og[:, T, h * 256:(h + 1) * 256],
                op0=ALU.mult, op1=ALU.mult), reads=[RpsA[h // 2], S["rowsc"], Rog[T]], writes=[Rym])

    def mlstm_transposes(T):
        for j in range(8):
            P.op("pe", lambda e, j=j: e.transpose(psD[:, j * 128:(j + 1) * 128], ym[:, j * 128:(j + 1) * 128], ident_bf),
                 reads=[Rym, Rcbf], writes=[RpsD])
        if T == 0:
            i0, n, src0 = 0, 5, 123
        else:
            i0, n, src0 = 5 + 128 * (T - 1), 128, 0
        P.op("act", lambda e: e.activation(out=yT[:, 8:16, i0:i0 + n],
                                           in_=psD[:, :].rearrange("p (j n) -> p j n", j=8)[:, :, src0:src0 + n], func=AF.Copy),
             reads=[RpsD], writes=RyT[8:16])

    def mlstm_update(G, T, kb, Rkb, vwb, Rvwb, need_c):
        first = mstate["first"]
        for h in range(4):
            pu = psNum[h // 2][:, h % 2, :]
            P.op("pe", lambda e, h=h, pu=pu: e.matmul(pu, lhsT=kb[:, T, h * 128:(h + 1) * 128], rhs=vwb[:, T, h, :], start=True, stop=True),
                 reads=[Rkb[T], Rvwb[T]], writes=[RpsA[h // 2]])
            P.op("pe", lambda e, h=h: e.matmul(psNU[:, h:h + 1], lhsT=kb[:, T, h * 128:(h + 1) * 128], rhs=G.wpb[:, T, h:h + 1], start=True, stop=True),
                 reads=[Rkb[T], G.Rwpb], writes=[RpsA[2]])
        pe4, Rpe = prev_ebt(G, T)
        for h in range(4):
            pu = psNum[h // 2][:, h % 2, :]
            if first:
                P.op("dve", lambda e, h=h, pu=pu: e.tensor_copy(out=E32[:, h, :], in_=pu), reads=[RpsA[h // 2]], writes=[RE32[h]])
            else:
                P.op("dve", lambda e, h=h, pu=pu: e.scalar_tensor_tensor(out=E32[:, h, :], in0=E32[:, h, :], scalar=pe4[:, h:h + 1], in1=pu,
                                                                           op0=ALU.mult, op1=ALU.add),
                     reads=[RE32[h], RpsA[h // 2], Rpe], writes=[RE32[h]])
            if need_c:
                P.op("act", lambda e, h=h: e.activation(out=Cbf[:, h, :], in_=E32[:, h, :], func=AF.Identity, scale=G.ebt[:, T, h:h + 1]),
                     reads=[RE32[h], G.Rebt], writes=[RCbf[h]])
        if first:
            P.op("dve", lambda e: e.tensor_copy(out=En, in_=psNU), reads=[RpsA[2]], writes=[REn])
        else:
            P.op("dve", lambda e: e.tensor_tensor(out=En, in0=En, in1=pe4, op=ALU.mult), reads=[REn, Rpe], writes=[REn])
            P.op("dve", lambda e: e.tensor_tensor(out=En, in0=En, in1=psNU, op=ALU.add), reads=[REn, RpsA[2]], writes=[REn])
        if need_c:
            P.op("dve", lambda e: e.tensor_tensor(out=nbf, in0=En, in1=G.ebt[:, T, :], op=ALU.mult), reads=[REn, G.Rebt], writes=[Rnbf])
        mstate["first"] = False

    cg_slices = [(121 + 343 * s, 345) for s in range(3)]

    def conv_chunk(kind, b, j, slot, Rs):
        ps, Rb = feat_major_proj(slot, Rs, j, hT, RhT, 16, cg_slices)
        if kind == "gc":
            for si in range(3):
                P.op("act", lambda e, si=si: e.activation(out=gcs[j][:, si, :], in_=ps[:, si, 0:345], func=AF.Copy),
                     reads=[Rb[si]], writes=[Rgcs[j]])
        elif kind == "u":
            for si in range(3):
                P.op("dve", lambda e, si=si: e.tensor_tensor(out=gcs[j][:, si, :], in0=ps[:, si, 0:345], in1=gcs[j][:, si, :], op=ALU.mult),
                     reads=[Rb[si], Rgcs[j]], writes=[Rgcs[j]])
        else:
            fc = 2 * b + j
            z = gcs[j]
            P.op("act", lambda e: e.activation(out=cacc[:, :, :], in_=z[:, :, 0:343], func=AF.Identity,
                                               scale=cs(C_SCW + 3 * fc + 0)), reads=[Rgcs[j], Rcst], writes=[Rcacc])
            P.op("dve", lambda e: e.scalar_tensor_tensor(out=cacc[:, :, :], in0=z[:, :, 1:344], scalar=cs(C_SCW + 3 * fc + 1),
                                                         in1=cacc[:, :, :], op0=ALU.mult, op1=ALU.add),
                 reads=[Rgcs[j], Rcacc, Rcst], writes=[Rcacc])
            P.op("dve", lambda e: e.scalar_tensor_tensor(out=cacc[:, :, :], in0=z[:, :, 2:345], scalar=cs(C_SCW + 3 * fc + 2),
                                                         in1=cacc[:, :, :], op0=ALU.mult, op1=ALU.add),
                 reads=[Rgcs[j], Rcacc, Rcst], writes=[Rcacc])
            for si in range(3):
                P.op("dve", lambda e, si=si: e.tensor_tensor(out=yT[:, fc, 343 * si:343 * (si + 1)],
                                                             in0=ps[:, si, 2:345], in1=cacc[:, si, :], op=ALU.mult),
                     reads=[Rb[si], Rcacc], writes=[RyT[fc]])

    def conv_task(kind, b, split):
        st = {}
        col = {"gc": 4, "u": 8, "gb": 0}[kind] + b

        def consume(slot, Rs):
            st["slot"], st["Rs"] = slot, Rs
            conv_chunk(kind, b, 0, slot, Rs)
            if not split:
                conv_chunk(kind, b, 1, slot, Rs)
        task(blk(w_in_b, col), 16, 256, consume)
        if split:
            return lambda: conv_chunk(kind, b, 1, st["slot"], st["Rs"])
        return None

    def setup():
        P.op("sp", lambda e: e.dma_start(out=cst[:, :], in_=cst_d[:, :]), writes=[Rcst], dma=True)
        P.op("dve", lambda e: e.tensor_copy(out=cbf[:, 0:128], in_=cst[:, C_ONE:C_ONE + 128]), reads=[Rcst], writes=[Rcbf])
        P.op("dve", lambda e: e.tensor_copy(out=cbf[:, 128:256], in_=cst[:, C_IDN:C_IDN + 128]), reads=[Rcst], writes=[Rcbf])
    call(setup)
    for s in range(7):
        call(lambda s=s: prefix_slab(s))
    call(prefix_rs)

    def ext_norm_alias():
        for r in Rslab:
            P.alias(r, Rslab_p)
    call(ext_norm_alias)
    ext_slabs = [lambda s=s: norm_slab(896, s, 1152, C_G1, hT, RhT) for s in range(5)]
    if_task(Gp, hTp, RhTp, rs_col)
    for h in range(4):
        v_task(Gp, hTp, RhTp, vw_p, Rvw_p, h, rs_col)
        call(ext_slabs[h])
    k_task(Gp, hTp, RhTp, k_tm_p, Rk_tm_p, 0, False, rs_col)
    call(ext_slabs[4])
    k_task(Gp, hTp, RhTp, k_tm_p, Rk_tm_p, 1, False, rs_col)
    call(lambda: tap("hT", hT[:, 0, 0:1152], RhT))

    def phase_c_begin():
        for r in resAB:
            P.alias(r, Rslab)
        set_pools(banksB, [(psB, RpsB)])
    call(phase_c_begin)
    sweep = [lambda T=T: mlstm_update(Gp, T, k_tm_p, Rk_tm_p, vw_p, Rvw_p, T == 6) for T in range(7)]
    call(sweep[0])
    if_task(Gx, hT, RhT)
    for h in range(4):
        call(sweep[1 + h])
        v_task(Gx, hT, RhT, vw, Rvw, h)
    for j in range(2):
        call(sweep[5 + j])
        k_task(Gx, hT, RhT, k_tm, Rk_tm, j, True)

    def phase_d_begin():
        for r in resC:
            P.alias(r, resCp)
        set_pools(banksA + banksB, [(psA, RpsA), (psB, RpsB)])
    call(phase_d_begin)
    for j in range(2):
        q_task(j)
    for j in range(4):
        o_task(j)

    def phase_e_begin():
        for r in conv_res:
            P.alias(r, [Rsqd])
        for r in RyT:
            P.alias(r, RhTp)
        set_pools(banksB, [(psB, RpsB)])
    call(phase_e_begin)
    conv_list = []
    for b in range(4):
        conv_list += [("gc", b), ("u", b), ("gb", b)]
    ci = 0
    for T in range(9):
        call(lambda T=T: mlstm_part1(Gx, T))
        second = conv_task(conv_list[ci][0], conv_list[ci][1], True); ci += 1
        if T < 8:
            call(lambda T=T: mlstm_update(Gx, T, k_tm, Rk_tm, vw, Rvw, True))
        call(lambda T=T: mlstm_transposes(T))
        call(second)
        if T < 3:
            conv_task(conv_list[ci][0], conv_list[ci][1], False); ci += 1
    assert ci == 12
    call(lambda: tap("yT", yTt[:, 0:16464], RyT))

    def xreload():
        set_pools(banksA + banksB, [(psA, RpsA), (psB, RpsB)])
        for k in range(16):
            P.alias(Rxres[k], resAB + resC + Rslab)
        for k4 in range(4):
            P.op("sp", lambda e, k4=k4: e.dma_start(out=xres[:, 4 * k4:4 * k4 + 4, :],
                                                    in_=xr[:, 4116 * k4:4116 * (k4 + 1)].rearrange("p (k n) -> p k n", k=4)),
                 writes=Rxres[4 * k4:4 * k4 + 4], dma=True)
        for r in (Racc, Rtmpq, Rrs2):
            P.alias(r, conv_res + mlstm_small_res)
    call(xreload)
    wo_slices = [(343 * s, 343) for s in range(3)]
    n2_slices = [(0, 345), (345, 342), (687, 342)]
    RhT2 = [[Res("hT2_%d_%d" % (k, si)) for si in range(3)] for k in range(16)]
    RhT3 = [[Res("hT3_%d_%d" % (k, si)) for si in range(2)] for k in range(16)]
    Rrs2_3 = [Res("rs2_%d" % si) for si in range(3)]
    ff_src = lambda k, si: [RhT2[k][si]] + ([RhT2[k][si - 1]] if si > 0 else [])

    def wout_task(b):
        def consume(slot, Rs):
            for j in range(2):
                f = 2 * b + j
                ps, Rb = feat_major_proj(slot, Rs, j, yT, RyT, 16, wo_slices, k_outer=True)
                P.op("dve", lambda e, ps=ps, f=f: e.tensor_tensor(out=xres[:, f, :].rearrange("p (s n) -> p s n", s=3),
                                                                  in0=ps[:, :, 0:343], in1=xres[:, f, :].rearrange("p (s n) -> p s n", s=3),
                                                                  op=ALU.add), reads=Rb + [Rxres[f]], writes=[Rxres[f]])
                sumsq_hook(f, 0, 1029, f == 0)
        task(blk(w_out_b, b), 16, 256, consume)
    for b in range(8):
        wout_task(b)

    def norm2():
        tap("x1", arena[:, 0:16464], Rxres)
        for k in range(16):
            for si in range(3):
                P.alias(RhT2[k][si], [RhT[k]])
        for si in range(3):
            P.alias(Rrs2_3[si], [Rrs2])
        norm_finish_sliced(0, n2_slices, C_G2, hT, RhT2, Rrs2_3)
        for r in RaT:
            P.alias(r, RyT)
        for r in Rupg + Rupu:
            P.alias(r, conv_res + mlstm_small_res)
    call(norm2)

    ff_slices = [(1 + 342 * s, 344) for s in range(3)]
    groups = [(0, 16), (16, 16), (32, 12)]

    def conv_evac(ps, Rb, dst, Rdst, cidx):
        w0, w1, w2 = (cs(C_FCW + 3 * cidx + i) for i in range(3))
        P.op("act", lambda e: e.activation(out=dst[:, :, :], in_=ps[:, :, 2:344], func=AF.Identity, bias=cs(C_FCB + cidx), scale=w2),
             reads=Rb + [Rcst], writes=[Rdst])
        P.op("dve", lambda e: e.scalar_tensor_tensor(out=dst[:, :, :], in0=ps[:, :, 1:343], scalar=w1, in1=dst[:, :, :], op0=ALU.mult, op1=ALU.add),
             reads=Rb + [Rdst, Rcst], writes=[Rdst])
        P.op("dve", lambda e: e.scalar_tensor_tensor(out=dst[:, :, :], in0=ps[:, :, 0:342], scalar=w0, in1=dst[:, :, :], op0=ALU.mult, op1=ALU.add),
             reads=Rb + [Rdst, Rcst], writes=[Rdst])

    def ffg_task(jj):
        def consume(slot, Rs):
            for j in range(2):
                c = 2 * jj + j
                ps, Rb = feat_major_proj(slot, Rs, j, hT, RhT, 16, ff_slices, ff_src)
                conv_evac(ps, Rb, upg[j], Rupg[j], c)
                P.op("act", lambda e, j=j: e.activation(out=upg[j][:, :, :], in_=upg[j][:, :, :], func=AF.Silu),
                     reads=[Rupg[j]], writes=[Rupg[j]])
        task(blk(w_up_b, jj), 16, 256, consume)

    def ffu_task(jj, g0):
        def consume(slot, Rs):
            for j in range(2):
                c = 2 * jj + j
                ps, Rb = feat_major_proj(slot, Rs, j, hT, RhT, 16, ff_slices, ff_src)
                conv_evac(ps, Rb, upu[j], Rupu[j], 44 + c)
                P.op("dve", lambda e, j=j, c=c: e.tensor_tensor(out=aT[:, c - g0, :].rearrange("p (s n) -> p s n", s=3),
                                                                 in0=upg[j][:, :, :], in1=upu[j][:, :, :], op=ALU.mult),
                     reads=[Rupg[j], Rupu[j]], writes=[RaT[c - g0]])
        task(blk(w_up_b, 22 + jj), 16, 256, consume)

    wd_slices = [(2, 512), (514, 512)]

    Rtl = Res("tbl")

    def table_preload():
        P.op("act", lambda e: e.activation(out=regE[:, 4110:4111], in_=cst[:, C_ONE:C_ONE + 1], func=AF.Ln), reads=[Rcst], writes=[Rtl])

    def wdown_task(g0, gn, cb, last):
        def consume(slot, Rs):
            if last and cb == 0:
                table_preload()
            for j in range(2):
                f = 2 * cb + j
                ps, Rb = feat_major_proj(slot, Rs, j, aT, RaT[0:gn], gn, wd_slices, k_outer=True)
                P.op("dve", lambda e, ps=ps, f=f: e.tensor_tensor(out=xres[:, f, 5:1029].rearrange("p (s n) -> p s n", s=2),
                                                                  in0=ps[:, 0:2, 0:512], in1=xres[:, f, 5:1029].rearrange("p (s n) -> p s n", s=2),
                                                                  op=ALU.add), reads=Rb[0:2] + [Rxres[f]], writes=[Rxres[f]])
                if last:
                    sumsq_hook(f, 5, 1024, f == 0)
        task(blk(w_down_b, cb, 44)[:, g0:g0 + gn, :], gn, 256, consume)

    for gi, (g0, gn) in enumerate(groups):
        for jj in range(g0 // 2, (g0 + gn) // 2):
            ffg_task(jj)
            ffu_task(jj, g0)
        for cb in range(8):
            wdown_task(g0, gn, cb, gi == len(groups) - 1)

    own_slices = [(0, 512), (512, 512)]

    def norm3():
        tap("x2", arena[:, 0:16464], Rxres)
        for k in range(16):
            for si in range(2):
                P.alias(RhT3[k][si], RhT2[k])
        for si in range(2):
            P.alias(Rrs2_3[si], Rrs2_3)
        norm_finish_sliced(5, own_slices, C_G3, hT, RhT3, Rrs2_3)
        P.alias(Rppw, RaT)
        P.alias(RpTb, RaT)
        P.op("pool", lambda e: e.dma_start(out=ppw, in_=w_ppv[:, :, :]), writes=[Rppw], dma=True)
        P.op("pool", lambda e: e.dma_start(out=pTb, in_=pTv[:, :, :]), writes=[RpTb], dma=True)
        for r in Rsgt + Rppt:
            P.alias(r, Rupg + Rupu)
    call(norm3)
    pairs = [((psA[:, 0, :], RpsA[0]), (psA[:, 1, :], RpsA[1])),
             ((psA[:, 2, :], RpsA[2]), (psB[:, 0, :], RpsB[0])),
             ((psB[:, 1, :], RpsB[1]), (psB[:, 2, :], RpsB[2]))]
    pr = {"i": 0}
    Rx2 = [[Res("x2_%d_%d" % (f, sl_)) for sl_ in range(2)] for f in range(16)]
    Racc2 = [Res("acc2_0"), Res("acc2_1")]
    Rrs2s = [Res("rs2s0"), Res("rs2s1")]

    def ple_alias():
        for f in range(16):
            for sl_ in range(2):
                P.alias(Rx2[f][sl_], [Rxres[f]])
        for sl_ in range(2):
            P.alias(Racc2[sl_], [Racc])
            P.alias(Rrs2s[sl_], [Rrs2] + Rrs2_3)
    call(ple_alias)

    def pg_task(b, s):
        def consume(slot, Rs):
            for j in range(2):
                f = 2 * b + j
                (pg_, Rg), (pp_, Rp) = pairs[pr["i"] % 3]
                bi = pr["i"] % 2
                pr["i"] += 1
                xs = xres[:, f, 5 + 512 * s:5 + 512 * (s + 1)]
                for k in range(16):
                    P.op("pe", lambda e, k=k, pg_=pg_, j=j: e.matmul(pg_[:, 0:512], lhsT=slot[:, k, j * 128:(j + 1) * 128],
                                                                     rhs=hT[:, k, 512 * s:512 * (s + 1)], start=(k == 0), stop=(k == 15)),
                         reads=[Rs, RhT3[k][s]], writes=[Rg])
                for k in range(2):
                    P.op("pe", lambda e, k=k, pp_=pp_, f=f: e.matmul(pp_[:, 0:512], lhsT=ppw[:, k, f * 128:(f + 1) * 128],
                                                                     rhs=pTb[:, k, 512 * s:512 * (s + 1)], start=(k == 0), stop=(k == 1)),
                         reads=[Rppw, RpTb], writes=[Rp])
                P.op("act", lambda e, pg_=pg_, bi=bi: e.activation(out=sgt[bi], in_=pg_[:, 0:512], func=AF.Sigmoid), reads=[Rg], writes=[Rsgt[bi]])
                P.op("dve", lambda e, pp_=pp_, bi=bi: e.tensor_tensor(out=ppt[bi], in0=pp_[:, 0:512], in1=sgt[bi], op=ALU.mult),
                     reads=[Rp, Rsgt[bi]], writes=[Rppt[bi]])
                P.op("dve", lambda e, bi=bi, xs=xs: e.tensor_tensor(out=xs, in0=xs, in1=ppt[bi], op=ALU.add),
                     reads=[Rppt[bi], Rx2[f][s]], writes=[Rx2[f][s]])
                a_ = acc[:, 512 * s:512 * (s + 1)]
                if f == 0:
                    P.op("act", lambda e, xs=xs, a_=a_: e.activation(out=a_, in_=xs, func=AF.Square), reads=[Rx2[f][s]], writes=[Racc2[s]])
                else:
                    P.op("act", lambda e, xs=xs: e.activation(out=tmpq[:, 0:512], in_=xs, func=AF.Square), reads=[Rx2[f][s]], writes=[Rtmpq])
                    P.op("dve", lambda e, a_=a_: e.tensor_tensor(out=a_, in0=a_, in1=tmpq[:, 0:512], op=ALU.add),
                         reads=[Racc2[s], Rtmpq], writes=[Racc2[s]])
        task(blk(w_pg_b, b), 16, 256, consume)

    def final_rs(s):
        P.op("pe", lambda e: e.matmul(psC[:, 0:512], lhsT=onesF, rhs=acc[:, 512 * s:512 * (s + 1)], start=True, stop=True),
             reads=[Racc2[s], Rcst], writes=[RpsC])
        rstd_from_psum(psC[:, 0:512], rs2[:, 512 * s:512 * (s + 1)], RpsC, Rrs2s[s], 2048.0)

    def final_ops(s, ks):
        for k in ks:
            xs = xres[:, k, 5 + 512 * s:5 + 512 * (s + 1)]
            P.op("dve", lambda e, k=k, xs=xs: e.scalar_tensor_tensor(out=xs, in0=xs, scalar=cs(C_GF + k), in1=rs2[:, 512 * s:512 * (s + 1)],
                                                                      op0=ALU.mult, op1=ALU.mult),
                 reads=[Rx2[k][s], Rrs2s[s], Rcst], writes=[Rx2[k][s]])
            P.op("sp", lambda e, k=k, xs=xs: e.dma_start(out=outTv[:, k, 512 * s:512 * (s + 1)], in_=xs),
                 reads=[Rx2[k][s]], writes=[Res("o")], dma=True)

    for b in range(8):
        pg_task(b, 0)
    call(lambda: final_rs(0))
    for b in range(8):
        pg_task(b, 1)
        call(lambda b=b: final_ops(0, [2 * b, 2 * b + 1]))
    call(table_preload)
    call(lambda: final_rs(1))
    call(lambda: final_ops(1, range(16)))

    def finish():
        Rfin = Res("fin")
        for o in P.ops["sp"]:
            if o.is_dma:
                Rfin.rdma.append(o)
        P.op("sp", lambda e: e.nop(), writes=[Rfin])
    call(finish)

    run_tasks()
    P.emit()
    return nc


def _consts(norm_mix_g, norm_ffn_g, norm_ple_g, final_norm_g, short_conv_w, ffn_conv_w, ffn_conv_b,
            b_igate, b_fgate, mh_norm_g):
    c = np.zeros((128, NCST), np.float32)
    pc = lambda v: np.ascontiguousarray(np.asarray(v, np.float32).reshape(-1, 128).T)
    c[:, C_G1:C_G1 + 16] = pc(norm_mix_g)
    c[:, C_G2:C_G2 + 16] = pc(norm_ffn_g)
    c[:, C_G3:C_G3 + 16] = pc(norm_ple_g)
    c[:, C_GF:C_GF + 16] = pc(final_norm_g)
    scw = np.asarray(short_conv_w, np.float32).reshape(3, 8, 128)
    c[:, C_SCW:C_SCW + 24] = scw.transpose(2, 1, 0).reshape(128, 24)
    fcw = np.asarray(ffn_conv_w, np.float32).reshape(3, 88, 128)
    c[:, C_FCW:C_FCW + 264] = fcw.transpose(2, 1, 0).reshape(128, 264)
    c[:, C_FCB:C_FCB + 88] = pc(ffn_conv_b)
    c[:, C_BIF:C_BIF + 4] = np.asarray(b_igate, np.float32).reshape(1, 4)
    c[:, C_BIF + 4:C_BIF + 8] = np.asarray(b_fgate, np.float32).reshape(1, 4)
    c[:, C_GMH:C_GMH + 1024] = np.asarray(mh_norm_g, np.float32).reshape(1, 1024)
    c[:, C_TRI:C_TRI + 128] = np.triu(np.ones((128, 128), np.float32))
    c[:, C_ONE:C_ONE + 128] = 1.0
    c[:, C_IDN:C_IDN + 128] = np.eye(128, dtype=np.float32)
    return c


def make_in_maps(x, p, norm_mix_g, w_in, b_igate, b_fgate, short_conv_w, mh_norm_g, w_out,
                 norm_ffn_g, w_up, ffn_conv_w, ffn_conv_b, w_down, norm_ple_g, w_pg, w_pp, final_norm_g):
    x = np.asarray(x, np.float32)
    p = np.asarray(p, np.float32)
    cst = _consts(norm_mix_g[0], norm_ffn_g[0], norm_ple_g[0], final_norm_g, short_conv_w[0], ffn_conv_w[0],
                  ffn_conv_b[0], b_igate[0], b_fgate[0], mh_norm_g[0])
    def blocked(w, nb):
        w = np.asarray(w, np.float32)
        kc = w.shape[0] // 128
        return np.ascontiguousarray(w.reshape(kc, 128, nb, 256).transpose(2, 1, 0, 3)).reshape(nb, 128, kc * 256)
    w_in0 = np.asarray(w_in[0], np.float32)
    shared = {
        "w_in_b": blocked(w_in0[:, 0:6144], 24),
        "w_if": np.ascontiguousarray(w_in0[:, 6144:6152].reshape(16, 128, 8).transpose(1, 0, 2)).reshape(128, 128),
        "w_out_b": blocked(w_out[0], 8),
        "w_up_b": blocked(w_up[0], 44),
        "w_down_b": blocked(w_down[0], 8),
        "w_pg_b": blocked(w_pg[0], 8),
        "w_pp": np.ascontiguousarray(np.asarray(w_pp[0], np.float32)),
        "cst": cst,
    }
    maps = []
    for c in range(8):
        b, half = c // 2, c % 2
        xT = np.zeros((2048, 2048), np.float32)
        if half == 1:
            xT[:, 0:1024] = x[b, 0:1024].T
        xT[:, 1024:2048] = x[b, half * 1024:(half + 1) * 1024].T
        pT = np.ascontiguousarray(p[0, b, half * 1024:(half + 1) * 1024].T)
        m = dict(shared)
        x4 = xT.reshape(16, 128, 2048)
        parts = []
        for (c0_, n_, W_) in ((0, 896, 128), (896, 1152, 256)):
            for s_ in range((n_ + W_ - 1) // W_):
                w_ = min(W_, n_ - W_ * s_)
                a0 = c0_ + W_ * s_
                parts.append(x4[:, :, a0:a0 + w_].transpose(1, 0, 2).reshape(128, 16 * w_))
        m["xsl"] = np.ascontiguousarray(np.concatenate(parts, axis=1))
        m["xr"] = np.ascontiguousarray(x4[:, :, 1019:2048].transpose(1, 0, 2)).reshape(128, 16 * 1029)
        m["pT"] = pT
        maps.append(m)
    return maps


def kernel(x, p, norm_mix_g, w_in, b_igate, b_fgate, short_conv_w, mh_norm_g, w_out,
           norm_ffn_g, w_up, ffn_conv_w, ffn_conv_b, w_down, norm_ple_g, w_ple_gate,
           w_ple_proj, final_norm_g):
    maps = make_in_maps(x, p, norm_mix_g, w_in, b_igate, b_fgate, short_conv_w, mh_norm_g, w_out,
                        norm_ffn_g, w_up, ffn_conv_w, ffn_conv_b, w_down, norm_ple_g, w_ple_gate,
                        w_ple_proj, final_norm_g)
    nc = build_nc()
    res = run_bass_kernel_spmd(nc, maps, core_ids=list(range(8)))
    out = np.empty((4, 2048, 2048), np.float32)
    for c in range(8):
        b, half = c // 2, c % 2
        out[b, half * 1024:(half + 1) * 1024, :] = res.results[c]["outT"].T
    return out
```

```python
import contextlib
import numpy as np
import concourse.bass as bass
import concourse.mybir as mybir
from concourse.bass_utils import run_bass_kernel_spmd

F32 = mybir.dt.float32
BF16 = mybir.dt.bfloat16
AF = mybir.ActivationFunctionType
ALU = mybir.AluOpType

EPS = 1e-6
NS = 4
N_DMA_SEMS = 8

C_G1, C_G2, C_G3, C_GF = 0, 16, 32, 48
C_SCW = 64
C_FCW = 88
C_FCB = 352
C_BIF = 440
C_GMH = 448
C_TRI = 1472
C_ONE = 1600
C_IDN = 1728
NCST = 1856


class Res:
    __slots__ = ("name", "writer", "readers", "rdma")

    def __init__(self, name):
        self.name = name
        self.writer = None
        self.readers = {}
        self.rdma = []


class Op:
    __slots__ = ("eng", "fn", "deps", "signal", "sem", "val", "is_dma")

    def __init__(self, eng, fn, is_dma):
        self.eng = eng
        self.fn = fn
        self.deps = []
        self.signal = False
        self.sem = None
        self.val = 0
        self.is_dma = is_dma


ENGS = ("pe", "act", "dve", "pool", "sp")


class Prog:
    def __init__(self, nc):
        self.nc = nc
        self.ops = {e: [] for e in ENGS}

    def op(self, eng, fn, reads=(), writes=(), dma=False):
        o = Op(eng, fn, dma)
        deps = []
        for r in reads:
            if r.writer is not None:
                deps.append(r.writer)
        for w in writes:
            if w.writer is not None:
                deps.append(w.writer)
            deps.extend(w.readers.values())
            deps.extend(w.rdma)
        seen = set()
        for d in deps:
            if id(d) in seen or d is o:
                continue
            seen.add(id(d))
            if d.eng == "pe" and eng == "pe" and not d.is_dma:
                continue
            d.signal = True
            o.deps.append(d)
        for r in reads:
            if dma:
                r.rdma.append(o)
            else:
                r.readers[eng] = o
        for w in writes:
            w.writer = o
            w.readers = {}
            w.rdma = []
        self.ops[eng].append(o)
        return o

    def alias(self, new, olds):
        for r in olds:
            for d in ([r.writer] if r.writer is not None else []) + list(r.readers.values()) + r.rdma:
                new.rdma.append(d)

    def emit(self):
        nc = self.nc
        with contextlib.ExitStack() as st:
            esem = {e: st.enter_context(nc.semaphore("s_" + e)) for e in ENGS}
            dsem = {e: [st.enter_context(nc.semaphore("d_%s%d" % (e, i))) for i in range(N_DMA_SEMS)]
                    for e in ("sp", "act", "pool")}
            for e in ENGS:
                cnt = 0
                nd = 0
                for o in self.ops[e]:
                    if o.is_dma:
                        o.sem = dsem[e][nd % N_DMA_SEMS]
                        o.val = 16 * (nd // N_DMA_SEMS + 1)
                        nd += 1
                    elif o.signal:
                        cnt += 1
                        o.sem = esem[e]
                        o.val = cnt
            block = st.enter_context(nc.Block())

            def run(e, eng):
                seen = {}
                nd = 0
                for o in self.ops[e]:
                    waits = {}
                    for d in o.deps:
                        k = id(d.sem)
                        if k not in waits or waits[k][1] < d.val:
                            waits[k] = (d.sem, d.val)
                    if o.is_dma:
                        if nd >= N_DMA_SEMS:
                            k = id(o.sem)
                            v = o.val - 16
                            if k not in waits or waits[k][1] < v:
                                waits[k] = (o.sem, v)
                        nd += 1
                    for k, (s, v) in waits.items():
                        if seen.get(k, 0) >= v:
                            continue
                        seen[k] = v
                        eng.wait_ge(s, v)
                    ins = o.fn(eng)
                    if o.is_dma:
                        ins.then_inc(o.sem, 16)
                    elif o.signal:
                        ins.then_inc(o.sem, 1)

            @block.tensor
            def _(eng):
                run("pe", eng)

            @block.scalar
            def _(eng):
                run("act", eng)

            @block.vector
            def _(eng):
                run("dve", eng)

            @block.gpsimd
            def _(eng):
                run("pool", eng)

            @block.sync
            def _(eng):
                run("sp", eng)


def build_nc(debug=()):
    nc = bass.Bass("TRN2", target_bir_lowering=False)
    dt_in = lambda n, s: nc.dram_tensor(n, s, F32, kind="ExternalInput").ap()
    xsl = dt_in("xsl", [128, 16 * 2048])
    xr = dt_in("xr", [128, 16 * 1029])
    pT = dt_in("pT", [256, 1024])
    w_in_b = dt_in("w_in_b", [24, 128, 16 * 256])
    w_if = dt_in("w_if", [128, 16 * 8])
    w_out_b = dt_in("w_out_b", [8, 128, 16 * 256])
    w_up_b = dt_in("w_up_b", [44, 128, 16 * 256])
    w_down_b = dt_in("w_down_b", [8, 128, 44 * 256])
    w_pg_b = dt_in("w_pg_b", [8, 128, 16 * 256])
    w_pp = dt_in("w_pp", [256, 2048])
    cst_d = dt_in("cst", [128, NCST])
    outT = nc.dram_tensor("outT", [2048, 1024], F32, kind="ExternalOutput").ap()
    dbg_out = {}
    for name, shape in debug:
        dbg_out[name] = nc.dram_tensor("dbg_" + name, list(shape), F32, kind="ExternalOutput").ap()

    kview = lambda w: w.rearrange("(k p) n -> p k n", p=128)
    w_ppv, pTv, outTv = map(kview, (w_pp, pT, outT))
    blk = lambda wb, cb, kc=16: wb[cb].rearrange("p (k n) -> p k n", k=kc)
    slab_off = {}
    _o = 0
    for (c0_, n_, W_) in ((0, 896, 128), (896, 1152, 256)):
        for s_ in range((n_ + W_ - 1) // W_):
            w_ = min(W_, n_ - W_ * s_)
            slab_off[(c0_, s_)] = (_o, w_)
            _o += 16 * w_

    P = Prog(nc)
    sb = nc.alloc_sbuf_tensor

    cst = sb("cst_sb", [128, NCST], F32)
    cbf = sb("cbf", [128, 256], BF16)
    ones_bf = cbf[:, 0:128]
    ident_bf = cbf[:, 128:256]
    wsl = sb("wsl", [128, NS, 4096], BF16)
    hT = sb("hT", [128, 16, 1152], BF16)
    arena = sb("arena", [128, 16464], F32)
    yTt = sb("yT", [128, 16464], BF16)
    NE = 7296
    regE = sb("regE", [128, NE], F32)

    psA = nc.alloc_psum_tensor("psA", [128, 3, 512], F32)
    psB = nc.alloc_psum_tensor("psB", [128, 3, 512], F32)
    psC = nc.alloc_psum_tensor("psC", [128, 512], F32)
    psD = nc.alloc_psum_tensor("psD", [128, 1024], BF16)
    RpsA = [Res("psA%d" % i) for i in range(3)]
    RpsB = [Res("psB%d" % i) for i in range(3)]
    RpsC = Res("psC")
    RpsD = Res("psD")
    RpsDh = [Res("psDh0"), Res("psDh1")]
    banksA = [(psA[:, i, :], RpsA[i]) for i in range(3)]
    banksB = [(psB[:, i, :], RpsB[i]) for i in range(3)]
    pools = {"tm": banksA + banksB, "fm": [(psA, RpsA), (psB, RpsB)], "tmi": 0, "fmi": 0}

    def set_pools(tm, fm):
        pools["tm"] = tm
        pools["fm"] = fm

    Rcst = Res("cst")
    Rcbf = Res("cbf")
    RhT = [Res("hT%d" % k) for k in range(16)]

    ar_bf = arena[:, :].bitcast(BF16)
    slab = [arena[:, 0:4096].rearrange("p (k n) -> p k n", k=16),
            arena[:, 4096:8192].rearrange("p (k n) -> p k n", k=16)]
    Rslab = [Res("slab0"), Res("slab1")]
    slab_p = [arena[:, 2048 * i:2048 * (i + 1)].rearrange("p (k n) -> p k n", k=16) for i in range(4)]
    Rslab_p = [Res("slabp%d" % i) for i in range(4)]
    k_tm = ar_bf[:, 0:4608].rearrange("p (t n) -> p t n", t=9)
    vw = ar_bf[:, 4608:13824].rearrange("p (t h n) -> p t h n", t=9, h=4)
    kT = ar_bf[:, 13824:18432].rearrange("p (h n) -> p h n", h=4)
    qT = ar_bf[:, 18432:23040].rearrange("p (h n) -> p h n", h=4)
    og = ar_bf[:, 23040:32256].rearrange("p (t n) -> p t n", t=9)
    k_tm_p = ar_bf[:, 18432:22016].rearrange("p (t n) -> p t n", t=7)
    vw_p = ar_bf[:, 22016:29184].rearrange("p (t h n) -> p t h n", t=7, h=4)
    RqT = [Res("qT%d" % h) for h in range(4)]
    RkT = [Res("kT%d" % h) for h in range(4)]
    Rk_tm = [Res("k_tm%d" % t) for t in range(9)]
    Rvw = [Res("vw%d" % t) for t in range(9)]
    Rog = [Res("og%d" % t) for t in range(9)]
    Rk_tm_p = [Res("k_tm_p%d" % t) for t in range(7)]
    Rvw_p = [Res("vw_p%d" % t) for t in range(7)]
    resAB = Rk_tm + Rvw + RkT
    resC = RqT + Rog
    resCp = Rk_tm_p + Rvw_p
    xres = arena[:, 0:16464].rearrange("p (k n) -> p k n", k=16)
    Rxres = [Res("xres%d" % k) for k in range(16)]

    yT = yTt[:, :].rearrange("p (k n) -> p k n", k=16)
    RyT = [Res("yT%d" % k) for k in range(16)]
    hTp = yTt[:, 0:16 * 896].rearrange("p (k n) -> p k n", k=16)
    RhTp = [Res("hTp%d" % k) for k in range(16)]
    aT = yTt[:, 0:16 * 1026].rearrange("p (k n) -> p k n", k=16)
    RaT = [Res("aT%d" % k) for k in range(16)]
    ppw = yTt[:, 0:4096].rearrange("p (k n) -> p k n", k=2)
    Rppw = Res("ppw")
    pTb = yTt[:, 4096:6144].rearrange("p (k n) -> p k n", k=2)
    RpTb = Res("pTb")

    e_bf = regE[:, :].bitcast(BF16)
    E32 = regE[:, 0:1024].rearrange("p (h n) -> p h n", h=4)
    Cbf = e_bf[:, 2048:3072].rearrange("p (h n) -> p h n", h=4)
    Sm = e_bf[:, 3072:3584].rearrange("p (h n) -> p h n", h=4)
    ym = e_bf[:, 3584:4608]
    junk = regE[:, 2304:2560]
    so = [2560]

    def small(n):
        v = regE[:, so[0]:so[0] + n]
        so[0] += n
        return v

    class GateSet:
        def __init__(self, T, tag):
            self.T = T
            r3 = lambda v: v.rearrange("p (t n) -> p t n", t=T)
            self.gates = r3(small(8 * T))
            self.etmp = r3(small(4 * T))
            self.nlf = r3(small(4 * T))
            self.arg = r3(small(4 * T))
            self.wp = r3(small(4 * T))
            self.ebt = r3(small(4 * T))
            self.ebs = r3(small(4 * T))
            o = so[0]
            self.wpb = e_bf[:, 2 * o:2 * o + 4 * T].rearrange("p (t n) -> p t n", t=T)
            so[0] += 2 * T
            mk = lambda n: Res(n + tag)
            self.Rgates, self.Retmp, self.Rnlf, self.Rarg = mk("gates"), mk("etmp"), mk("nlf"), mk("arg")
            self.Rwp, self.Rwpb, self.Rebt, self.Rebs = mk("wp"), mk("wpb"), mk("ebt"), mk("ebs")

        def all_res(self):
            return [self.Rgates, self.Retmp, self.Rnlf, self.Rarg, self.Rwp, self.Rwpb, self.Rebt, self.Rebs]

    Gp = GateSet(7, "_p")
    Gx = GateSet(9, "_x")
    En = small(4)
    d1 = small(4); d2 = small(4); d3 = small(4); rec = small(4); cc = small(4)
    ss = small(4); t1 = small(4); t2 = small(4); lnr = small(4); rr = small(4); rowsc = small(4)
    nbf = e_bf[:, 2 * so[0]:2 * so[0] + 4]; so[0] += 2
    rs_col = small(8)
    Rrs_col = Res("rs_col")
    rs = small(512)
    assert so[0] <= 4100
    RE32 = [Res("E32_%d" % h) for h in range(4)]
    RCbf = [Res("Cbf_%d" % h) for h in range(4)]
    RSm = Res("Sm"); Rym = Res("ym"); Rjunk = Res("junk")
    REn = Res("En"); Rnbf = Res("nbf")
    Rsm = {n: Res(n) for n in "d1 d2 d3 rec cc ss t1 t2 lnr rr rowsc".split()}
    Rrs = Res("rs")
    mlstm_small_res = RE32 + RCbf + [RSm, Rym, Rjunk, REn, Rnbf, Rrs] + list(Rsm.values()) + Gp.all_res() + Gx.all_res()
    gcs = [regE[:, 4100:5135].rearrange("p (s n) -> p s n", s=3), regE[:, 5135:6170].rearrange("p (s n) -> p s n", s=3)]
    cacc = regE[:, 6170:7199].rearrange("p (s n) -> p s n", s=3)
    Rgcs = [Res("gcs0"), Res("gcs1")]
    Rcacc = Res("cacc")
    conv_res = Rgcs + [Rcacc]
    sqd = e_bf[:, 8200:12296].rearrange("p (k n) -> p k n", k=16)
    Rsqd = Res("sqd")
    upg = [regE[:, 0:1026].rearrange("p (s n) -> p s n", s=3), regE[:, 1026:2052].rearrange("p (s n) -> p s n", s=3)]
    upu = [regE[:, 2052:3078].rearrange("p (s n) -> p s n", s=3), regE[:, 3078:4104].rearrange("p (s n) -> p s n", s=3)]
    Rupg = [Res("upg0"), Res("upg1")]
    Rupu = [Res("upu0"), Res("upu1")]
    rs2 = regE[:, 4140:5169]
    acc = regE[:, 5169:6198]
    tmpq = regE[:, 6198:7227]
    Rrs2 = Res("rs2"); Racc = Res("acc"); Rtmpq = Res("tmpq")
    sgt = [regE[:, 0:512], regE[:, 512:1024]]
    ppt = [regE[:, 1024:1536], regE[:, 1536:2048]]
    Rsgt = [Res("sgt0"), Res("sgt1")]
    Rppt = [Res("ppt0"), Res("ppt1")]

    cs = lambda off, n=1: cst[:, off:off + n]
    triM = cst[:, C_TRI:C_TRI + 128]
    onesF = cst[:, C_ONE:C_ONE + 128]

    Rslot = [Res("slot%d" % i) for i in range(NS)]
    wq = []
    wstate = {"issued": 0}

    def wget(i):
        while wstate["issued"] < min(i + NS, len(wq)):
            j = wstate["issued"]
            src, kc, ncol = wq[j]
            dst = wsl[:, j % NS, 0:kc * ncol].rearrange("p (k n) -> p k n", k=kc)
            o = P.op("pool", lambda e, dst=dst, src=src: e.dma_start(out=dst, in_=src),
                     writes=[Rslot[j % NS]], dma=True)
            if j < NS and wstate.get("gate") is not None:
                g = wstate["gate"]
                g.signal = True
                o.deps.append(g)
            wstate["issued"] += 1
        src, kc, ncol = wq[i]
        return wsl[:, i % NS, 0:kc * ncol].rearrange("p (k n) -> p k n", k=kc), Rslot[i % NS]

    tasks = []

    def task(src, kc, ncol, fn):
        tasks.append(("w", (src, kc, ncol), fn))

    def call(fn):
        tasks.append(("c", None, fn))

    def run_tasks():
        for t in tasks:
            if t[0] == "w":
                wq.append(t[1])
        n = 0
        for t in tasks:
            if t[0] == "w":
                slot, rs_ = wget(n)
                n += 1
                t[2](slot, rs_)
            else:
                t[2]()

    def tap(name, ap, reads):
        if name in dbg_out:
            P.op("pool", lambda e: e.dma_start(out=dbg_out[name], in_=ap), reads=reads, writes=[Res("dbg")], dma=True)

    def rstd_from_psum(ps_ap, out_rs, Rps, Rout, div):
        P.op("act", lambda e: e.activation(out=out_rs, in_=ps_ap, func=AF.Ln, bias=EPS, scale=1.0 / div),
             reads=[Rps], writes=[Rout])
        P.op("act", lambda e: e.activation(out=out_rs, in_=out_rs, func=AF.Exp, scale=-0.5),
             reads=[Rout], writes=[Rout])

    def norm_slab(c0, s, n, goff, dst, Rdst, W=256):
        w = min(W, n - s * W)
        if W == 256:
            sl = slab[s % 2][:, :, 0:w]
            Rs = Rslab[s % 2]
        else:
            sl = slab_p[s % 4][:, :, 0:w]
            Rs = Rslab_p[s % 4]
        so_, _w = slab_off[(c0, s)]
        assert _w == w
        o = P.op("sp", lambda e: e.dma_start(out=sl, in_=xsl[:, so_:so_ + 16 * w].rearrange("p (k n) -> p k n", k=16)),
                 writes=[Rs], dma=True)
        if c0 == 0 and s == 3:
            wstate["gate"] = o
        sq = sqd[:, :, 0:w]
        P.op("act", lambda e: e.activation(out=sq, in_=sl, func=AF.Square), reads=[Rs], writes=[Rsqd])
        for k in range(16):
            P.op("pe", lambda e, k=k: e.matmul(psC[:, 0:w], lhsT=ones_bf, rhs=sqd[:, k, 0:w], start=(k == 0), stop=(k == 15)),
                 reads=[Rsqd, Rcbf], writes=[RpsC])
        rstd_from_psum(psC[:, 0:w], rs[:, 0:w], RpsC, Rrs, 2048.0)
        for k in range(16):
            if True:
                P.op("dve", lambda e, k=k: e.scalar_tensor_tensor(
                    out=dst[:, k, s * W:s * W + w], in0=sl[:, k, :], scalar=cs(goff + k), in1=rs[:, 0:w],
                    op0=ALU.mult, op1=ALU.mult), reads=[Rs, Rrs, Rcst], writes=[Rdst[k]])
            else:
                Rk = Res("slabk")
                P.op("pool", lambda e, k=k: e.tensor_scalar(out=sl[:, k, :], in0=sl[:, k, :], scalar1=cs(goff + k), scalar2=None, op0=ALU.mult),
                     reads=[Rs, Rcst], writes=[Rk])
                P.op("pool", lambda e, k=k: e.tensor_tensor(out=dst[:, k, s * 256:s * 256 + w], in0=sl[:, k, :], in1=rs[:, 0:w], op=ALU.mult),
                     reads=[Rs, Rk, Rrs], writes=[Rdst[k]])

    def prefix_slab(s):
        sl = slab_p[s % 4]
        Rs = Rslab_p[s % 4]
        so_, _w = slab_off[(0, s)]
        o = P.op("sp", lambda e: e.dma_start(out=sl, in_=xsl[:, so_:so_ + 16 * 128].rearrange("p (k n) -> p k n", k=16)),
                 writes=[Rs], dma=True)
        if s == 3:
            wstate["gate"] = o
        P.op("act", lambda e: e.activation(out=sqd[:, :, 0:128], in_=sl, func=AF.Square), reads=[Rs], writes=[Rsqd])
        for k in range(16):
            P.op("pe", lambda e, k=k: e.matmul(psC[:, 64 + s:65 + s], lhsT=sqd[:, k, 0:128], rhs=ones_bf[:, 0:1], start=(k == 0), stop=(k == 15)),
                 reads=[Rsqd, Rcbf], writes=[RpsC])
        P.op("dve", lambda e: e.tensor_tensor(out=hTp[:, :, 128 * s:128 * (s + 1)], in0=sl,
                                              in1=cst[:, C_G1:C_G1 + 16].unsqueeze(2).broadcast_to([128, 16, 128]), op=ALU.mult),
             reads=[Rs, Rcst], writes=RhTp)

    def prefix_rs():
        P.op("act", lambda e: e.activation(out=rs_col[:, 0:7], in_=psC[:, 64:71], func=AF.Ln, bias=EPS, scale=1.0 / 2048.0),
             reads=[RpsC], writes=[Rrs_col])
        P.op("act", lambda e: e.activation(out=rs_col[:, 0:7], in_=rs_col[:, 0:7], func=AF.Exp, scale=-0.5),
             reads=[Rrs_col], writes=[Rrs_col])

    def sumsq_hook(f, col0, w, first):
        if first:
            P.op("act", lambda e: e.activation(out=acc[:, 0:w], in_=xres[:, f, col0:col0 + w], func=AF.Square),
                 reads=[Rxres[f]], writes=[Racc])
        else:
            P.op("act", lambda e: e.activation(out=tmpq[:, 0:w], in_=xres[:, f, col0:col0 + w], func=AF.Square),
                 reads=[Rxres[f]], writes=[Rtmpq])
            P.op("dve", lambda e: e.tensor_tensor(out=acc[:, 0:w], in0=acc[:, 0:w], in1=tmpq[:, 0:w], op=ALU.add),
                 reads=[Racc, Rtmpq], writes=[Racc])

    def norm_finish_sliced(col0, slices, goff, dst, Rdst2, Rrs_list):
        for si, (s0, w) in enumerate(slices):
            P.op("pe", lambda e, s0=s0, w=w: e.matmul(psC[:, 0:w], lhsT=onesF, rhs=acc[:, s0:s0 + w], start=True, stop=True),
                 reads=[Racc, Rcst], writes=[RpsC])
            rstd_from_psum(psC[:, 0:w], rs2[:, s0:s0 + w], RpsC, Rrs_list[si], 2048.0)
            for k in range(16):
                P.op("dve", lambda e, k=k, s0=s0, w=w: e.scalar_tensor_tensor(
                    out=dst[:, k, s0:s0 + w], in0=xres[:, k, col0 + s0:col0 + s0 + w], scalar=cs(goff + k), in1=rs2[:, s0:s0 + w],
                    op0=ALU.mult, op1=ALU.mult), reads=[Rxres[k], Rrs_list[si], Rcst], writes=[Rdst2[k][si]])

    def norm_finish(col0, W, slices, goff, out_fn, out_res_fn, post=None):
        for (s0, w) in slices:
            P.op("pe", lambda e, s0=s0, w=w: e.matmul(psC[:, 0:w], lhsT=onesF, rhs=acc[:, s0:s0 + w], start=True, stop=True),
                 reads=[Racc, Rcst], writes=[RpsC])
            rstd_from_psum(psC[:, 0:w], rs2[:, s0:s0 + w], RpsC, Rrs2, 2048.0)
        for k in range(16):
            if True:
                P.op("dve", lambda e, k=k: e.scalar_tensor_tensor(
                    out=out_fn(k), in0=xres[:, k, col0:col0 + W], scalar=cs(goff + k), in1=rs2[:, 0:W],
                    op0=ALU.mult, op1=ALU.mult), reads=[Rxres[k], Rrs2, Rcst], writes=[out_res_fn(k)])
            else:
                P.op("pool", lambda e, k=k: e.tensor_scalar(out=tmpq[:, 0:W], in0=xres[:, k, col0:col0 + W], scalar1=cs(goff + k),
                                                            scalar2=None, op0=ALU.mult), reads=[Rxres[k], Rcst], writes=[Rtmpq])
                P.op("pool", lambda e, k=k: e.tensor_tensor(out=out_fn(k), in0=tmpq[:, 0:W], in1=rs2[:, 0:W], op=ALU.mult),
                     reads=[Rtmpq, Rrs2], writes=[out_res_fn(k)])
            if post is not None:
                post(k)

    def next_bank():
        tm = pools["tm"]
        b = tm[pools["tmi"] % len(tm)]
        pools["tmi"] += 1
        return b

    def next_group():
        fm = pools["fm"]
        g = fm[pools["fmi"] % len(fm)]
        pools["fmi"] += 1
        return g

    def tok_major_proj(slot, Rs, src, Rsrc, tile_list, ncol, evac):
        for t in tile_list:
            ps, Rp = next_bank()
            for k in range(16):
                P.op("pe", lambda e, k=k, t=t, ps=ps: e.matmul(ps[:, 0:ncol], lhsT=src[:, k, t * 128:(t + 1) * 128],
                                                                rhs=slot[:, k, 0:ncol], start=(k == 0), stop=(k == 15)),
                     reads=[Rsrc[k], Rs], writes=[Rp])
            evac(t, ps[:, 0:ncol], Rp)

    def feat_major_proj(slot, Rs, j, src, Rsrc, kc, col_slices, Rsrc_fn=None, k_outer=False):
        ps, Rb = next_group()
        if k_outer:
            order = [(si, k) for k in range(kc) for si in range(len(col_slices))]
        else:
            order = [(si, k) for si in range(len(col_slices)) for k in range(kc)]
        for si, k in order:
            c0, w = col_slices[si]
            rd = [Rs] + (Rsrc_fn(k, si) if Rsrc_fn is not None else [Rsrc[k]])
            P.op("pe", lambda e, k=k, si=si, c0=c0, w=w: e.matmul(
                ps[:, si, 0:w], lhsT=slot[:, k, j * 128:(j + 1) * 128], rhs=src[:, k, c0:c0 + w],
                start=(k == 0), stop=(k == kc - 1)),
                 reads=rd, writes=[Rb[si]])
        return ps, Rb

    def gate_math(G):
        T = G.T
        P.op("act", lambda e: e.activation(out=G.etmp, in_=G.gates[:, :, 4:8], func=AF.Exp, scale=-1.0),
             reads=[G.Rgates], writes=[G.Retmp])
        P.op("act", lambda e: e.activation(out=G.nlf, in_=G.etmp, func=AF.Ln, bias=1.0),
             reads=[G.Retmp], writes=[G.Rnlf])
        nlf_all = G.nlf.rearrange("p t n -> p (t n)")
        P.op("pe", lambda e: e.matmul(psC[:, 0:4 * T], lhsT=triM, rhs=nlf_all, start=True, stop=True),
             reads=[G.Rnlf, Rcst], writes=[RpsC])
        P.op("pe", lambda e: e.matmul(psC[:, 4 * T:8 * T], lhsT=onesF, rhs=nlf_all, start=True, stop=True),
             reads=[G.Rnlf, Rcst], writes=[RpsC])
        pcum = psC[:, 0:4 * T].rearrange("p (t n) -> p t n", t=T)
        ptot = psC[:, 4 * T:8 * T].rearrange("p (t n) -> p t n", t=T)
        P.op("dve", lambda e: e.tensor_tensor(out=G.arg, in0=pcum, in1=G.gates[:, :, 0:4], op=ALU.add),
             reads=[RpsC, G.Rgates], writes=[G.Rarg])
        P.op("act", lambda e: e.activation(out=G.wp, in_=G.arg, func=AF.Exp), reads=[G.Rarg], writes=[G.Rwp])
        P.op("dve", lambda e: e.tensor_copy(out=G.wpb, in_=G.wp), reads=[G.Rwp], writes=[G.Rwpb])
        P.op("act", lambda e: e.activation(out=G.ebt, in_=ptot, func=AF.Exp, scale=-1.0),
             reads=[RpsC], writes=[G.Rebt])
        P.op("act", lambda e: e.activation(out=G.ebs, in_=pcum, func=AF.Exp, scale=-1.0),
             reads=[RpsC], writes=[G.Rebs])

    def if_task(G, src, Rsrc, rsc=None):
        T = G.T

        def consume(slot, Rs):
            for t in range(T):
                for k in range(16):
                    P.op("pe", lambda e, k=k, t=t: e.matmul(psC[:, t * 8:t * 8 + 8], lhsT=src[:, k, t * 128:(t + 1) * 128],
                                                             rhs=slot[:, k, 0:8], start=(k == 0), stop=(k == 15)),
                         reads=[Rsrc[k], Rs], writes=[RpsC])
            pc = psC[:, 0:8 * T].rearrange("p (t n) -> p t n", t=T)
            bias_b = cst[:, C_BIF:C_BIF + 8].unsqueeze(1).broadcast_to([128, T, 8])
            if rsc is None:
                P.op("dve", lambda e: e.tensor_tensor(out=G.gates, in0=pc, in1=bias_b, op=ALU.add),
                     reads=[RpsC, Rcst], writes=[G.Rgates])
            else:
                P.op("dve", lambda e: e.tensor_tensor(out=G.gates, in0=pc, in1=rsc[:, 0:T].unsqueeze(2).broadcast_to([128, T, 8]), op=ALU.mult),
                     reads=[RpsC, Rrs_col], writes=[G.Rgates])
                P.op("dve", lambda e: e.tensor_tensor(out=G.gates, in0=G.gates, in1=bias_b, op=ALU.add),
                     reads=[G.Rgates, Rcst], writes=[G.Rgates])
            gate_math(G)
        task(w_if[:, :].rearrange("p (k n) -> p k n", k=16), 16, 8, consume)

    def v_task(G, src, Rsrc, vwb, Rvwb, h, rsc=None):
        def consume(slot, Rs):
            def evac(t, ps, Rp):
                if rsc is None:
                    P.op("dve", lambda e: e.tensor_scalar(out=vwb[:, t, h, :], in0=ps, scalar1=G.wp[:, t, h:h + 1], scalar2=None,
                                                          op0=ALU.mult), reads=[Rp, G.Rwp], writes=[Rvwb[t]])
                else:
                    P.op("dve", lambda e: e.tensor_scalar(out=vwb[:, t, h, :], in0=ps, scalar1=rsc[:, t:t + 1], scalar2=G.wp[:, t, h:h + 1],
                                                          op0=ALU.mult, op1=ALU.mult), reads=[Rp, G.Rwp, Rrs_col], writes=[Rvwb[t]])
            tok_major_proj(slot, Rs, src, Rsrc, range(G.T), 256, evac)
        task(blk(w_in_b, 16 + h), 16, 256, consume)

    qk_slices = [(0, 384), (384, 384), (768, 384)]

    def k_task(G, src, Rsrc, kb, Rkb, j, with_kT, rsc=None):
        def consume(slot, Rs):
            def evac(t, ps, Rp):
                if rsc is None:
                    P.op("act", lambda e: e.activation(out=kb[:, t, 256 * j:256 * (j + 1)], in_=ps, func=AF.Copy),
                         reads=[Rp], writes=[Rkb[t]])
                else:
                    P.op("act", lambda e: e.activation(out=kb[:, t, 256 * j:256 * (j + 1)], in_=ps, func=AF.Identity, scale=rsc[:, t:t + 1]),
                         reads=[Rp, Rrs_col], writes=[Rkb[t]])
            tok_major_proj(slot, Rs, src, Rsrc, range(G.T), 256, evac)
            if with_kT and j == 1:
                tb = [(psD[:, 0:512], RpsD), (psC[:, :].bitcast(BF16)[:, 0:512], RpsC)]
                for t in range(G.T):
                    pt, Rpt = tb[t % 2]
                    for h in range(4):
                        P.op("pe", lambda e, t=t, h=h, pt=pt: e.transpose(pt[:, h * 128:(h + 1) * 128], kb[:, t, h * 128:(h + 1) * 128], ident_bf),
                             reads=[Rkb[t], Rcbf], writes=[Rpt])
                    P.op("act", lambda e, t=t, pt=pt: e.activation(out=kT[:, :, t * 128:(t + 1) * 128],
                                                                   in_=pt.rearrange("p (h n) -> p h n", h=4), func=AF.Copy),
                         reads=[Rpt], writes=RkT)
        task(blk(w_in_b, 14 + j), 16, 256, consume)

    def q_task(j):
        def consume(slot, Rs):
            for jj in range(2):
                h = 2 * j + jj
                ps, Rb = feat_major_proj(slot, Rs, jj, hT, RhT, 16, qk_slices)
                P.op("act", lambda e, ps=ps, h=h: e.activation(out=qT[:, h, :].rearrange("p (s n) -> p s n", s=3),
                                                                in_=ps[:, :, 0:384], func=AF.Identity, scale=float(128 ** -0.5)),
                     reads=Rb, writes=[RqT[h]])
        task(blk(w_in_b, 12 + j), 16, 256, consume)

    def o_task(j):
        def consume(slot, Rs):
            def evac(t, ps, Rp):
                P.op("act", lambda e: e.activation(out=junk[:, 0:256], in_=ps, func=AF.Sigmoid), reads=[Rp], writes=[Rjunk])
                P.op("dve", lambda e: e.tensor_tensor(out=og[:, t, 256 * j:256 * (j + 1)], in0=junk[:, 0:256],
                                                      in1=cst[:, C_GMH + 256 * j:C_GMH + 256 * (j + 1)], op=ALU.mult),
                     reads=[Rjunk, Rcst], writes=[Rog[t]])
            tok_major_proj(slot, Rs, hT, RhT, range(9), 256, evac)
        task(blk(w_in_b, 20 + j), 16, 256, consume)

    psNum = [psA[:, 0, :].rearrange("p (h n) -> p h n", h=2), psA[:, 1, :].rearrange("p (h n) -> p h n", h=2)]
    psDen = psA[:, 2, 0:4]
    psNU = psA[:, 2, 4:8]
    mstate = {"first": True}

    def prev_ebt(G, T):
        if T > 0:
            return G.ebt[:, T - 1, :], G.Rebt
        return Gp.ebt[:, 6, :], Gp.Rebt

    def mlstm_part1(G, T):
        cols = slice(T * 128, (T + 1) * 128)
        for h in range(4):
            P.op("pe", lambda e, h=h: e.matmul(psC[:, h * 128:(h + 1) * 128], lhsT=kT[:, h, cols], rhs=qT[:, h, cols],
                                               start=True, stop=True), reads=[RkT[h], RqT[h]], writes=[RpsC])
        P.op("dve", lambda e: e.tensor_tensor(out=Sm[:, :, :], in0=psC[:, 0:512].rearrange("p (h n) -> p h n", h=4),
                                              in1=triM.unsqueeze(1).broadcast_to([128, 4, 128]), op=ALU.mult),
             reads=[RpsC, Rcst], writes=[RSm])
        for h in range(4):
            pn = psNum[h // 2][:, h % 2, :]
            P.op("pe", lambda e, h=h, pn=pn: e.matmul(pn, lhsT=Sm[:, h, :], rhs=vw[:, T, h, :], start=True, stop=False),
                 reads=[RSm, Rvw[T]], writes=[RpsA[h // 2]])
            P.op("pe", lambda e, h=h, pn=pn: e.matmul(pn, lhsT=qT[:, h, cols], rhs=Cbf[:, h, :], start=False, stop=True),
                 reads=[RqT[h], RCbf[h]], writes=[RpsA[h // 2]])
            P.op("pe", lambda e, h=h: e.matmul(psDen[:, h:h + 1], lhsT=Sm[:, h, :], rhs=G.wpb[:, T, h:h + 1], start=True, stop=False),
                 reads=[RSm, G.Rwpb], writes=[RpsA[2]])
            P.op("pe", lambda e, h=h: e.matmul(psDen[:, h:h + 1], lhsT=qT[:, h, cols], rhs=nbf[:, h:h + 1], start=False, stop=True),
                 reads=[RqT[h], Rnbf], writes=[RpsA[2]])
        S = Rsm
        P.op("dve", lambda e: e.tensor_tensor(out=d1, in0=psDen, in1=G.ebs[:, T, :], op=ALU.mult), reads=[RpsA[2], G.Rebs], writes=[S["d1"]])
        P.op("dve", lambda e: e.scalar_tensor_tensor(out=d2, in0=d1, scalar=-1.0, in1=d1, op0=ALU.mult, op1=ALU.max),
             reads=[S["d1"]], writes=[S["d2"]])
        P.op("dve", lambda e: e.tensor_scalar(out=d3, in0=d2, scalar1=1.0, scalar2=None, op0=ALU.max), reads=[S["d2"]], writes=[S["d3"]])
        P.op("dve", lambda e: e.reciprocal(out=rec, in_=d3), reads=[S["d3"]], writes=[S["rec"]])
        P.op("dve", lambda e: e.tensor_tensor(out=cc, in0=rec, in1=G.ebs[:, T, :], op=ALU.mult), reads=[S["rec"], G.Rebs], writes=[S["cc"]])
        P.op("dve", lambda e: e.memset(ss, 0.0), writes=[S["ss"]])
        for h in range(4):
            pn = psNum[h // 2][:, h % 2, :]
            P.op("act", lambda e, h=h, pn=pn: e.activation(out=junk[:, 0:256], in_=pn, func=AF.Square, scale=cc[:, h:h + 1],
                                                           accum_out=ss[:, h:h + 1]),
                 reads=[RpsA[h // 2], S["ss"], S["cc"]], writes=[Rjunk, S["ss"]])
        P.op("act", lambda e: e.activation(out=lnr, in_=ss, func=AF.Ln, bias=EPS, scale=1.0 / 256.0), reads=[S["ss"]], writes=[S["lnr"]])
        P.op("act", lambda e: e.activation(out=rr, in_=lnr, func=AF.Exp, scale=-0.5), reads=[S["lnr"]], writes=[S["rr"]])
        P.op("dve", lambda e: e.tensor_tensor(out=rowsc, in0=cc, in1=rr, op=ALU.mult), reads=[S["cc"], S["rr"]], writes=[S["rowsc"]])
        for h in range(4):
            pn = psNum[h // 2][:, h % 2, :]
            P.op("dve", lambda e, h=h, pn=pn: e.scalar_tensor_tensor(
                out=ym[:, h * 256:(h + 1) * 256], in0=pn, scalar=rowsc[:, h:h + 1], in1=og[:, T, h * 256:(h + 1) * 256],
                op0=ALU.mult, op1=ALU.mult), reads=[RpsA[h // 2], S["rowsc"], Rog[T]], writes=[Rym])

    def mlstm_transposes(T):
        for j in range(8):
            P.op("pe", lambda e, j=j: e.transpose(psD[:, j * 128:(j + 1) * 128], ym[:, j * 128:(j + 1) * 128], ident_bf),
                 reads=[Rym, Rcbf], writes=[RpsD])
        if T == 0:
            i0, n, src0 = 0, 5, 123
        else:
            i0, n, src0 = 5 + 128 * (T - 1), 128, 0
        P.op("act", lambda e: e.activation(out=yT[:, 8:16, i0:i0 + n],
                                           in_=psD[:, :].rearrange("p (j n) -> p j n", j=8)[:, :, src0:src0 + n], func=AF.Copy),
             reads=[RpsD], writes=RyT[8:16])

    def mlstm_update(G, T, kb, Rkb, vwb, Rvwb, need_c):
        first = mstate["first"]
        for h in range(4):
            pu = psNum[h // 2][:, h % 2, :]
            P.op("pe", lambda e, h=h, pu=pu: e.matmul(pu, lhsT=kb[:, T, h * 128:(h + 1) * 128], rhs=vwb[:, T, h, :], start=True, stop=True),
                 reads=[Rkb[T], Rvwb[T]], writes=[RpsA[h // 2]])
            P.op("pe", lambda e, h=h: e.matmul(psNU[:, h:h + 1], lhsT=kb[:, T, h * 128:(h + 1) * 128], rhs=G.wpb[:, T, h:h + 1], start=True, stop=True),
                 reads=[Rkb[T], G.Rwpb], writes=[RpsA[2]])
        pe4, Rpe = prev_ebt(G, T)
        for h in range(4):
            pu = psNum[h // 2][:, h % 2, :]
            if first:
                P.op("dve", lambda e, h=h, pu=pu: e.tensor_copy(out=E32[:, h, :], in_=pu), reads=[RpsA[h // 2]], writes=[RE32[h]])
            else:
                P.op("dve", lambda e, h=h, pu=pu: e.scalar_tensor_tensor(out=E32[:, h, :], in0=E32[:, h, :], scalar=pe4[:, h:h + 1], in1=pu,
                                                                           op0=ALU.mult, op1=ALU.add),
                     reads=[RE32[h], RpsA[h // 2], Rpe], writes=[RE32[h]])
            if need_c:
                P.op("act", lambda e, h=h: e.activation(out=Cbf[:, h, :], in_=E32[:, h, :], func=AF.Identity, scale=G.ebt[:, T, h:h + 1]),
                     reads=[RE32[h], G.Rebt], writes=[RCbf[h]])
        if first:
            P.op("dve", lambda e: e.tensor_copy(out=En, in_=psNU), reads=[RpsA[2]], writes=[REn])
        else:
            P.op("dve", lambda e: e.tensor_tensor(out=En, in0=En, in1=pe4, op=ALU.mult), reads=[REn, Rpe], writes=[REn])
            P.op("dve", lambda e: e.tensor_tensor(out=En, in0=En, in1=psNU, op=ALU.add), reads=[REn, RpsA[2]], writes=[REn])
        if need_c:
            P.op("dve", lambda e: e.tensor_tensor(out=nbf, in0=En, in1=G.ebt[:, T, :], op=ALU.mult), reads=[REn, G.Rebt], writes=[Rnbf])
        mstate["first"] = False

    cg_slices = [(121 + 343 * s, 345) for s in range(3)]

    def conv_chunk(kind, b, j, slot, Rs):
        ps, Rb = feat_major_proj(slot, Rs, j, hT, RhT, 16, cg_slices)
        if kind == "gc":
            for si in range(3):
                P.op("act", lambda e, si=si: e.activation(out=gcs[j][:, si, :], in_=ps[:, si, 0:345], func=AF.Copy),
                     reads=[Rb[si]], writes=[Rgcs[j]])
        elif kind == "u":
            for si in range(3):
                P.op("dve", lambda e, si=si: e.tensor_tensor(out=gcs[j][:, si, :], in0=ps[:, si, 0:345], in1=gcs[j][:, si, :], op=ALU.mult),
                     reads=[Rb[si], Rgcs[j]], writes=[Rgcs[j]])
        else:
            fc = 2 * b + j
            z = gcs[j]
            P.op("act", lambda e: e.activation(out=cacc[:, :, :], in_=z[:, :, 0:343], func=AF.Identity,
                                               scale=cs(C_SCW + 3 * fc + 0)), reads=[Rgcs[j], Rcst], writes=[Rcacc])
            P.op("dve", lambda e: e.scalar_tensor_tensor(out=cacc[:, :, :], in0=z[:, :, 1:344], scalar=cs(C_SCW + 3 * fc + 1),
                                                         in1=cacc[:, :, :], op0=ALU.mult, op1=ALU.add),
                 reads=[Rgcs[j], Rcacc, Rcst], writes=[Rcacc])
            P.op("dve", lambda e: e.scalar_tensor_tensor(out=cacc[:, :, :], in0=z[:, :, 2:345], scalar=cs(C_SCW + 3 * fc + 2),
                                                         in1=cacc[:, :, :], op0=ALU.mult, op1=ALU.add),
                 reads=[Rgcs[j], Rcacc, Rcst], writes=[Rcacc])
            for si in range(3):
                P.op("dve", lambda e, si=si: e.tensor_tensor(out=yT[:, fc, 343 * si:343 * (si + 1)],
                                                             in0=ps[:, si, 2:345], in1=cacc[:, si, :], op=ALU.mult),
                     reads=[Rb[si], Rcacc], writes=[RyT[fc]])

    def conv_task(kind, b, split):
        st = {}
        col = {"gc": 4, "u": 8, "gb": 0}[kind] + b

        def consume(slot, Rs):
            st["slot"], st["Rs"] = slot, Rs
            conv_chunk(kind, b, 0, slot, Rs)
            if not split:
                conv_chunk(kind, b, 1, slot, Rs)
        task(blk(w_in_b, col), 16, 256, consume)
        if split:
            return lambda: conv_chunk(kind, b, 1, st["slot"], st["Rs"])
        return None

    def setup():
        P.op("sp", lambda e: e.dma_start(out=cst[:, :], in_=cst_d[:, :]), writes=[Rcst], dma=True)
        P.op("dve", lambda e: e.tensor_copy(out=cbf[:, 0:128], in_=cst[:, C_ONE:C_ONE + 128]), reads=[Rcst], writes=[Rcbf])
        P.op("dve", lambda e: e.tensor_copy(out=cbf[:, 128:256], in_=cst[:, C_IDN:C_IDN + 128]), reads=[Rcst], writes=[Rcbf])
    call(setup)
    for s in range(7):
        call(lambda s=s: prefix_slab(s))
    call(prefix_rs)

    def ext_norm_alias():
        for r in Rslab:
            P.alias(r, Rslab_p)
    call(ext_norm_alias)
    ext_slabs = [lambda s=s: norm_slab(896, s, 1152, C_G1, hT, RhT) for s in range(5)]
    if_task(Gp, hTp, RhTp, rs_col)
    for h in range(4):
        v_task(Gp, hTp, RhTp, vw_p, Rvw_p, h, rs_col)
        call(ext_slabs[h])
    k_task(Gp, hTp, RhTp, k_tm_p, Rk_tm_p, 0, False, rs_col)
    call(ext_slabs[4])
    k_task(Gp, hTp, RhTp, k_tm_p, Rk_tm_p, 1, False, rs_col)
    call(lambda: tap("hT", hT[:, 0, 0:1152], RhT))

    def phase_c_begin():
        for r in resAB:
            P.alias(r, Rslab)
        set_pools(banksB, [(psB, RpsB)])
    call(phase_c_begin)
    sweep = [lambda T=T: mlstm_update(Gp, T, k_tm_p, Rk_tm_p, vw_p, Rvw_p, T == 6) for T in range(7)]
    call(sweep[0])
    if_task(Gx, hT, RhT)
    for h in range(4):
        call(sweep[1 + h])
        v_task(Gx, hT, RhT, vw, Rvw, h)
    for j in range(2):
        call(sweep[5 + j])
        k_task(Gx, hT, RhT, k_tm, Rk_tm, j, True)

    def phase_d_begin():
        for r in resC:
            P.alias(r, resCp)
        set_pools(banksA + banksB, [(psA, RpsA), (psB, RpsB)])
    call(phase_d_begin)
    for j in range(2):
        q_task(j)
    for j in range(4):
        o_task(j)

    def phase_e_begin():
        for r in conv_res:
            P.alias(r, [Rsqd])
        for r in RyT:
            P.alias(r, RhTp)
        set_pools(banksB, [(psB, RpsB)])
    call(phase_e_begin)
    conv_list = []
    for b in range(4):
        conv_list += [("gc", b), ("u", b), ("gb", b)]
    ci = 0
    for T in range(9):
        call(lambda T=T: mlstm_part1(Gx, T))
        second = conv_task(conv_list[ci][0], conv_list[ci][1], True); ci += 1
        if T < 8:
            call(lambda T=T: mlstm_update(Gx, T, k_tm, Rk_tm, vw, Rvw, True))
        call(lambda T=T: mlstm_transposes(T))
        call(second)
        if T < 3:
            conv_task(conv_list[ci][0], conv_list[ci][1], False); ci += 1
    assert ci == 12
    call(lambda: tap("yT", yTt[:, 0:16464], RyT))

    def xreload():
        set_pools(banksA + banksB, [(psA, RpsA), (psB, RpsB)])
        for k in range(16):
            P.alias(Rxres[k], resAB + resC + Rslab)
        for k4 in range(4):
            P.op("sp", lambda e, k4=k4: e.dma_start(out=xres[:, 4 * k4:4 * k4 + 4, :],
                                                    in_=xr[:, 4116 * k4:4116 * (k4 + 1)].rearrange("p (k n) -> p k n", k=4)),
                 writes=Rxres[4 * k4:4 * k4 + 4], dma=True)
        for r in (Racc, Rtmpq, Rrs2):
            P.alias(r, conv_res + mlstm_small_res)
    call(xreload)
    wo_slices = [(343 * s, 343) for s in range(3)]
    n2_slices = [(0, 345), (345, 342), (687, 342)]
    RhT2 = [[Res("hT2_%d_%d" % (k, si)) for si in range(3)] for k in range(16)]
    RhT3 = [[Res("hT3_%d_%d" % (k, si)) for si in range(2)] for k in range(16)]
    Rrs2_3 = [Res("rs2_%d" % si) for si in range(3)]
    ff_src = lambda k, si: [RhT2[k][si]] + ([RhT2[k][si - 1]] if si > 0 else [])

    def wout_task(b):
        def consume(slot, Rs):
            for j in range(2):
                f = 2 * b + j
                ps, Rb = feat_major_proj(slot, Rs, j, yT, RyT, 16, wo_slices)
                P.op("dve", lambda e, ps=ps, f=f: e.tensor_tensor(out=xres[:, f, :].rearrange("p (s n) -> p s n", s=3),
                                                                  in0=ps[:, :, 0:343], in1=xres[:, f, :].rearrange("p (s n) -> p s n", s=3),
                                                                  op=ALU.add), reads=Rb + [Rxres[f]], writes=[Rxres[f]])
                sumsq_hook(f, 0, 1029, f == 0)
        task(blk(w_out_b, b), 16, 256, consume)
    for b in range(8):
        wout_task(b)

    def norm2():
        tap("x1", arena[:, 0:16464], Rxres)
        for k in range(16):
            for si in range(3):
                P.alias(RhT2[k][si], [RhT[k]])
        for si in range(3):
            P.alias(Rrs2_3[si], [Rrs2])
        norm_finish_sliced(0, n2_slices, C_G2, hT, RhT2, Rrs2_3)
        for r in RaT:
            P.alias(r, RyT)
        for r in Rupg + Rupu:
            P.alias(r, conv_res + mlstm_small_res)
    call(norm2)

    ff_slices = [(1 + 342 * s, 344) for s in range(3)]
    groups = [(0, 16), (16, 16), (32, 12)]

    def conv_evac(ps, Rb, dst, Rdst, cidx):
        w0, w1, w2 = (cs(C_FCW + 3 * cidx + i) for i in range(3))
        P.op("act", lambda e: e.activation(out=dst[:, :, :], in_=ps[:, :, 2:344], func=AF.Identity, bias=cs(C_FCB + cidx), scale=w2),
             reads=Rb + [Rcst], writes=[Rdst])
        P.op("dve", lambda e: e.scalar_tensor_tensor(out=dst[:, :, :], in0=ps[:, :, 1:343], scalar=w1, in1=dst[:, :, :], op0=ALU.mult, op1=ALU.add),
             reads=Rb + [Rdst, Rcst], writes=[Rdst])
        P.op("dve", lambda e: e.scalar_tensor_tensor(out=dst[:, :, :], in0=ps[:, :, 0:342], scalar=w0, in1=dst[:, :, :], op0=ALU.mult, op1=ALU.add),
             reads=Rb + [Rdst, Rcst], writes=[Rdst])

    def ffg_task(jj):
        def consume(slot, Rs):
            for j in range(2):
                c = 2 * jj + j
                ps, Rb = feat_major_proj(slot, Rs, j, hT, RhT, 16, ff_slices, ff_src)
                conv_evac(ps, Rb, upg[j], Rupg[j], c)
                P.op("act", lambda e, j=j: e.activation(out=upg[j][:, :, :], in_=upg[j][:, :, :], func=AF.Silu),
                     reads=[Rupg[j]], writes=[Rupg[j]])
        task(blk(w_up_b, jj), 16, 256, consume)

    def ffu_task(jj, g0):
        def consume(slot, Rs):
            for j in range(2):
                c = 2 * jj + j
                ps, Rb = feat_major_proj(slot, Rs, j, hT, RhT, 16, ff_slices, ff_src)
                conv_evac(ps, Rb, upu[j], Rupu[j], 44 + c)
                P.op("dve", lambda e, j=j, c=c: e.tensor_tensor(out=aT[:, c - g0, :].rearrange("p (s n) -> p s n", s=3),
                                                                 in0=upg[j][:, :, :], in1=upu[j][:, :, :], op=ALU.mult),
                     reads=[Rupg[j], Rupu[j]], writes=[RaT[c - g0]])
        task(blk(w_up_b, 22 + jj), 16, 256, consume)

    wd_slices = [(2, 512), (514, 512)]

    Rtl = Res("tbl")

    def table_preload():
        P.op("act", lambda e: e.activation(out=regE[:, 4110:4111], in_=cst[:, C_ONE:C_ONE + 1], func=AF.Ln), reads=[Rcst], writes=[Rtl])

    def wdown_task(g0, gn, cb, last):
        def consume(slot, Rs):
            if last and cb == 0:
                table_preload()
            for j in range(2):
                f = 2 * cb + j
                ps, Rb = feat_major_proj(slot, Rs, j, aT, RaT[0:gn], gn, wd_slices, k_outer=(cb == 0 and j == 0))
                P.op("dve", lambda e, ps=ps, f=f: e.tensor_tensor(out=xres[:, f, 5:1029].rearrange("p (s n) -> p s n", s=2),
                                                                  in0=ps[:, 0:2, 0:512], in1=xres[:, f, 5:1029].rearrange("p (s n) -> p s n", s=2),
                                                                  op=ALU.add), reads=Rb[0:2] + [Rxres[f]], writes=[Rxres[f]])
                if last:
                    sumsq_hook(f, 5, 1024, f == 0)
        task(blk(w_down_b, cb, 44)[:, g0:g0 + gn, :], gn, 256, consume)

    for gi, (g0, gn) in enumerate(groups):
        for jj in range(g0 // 2, (g0 + gn) // 2):
            ffg_task(jj)
            ffu_task(jj, g0)
        for cb in range(8):
            wdown_task(g0, gn, cb, gi == len(groups) - 1)

    own_slices = [(0, 512), (512, 512)]

    def norm3():
        tap("x2", arena[:, 0:16464], Rxres)
        for k in range(16):
            for si in range(2):
                P.alias(RhT3[k][si], RhT2[k])
        for si in range(2):
            P.alias(Rrs2_3[si], Rrs2_3)
        norm_finish_sliced(5, own_slices, C_G3, hT, RhT3, Rrs2_3)
        P.alias(Rppw, RaT)
        P.alias(RpTb, RaT)
        P.op("pool", lambda e: e.dma_start(out=ppw, in_=w_ppv[:, :, :]), writes=[Rppw], dma=True)
        P.op("pool", lambda e: e.dma_start(out=pTb, in_=pTv[:, :, :]), writes=[RpTb], dma=True)
        for r in Rsgt + Rppt:
            P.alias(r, Rupg + Rupu)
    call(norm3)
    pairs = [((psA[:, 0, :], RpsA[0]), (psA[:, 1, :], RpsA[1])),
             ((psA[:, 2, :], RpsA[2]), (psB[:, 0, :], RpsB[0])),
             ((psB[:, 1, :], RpsB[1]), (psB[:, 2, :], RpsB[2]))]
    pr = {"i": 0}
    Rx2 = [[Res("x2_%d_%d" % (f, sl_)) for sl_ in range(2)] for f in range(16)]
    Racc2 = [Res("acc2_0"), Res("acc2_1")]
    Rrs2s = [Res("rs2s0"), Res("rs2s1")]

    def ple_alias():
        for f in range(16):
            for sl_ in range(2):
                P.alias(Rx2[f][sl_], [Rxres[f]])
        for sl_ in range(2):
            P.alias(Racc2[sl_], [Racc])
            P.alias(Rrs2s[sl_], [Rrs2] + Rrs2_3)
    call(ple_alias)

    def pg_task(b, s):
        def consume(slot, Rs):
            for j in range(2):
                f = 2 * b + j
                (pg_, Rg), (pp_, Rp) = pairs[pr["i"] % 3]
                bi = pr["i"] % 2
                pr["i"] += 1
                xs = xres[:, f, 5 + 512 * s:5 + 512 * (s + 1)]
                for k in range(16):
                    P.op("pe", lambda e, k=k, pg_=pg_, j=j: e.matmul(pg_[:, 0:512], lhsT=slot[:, k, j * 128:(j + 1) * 128],
                                                                     rhs=hT[:, k, 512 * s:512 * (s + 1)], start=(k == 0), stop=(k == 15)),
                         reads=[Rs, RhT3[k][s]], writes=[Rg])
                for k in range(2):
                    P.op("pe", lambda e, k=k, pp_=pp_, f=f: e.matmul(pp_[:, 0:512], lhsT=ppw[:, k, f * 128:(f + 1) * 128],
                                                                     rhs=pTb[:, k, 512 * s:512 * (s + 1)], start=(k == 0), stop=(k == 1)),
                         reads=[Rppw, RpTb], writes=[Rp])
                P.op("act", lambda e, pg_=pg_, bi=bi: e.activation(out=sgt[bi], in_=pg_[:, 0:512], func=AF.Sigmoid), reads=[Rg], writes=[Rsgt[bi]])
                P.op("dve", lambda e, pp_=pp_, bi=bi: e.tensor_tensor(out=ppt[bi], in0=pp_[:, 0:512], in1=sgt[bi], op=ALU.mult),
                     reads=[Rp, Rsgt[bi]], writes=[Rppt[bi]])
                P.op("dve", lambda e, bi=bi, xs=xs: e.tensor_tensor(out=xs, in0=xs, in1=ppt[bi], op=ALU.add),
                     reads=[Rppt[bi], Rx2[f][s]], writes=[Rx2[f][s]])
                a_ = acc[:, 512 * s:512 * (s + 1)]
                if f == 0:
                    P.op("act", lambda e, xs=xs, a_=a_: e.activation(out=a_, in_=xs, func=AF.Square), reads=[Rx2[f][s]], writes=[Racc2[s]])
                else:
                    P.op("act", lambda e, xs=xs: e.activation(out=tmpq[:, 0:512], in_=xs, func=AF.Square), reads=[Rx2[f][s]], writes=[Rtmpq])
                    P.op("dve", lambda e, a_=a_: e.tensor_tensor(out=a_, in0=a_, in1=tmpq[:, 0:512], op=ALU.add),
                         reads=[Racc2[s], Rtmpq], writes=[Racc2[s]])
        task(blk(w_pg_b, b), 16, 256, consume)

    def final_rs(s):
        P.op("pe", lambda e: e.matmul(psC[:, 0:512], lhsT=onesF, rhs=acc[:, 512 * s:512 * (s + 1)], start=True, stop=True),
             reads=[Racc2[s], Rcst], writes=[RpsC])
        rstd_from_psum(psC[:, 0:512], rs2[:, 512 * s:512 * (s + 1)], RpsC, Rrs2s[s], 2048.0)

    def final_ops(s, ks):
        for k in ks:
            xs = xres[:, k, 5 + 512 * s:5 + 512 * (s + 1)]
            P.op("dve", lambda e, k=k, xs=xs: e.scalar_tensor_tensor(out=xs, in0=xs, scalar=cs(C_GF + k), in1=rs2[:, 512 * s:512 * (s + 1)],
                                                                      op0=ALU.mult, op1=ALU.mult),
                 reads=[Rx2[k][s], Rrs2s[s], Rcst], writes=[Rx2[k][s]])
            P.op("sp", lambda e, k=k, xs=xs: e.dma_start(out=outTv[:, k, 512 * s:512 * (s + 1)], in_=xs),
                 reads=[Rx2[k][s]], writes=[Res("o")], dma=True)

    for b in range(8):
        pg_task(b, 0)
    call(lambda: final_rs(0))
    for b in range(8):
        pg_task(b, 1)
        call(lambda b=b: final_ops(0, [2 * b, 2 * b + 1]))
    call(table_preload)
    call(lambda: final_rs(1))
    call(lambda: final_ops(1, range(16)))

    def finish():
        Rfin = Res("fin")
        for o in P.ops["sp"]:
            if o.is_dma:
                Rfin.rdma.append(o)
        P.op("sp", lambda e: e.nop(), writes=[Rfin])
    call(finish)

    run_tasks()
    P.emit()
    return nc


def _consts(norm_mix_g, norm_ffn_g, norm_ple_g, final_norm_g, short_conv_w, ffn_conv_w, ffn_conv_b,
            b_igate, b_fgate, mh_norm_g):
    c = np.zeros((128, NCST), np.float32)
    pc = lambda v: np.ascontiguousarray(np.asarray(v, np.float32).reshape(-1, 128).T)
    c[:, C_G1:C_G1 + 16] = pc(norm_mix_g)
    c[:, C_G2:C_G2 + 16] = pc(norm_ffn_g)
    c[:, C_G3:C_G3 + 16] = pc(norm_ple_g)
    c[:, C_GF:C_GF + 16] = pc(final_norm_g)
    scw = np.asarray(short_conv_w, np.float32).reshape(3, 8, 128)
    c[:, C_SCW:C_SCW + 24] = scw.transpose(2, 1, 0).reshape(128, 24)
    fcw = np.asarray(ffn_conv_w, np.float32).reshape(3, 88, 128)
    c[:, C_FCW:C_FCW + 264] = fcw.transpose(2, 1, 0).reshape(128, 264)
    c[:, C_FCB:C_FCB + 88] = pc(ffn_conv_b)
    c[:, C_BIF:C_BIF + 4] = np.asarray(b_igate, np.float32).reshape(1, 4)
    c[:, C_BIF + 4:C_BIF + 8] = np.asarray(b_fgate, np.float32).reshape(1, 4)
    c[:, C_GMH:C_GMH + 1024] = np.asarray(mh_norm_g, np.float32).reshape(1, 1024)
    c[:, C_TRI:C_TRI + 128] = np.triu(np.ones((128, 128), np.float32))
    c[:, C_ONE:C_ONE + 128] = 1.0
    c[:, C_IDN:C_IDN + 128] = np.eye(128, dtype=np.float32)
    return c


def make_in_maps(x, p, norm_mix_g, w_in, b_igate, b_fgate, short_conv_w, mh_norm_g, w_out,
                 norm_ffn_g, w_up, ffn_conv_w, ffn_conv_b, w_down, norm_ple_g, w_pg, w_pp, final_norm_g):
    x = np.asarray(x, np.float32)
    p = np.asarray(p, np.float32)
    cst = _consts(norm_mix_g[0], norm_ffn_g[0], norm_ple_g[0], final_norm_g, short_conv_w[0], ffn_conv_w[0],
                  ffn_conv_b[0], b_igate[0], b_fgate[0], mh_norm_g[0])
    def blocked(w, nb):
        w = np.asarray(w, np.float32)
        kc = w.shape[0] // 128
        return np.ascontiguousarray(w.reshape(kc, 128, nb, 256).transpose(2, 1, 0, 3)).reshape(nb, 128, kc * 256)
    w_in0 = np.asarray(w_in[0], np.float32)
    shared = {
        "w_in_b": blocked(w_in0[:, 0:6144], 24),
        "w_if": np.ascontiguousarray(w_in0[:, 6144:6152].reshape(16, 128, 8).transpose(1, 0, 2)).reshape(128, 128),
        "w_out_b": blocked(w_out[0], 8),
        "w_up_b": blocked(w_up[0], 44),
        "w_down_b": blocked(w_down[0], 8),
        "w_pg_b": blocked(w_pg[0], 8),
        "w_pp": np.ascontiguousarray(np.asarray(w_pp[0], np.float32)),
        "cst": cst,
    }
    maps = []
    for c in range(8):
        b, half = c // 2, c % 2
        xT = np.zeros((2048, 2048), np.float32)
        if half == 1:
            xT[:, 0:1024] = x[b, 0:1024].T
        xT[:, 1024:2048] = x[b, half * 1024:(half + 1) * 1024].T
        pT = np.ascontiguousarray(p[0, b, half * 1024:(half + 1) * 1024].T)
        m = dict(shared)
        x4 = xT.reshape(16, 128, 2048)
        parts = []
        for (c0_, n_, W_) in ((0, 896, 128), (896, 1152, 256)):
            for s_ in range((n_ + W_ - 1) // W_):
                w_ = min(W_, n_ - W_ * s_)
                a0 = c0_ + W_ * s_
                parts.append(x4[:, :, a0:a0 + w_].transpose(1, 0, 2).reshape(128, 16 * w_))
        m["xsl"] = np.ascontiguousarray(np.concatenate(parts, axis=1))
        m["xr"] = np.ascontiguousarray(x4[:, :, 1019:2048].transpose(1, 0, 2)).reshape(128, 16 * 1029)
        m["pT"] = pT
        maps.append(m)
    return maps


def kernel(x, p, norm_mix_g, w_in, b_igate, b_fgate, short_conv_w, mh_norm_g, w_out,
           norm_ffn_g, w_up, ffn_conv_w, ffn_conv_b, w_down, norm_ple_g, w_ple_gate,
           w_ple_proj, final_norm_g):
    maps = make_in_maps(x, p, norm_mix_g, w_in, b_igate, b_fgate, short_conv_w, mh_norm_g, w_out,
                        norm_ffn_g, w_up, ffn_conv_w, ffn_conv_b, w_down, norm_ple_g, w_ple_gate,
                        w_ple_proj, final_norm_g)
    nc = build_nc()
    res = run_bass_kernel_spmd(nc, maps, core_ids=list(range(8)))
    out = np.empty((4, 2048, 2048), np.float32)
    for c in range(8):
        b, half = c // 2, c % 2
        out[b, half * 1024:(half + 1) * 1024, :] = res.results[c]["outT"].T
    return out
```

```python
import contextlib
import numpy as np
import concourse.bass as bass
import concourse.mybir as mybir
from concourse.bass_utils import run_bass_kernel_spmd

F32 = mybir.dt.float32
BF16 = mybir.dt.bfloat16
AF = mybir.ActivationFunctionType
ALU = mybir.AluOpType

EPS = 1e-6
NS = 4
N_DMA_SEMS = 8

C_G1, C_G2, C_G3, C_GF = 0, 16, 32, 48
C_SCW = 64
C_FCW = 88
C_FCB = 352
C_BIF = 440
C_GMH = 448
C_TRI = 1472
C_ONE = 1600
C_IDN = 1728
NCST = 1856


class Res:
    __slots__ = ("name", "writer", "readers", "rdma")

    def __init__(self, name):
        self.name = name
        self.writer = None
        self.readers = {}
        self.rdma = []


class Op:
    __slots__ = ("eng", "fn", "deps", "signal", "sem", "val", "is_dma")

    def __init__(self, eng, fn, is_dma):
        self.eng = eng
        self.fn = fn
        self.deps = []
        self.signal = False
        self.sem = None
        self.val = 0
        self.is_dma = is_dma


ENGS = ("pe", "act", "dve", "pool", "sp")


class Prog:
    def __init__(self, nc):
        self.nc = nc
        self.ops = {e: [] for e in ENGS}

    def op(self, eng, fn, reads=(), writes=(), dma=False):
        o = Op(eng, fn, dma)
        deps = []
        for r in reads:
            if r.writer is not None:
                deps.append(r.writer)
        for w in writes:
            if w.writer is not None:
                deps.append(w.writer)
            deps.extend(w.readers.values())
            deps.extend(w.rdma)
        seen = set()
        for d in deps:
            if id(d) in seen or d is o:
                continue
            seen.add(id(d))
            if d.eng == "pe" and eng == "pe" and not d.is_dma:
                continue
            d.signal = True
            o.deps.append(d)
        for r in reads:
            if dma:
                r.rdma.append(o)
            else:
                r.readers[eng] = o
        for w in writes:
            w.writer = o
            w.readers = {}
            w.rdma = []
        self.ops[eng].append(o)
        return o

    def alias(self, new, olds):
        for r in olds:
            for d in ([r.writer] if r.writer is not None else []) + list(r.readers.values()) + r.rdma:
                new.rdma.append(d)

    def emit(self):
        nc = self.nc
        with contextlib.ExitStack() as st:
            esem = {e: st.enter_context(nc.semaphore("s_" + e)) for e in ENGS}
            dsem = {e: [st.enter_context(nc.semaphore("d_%s%d" % (e, i))) for i in range(N_DMA_SEMS)]
                    for e in ("sp", "act", "pool")}
            for e in ENGS:
                cnt = 0
                nd = 0
                for o in self.ops[e]:
                    if o.is_dma:
                        o.sem = dsem[e][nd % N_DMA_SEMS]
                        o.val = 16 * (nd // N_DMA_SEMS + 1)
                        nd += 1
                    elif o.signal:
                        cnt += 1
                        o.sem = esem[e]
                        o.val = cnt
            block = st.enter_context(nc.Block())

            def run(e, eng):
                seen = {}
                nd = 0
                for o in self.ops[e]:
                    waits = {}
                    for d in o.deps:
                        k = id(d.sem)
                        if k not in waits or waits[k][1] < d.val:
                            waits[k] = (d.sem, d.val)
                    if o.is_dma:
                        if nd >= N_DMA_SEMS:
                            k = id(o.sem)
                            v = o.val - 16
                            if k not in waits or waits[k][1] < v:
                                waits[k] = (o.sem, v)
                        nd += 1
                    for k, (s, v) in waits.items():
                        if seen.get(k, 0) >= v:
                            continue
                        seen[k] = v
                        eng.wait_ge(s, v)
                    ins = o.fn(eng)
                    if o.is_dma:
                        ins.then_inc(o.sem, 16)
                    elif o.signal:
                        ins.then_inc(o.sem, 1)

            @block.tensor
            def _(eng):
                run("pe", eng)

            @block.scalar
            def _(eng):
                run("act", eng)

            @block.vector
            def _(eng):
                run("dve", eng)

            @block.gpsimd
            def _(eng):
                run("pool", eng)

            @block.sync
            def _(eng):
                run("sp", eng)


def build_nc(debug=()):
    nc = bass.Bass("TRN2", target_bir_lowering=False)
    dt_in = lambda n, s: nc.dram_tensor(n, s, F32, kind="ExternalInput").ap()
    xsl = dt_in("xsl", [128, 16 * 2048])
    xr = dt_in("xr", [128, 16 * 1029])
    pT = dt_in("pT", [256, 1024])
    w_in_b = dt_in("w_in_b", [24, 128, 16 * 256])
    w_if = dt_in("w_if", [128, 16 * 8])
    w_out_b = dt_in("w_out_b", [8, 128, 16 * 256])
    w_up_b = dt_in("w_up_b", [44, 128, 16 * 256])
    w_down_b = dt_in("w_down_b", [8, 128, 44 * 256])
    w_pg_b = dt_in("w_pg_b", [8, 128, 16 * 256])
    w_pp = dt_in("w_pp", [256, 2048])
    cst_d = dt_in("cst", [128, NCST])
    outT = nc.dram_tensor("outT", [2048, 1024], F32, kind="ExternalOutput").ap()
    dbg_out = {}
    for name, shape in debug:
        dbg_out[name] = nc.dram_tensor("dbg_" + name, list(shape), F32, kind="ExternalOutput").ap()

    kview = lambda w: w.rearrange("(k p) n -> p k n", p=128)
    w_ppv, pTv, outTv = map(kview, (w_pp, pT, outT))
    blk = lambda wb, cb, kc=16: wb[cb].rearrange("p (k n) -> p k n", k=kc)
    slab_off = {}
    _o = 0
    for (c0_, n_, W_) in ((0, 896, 128), (896, 1152, 256)):
        for s_ in range((n_ + W_ - 1) // W_):
            w_ = min(W_, n_ - W_ * s_)
            slab_off[(c0_, s_)] = (_o, w_)
            _o += 16 * w_

    P = Prog(nc)
    sb = nc.alloc_sbuf_tensor

    cst = sb("cst_sb", [128, NCST], F32)
    cbf = sb("cbf", [128, 256], BF16)
    ones_bf = cbf[:, 0:128]
    ident_bf = cbf[:, 128:256]
    wsl = sb("wsl", [128, NS, 4096], BF16)
    hT = sb("hT", [128, 16, 1152], BF16)
    arena = sb("arena", [128, 16464], F32)
    yTt = sb("yT", [128, 16464], BF16)
    NE = 7296
    regE = sb("regE", [128, NE], F32)

    psA = nc.alloc_psum_tensor("psA", [128, 3, 512], F32)
    psB = nc.alloc_psum_tensor("psB", [128, 3, 512], F32)
    psC = nc.alloc_psum_tensor("psC", [128, 512], F32)
    psD = nc.alloc_psum_tensor("psD", [128, 1024], BF16)
    RpsA = [Res("psA%d" % i) for i in range(3)]
    RpsB = [Res("psB%d" % i) for i in range(3)]
    RpsC = Res("psC")
    RpsD = Res("psD")
    RpsDh = [Res("psDh0"), Res("psDh1")]
    banksA = [(psA[:, i, :], RpsA[i]) for i in range(3)]
    banksB = [(psB[:, i, :], RpsB[i]) for i in range(3)]
    pools = {"tm": banksA + banksB, "fm": [(psA, RpsA), (psB, RpsB)], "tmi": 0, "fmi": 0}

    def set_pools(tm, fm):
        pools["tm"] = tm
        pools["fm"] = fm

    Rcst = Res("cst")
    Rcbf = Res("cbf")
    RhT = [Res("hT%d" % k) for k in range(16)]

    ar_bf = arena[:, :].bitcast(BF16)
    slab = [arena[:, 0:4096].rearrange("p (k n) -> p k n", k=16),
            arena[:, 4096:8192].rearrange("p (k n) -> p k n", k=16)]
    Rslab = [Res("slab0"), Res("slab1")]
    slab_p = [arena[:, 2048 * i:2048 * (i + 1)].rearrange("p (k n) -> p k n", k=16) for i in range(4)]
    Rslab_p = [Res("slabp%d" % i) for i in range(4)]
    k_tm = ar_bf[:, 0:4608].rearrange("p (t n) -> p t n", t=9)
    vw = ar_bf[:, 4608:13824].rearrange("p (t h n) -> p t h n", t=9, h=4)
    kT = ar_bf[:, 13824:18432].rearrange("p (h n) -> p h n", h=4)
    qT = ar_bf[:, 18432:23040].rearrange("p (h n) -> p h n", h=4)
    og = ar_bf[:, 23040:32256].rearrange("p (t n) -> p t n", t=9)
    k_tm_p = ar_bf[:, 18432:22016].rearrange("p (t n) -> p t n", t=7)
    vw_p = ar_bf[:, 22016:29184].rearrange("p (t h n) -> p t h n", t=7, h=4)
    RqT = [Res("qT%d" % h) for h in range(4)]
    RkT = [Res("kT%d" % h) for h in range(4)]
    Rk_tm = [Res("k_tm%d" % t) for t in range(9)]
    Rvw = [Res("vw%d" % t) for t in range(9)]
    Rog = [Res("og%d" % t) for t in range(9)]
    Rk_tm_p = [Res("k_tm_p%d" % t) for t in range(7)]
    Rvw_p = [Res("vw_p%d" % t) for t in range(7)]
    resAB = Rk_tm + Rvw + RkT
    resC = RqT + Rog
    resCp = Rk_tm_p + Rvw_p
    xres = arena[:, 0:16464].rearrange("p (k n) -> p k n", k=16)
    Rxres = [Res("xres%d" % k) for k in range(16)]

    yT = yTt[:, :].rearrange("p (k n) -> p k n", k=16)
    RyT = [Res("yT%d" % k) for k in range(16)]
    hTp = yTt[:, 0:16 * 896].rearrange("p (k n) -> p k n", k=16)
    RhTp = [Res("hTp%d" % k) for k in range(16)]
    aT = yTt[:, 0:16 * 1026].rearrange("p (k n) -> p k n", k=16)
    RaT = [Res("aT%d" % k) for k in range(16)]
    ppw = yTt[:, 0:4096].rearrange("p (k n) -> p k n", k=2)
    Rppw = Res("ppw")
    pTb = yTt[:, 4096:6144].rearrange("p (k n) -> p k n", k=2)
    RpTb = Res("pTb")

    e_bf = regE[:, :].bitcast(BF16)
    E32 = regE[:, 0:1024].rearrange("p (h n) -> p h n", h=4)
    Cbf = e_bf[:, 2048:3072].rearrange("p (h n) -> p h n", h=4)
    Sm = e_bf[:, 3072:3584].rearrange("p (h n) -> p h n", h=4)
    ym = e_bf[:, 3584:4608]
    junk = regE[:, 2304:2560]
    so = [2560]

    def small(n):
        v = regE[:, so[0]:so[0] + n]
        so[0] += n
        return v

    class GateSet:
        def __init__(self, T, tag):
            self.T = T
            r3 = lambda v: v.rearrange("p (t n) -> p t n", t=T)
            self.gates = r3(small(8 * T))
            self.etmp = r3(small(4 * T))
            self.nlf = r3(small(4 * T))
            self.arg = r3(small(4 * T))
            self.wp = r3(small(4 * T))
            self.ebt = r3(small(4 * T))
            self.ebs = r3(small(4 * T))
            o = so[0]
            self.wpb = e_bf[:, 2 * o:2 * o + 4 * T].rearrange("p (t n) -> p t n", t=T)
            so[0] += 2 * T
            mk = lambda n: Res(n + tag)
            self.Rgates, self.Retmp, self.Rnlf, self.Rarg = mk("gates"), mk("etmp"), mk("nlf"), mk("arg")
            self.Rwp, self.Rwpb, self.Rebt, self.Rebs = mk("wp"), mk("wpb"), mk("ebt"), mk("ebs")

        def all_res(self):
            return [self.Rgates, self.Retmp, self.Rnlf, self.Rarg, self.Rwp, self.Rwpb, self.Rebt, self.Rebs]

    Gp = GateSet(7, "_p")
    Gx = GateSet(9, "_x")
    En = small(4)
    d1 = small(4); d2 = small(4); d3 = small(4); rec = small(4); cc = small(4)
    ss = small(4); t1 = small(4); t2 = small(4); lnr = small(4); rr = small(4); rowsc = small(4)
    nbf = e_bf[:, 2 * so[0]:2 * so[0] + 4]; so[0] += 2
    rs_col = small(8)
    Rrs_col = Res("rs_col")
    rs = small(512)
    assert so[0] <= 4100
    RE32 = [Res("E32_%d" % h) for h in range(4)]
    RCbf = [Res("Cbf_%d" % h) for h in range(4)]
    RSm = Res("Sm"); Rym = Res("ym"); Rjunk = Res("junk")
    REn = Res("En"); Rnbf = Res("nbf")
    Rsm = {n: Res(n) for n in "d1 d2 d3 rec cc ss t1 t2 lnr rr rowsc".split()}
    Rrs = Res("rs")
    mlstm_small_res = RE32 + RCbf + [RSm, Rym, Rjunk, REn, Rnbf, Rrs] + list(Rsm.values()) + Gp.all_res() + Gx.all_res()
    gcs = [regE[:, 4100:5135].rearrange("p (s n) -> p s n", s=3), regE[:, 5135:6170].rearrange("p (s n) -> p s n", s=3)]
    cacc = regE[:, 6170:7199].rearrange("p (s n) -> p s n", s=3)
    Rgcs = [Res("gcs0"), Res("gcs1")]
    Rcacc = Res("cacc")
    conv_res = Rgcs + [Rcacc]
    sqd = e_bf[:, 8200:12296].rearrange("p (k n) -> p k n", k=16)
    Rsqd = Res("sqd")
    upg = [regE[:, 0:1026].rearrange("p (s n) -> p s n", s=3), regE[:, 1026:2052].rearrange("p (s n) -> p s n", s=3)]
    upu = [regE[:, 2052:3078].rearrange("p (s n) -> p s n", s=3), regE[:, 3078:4104].rearrange("p (s n) -> p s n", s=3)]
    Rupg = [Res("upg0"), Res("upg1")]
    Rupu = [Res("upu0"), Res("upu1")]
    rs2 = regE[:, 4140:5169]
    acc = regE[:, 5169:6198]
    tmpq = regE[:, 6198:7227]
    Rrs2 = Res("rs2"); Racc = Res("acc"); Rtmpq = Res("tmpq")
    sgt = [regE[:, 0:512], regE[:, 512:1024]]
    ppt = [regE[:, 1024:1536], regE[:, 1536:2048]]
    Rsgt = [Res("sgt0"), Res("sgt1")]
    Rppt = [Res("ppt0"), Res("ppt1")]

    cs = lambda off, n=1: cst[:, off:off + n]
    triM = cst[:, C_TRI:C_TRI + 128]
    onesF = cst[:, C_ONE:C_ONE + 128]

    Rslot = [Res("slot%d" % i) for i in range(NS)]
    wq = []
    wstate = {"issued": 0}

    def wget(i):
        while wstate["issued"] < min(i + NS, len(wq)):
            j = wstate["issued"]
            src, kc, ncol = wq[j]
            dst = wsl[:, j % NS, 0:kc * ncol].rearrange("p (k n) -> p k n", k=kc)
            o = P.op("pool", lambda e, dst=dst, src=src: e.dma_start(out=dst, in_=src),
                     writes=[Rslot[j % NS]], dma=True)
            if j < NS and wstate.get("gate") is not None:
                g = wstate["gate"]
                g.signal = True
                o.deps.append(g)
            wstate["issued"] += 1
        src, kc, ncol = wq[i]
        return wsl[:, i % NS, 0:kc * ncol].rearrange("p (k n) -> p k n", k=kc), Rslot[i % NS]

    tasks = []

    def task(src, kc, ncol, fn):
        tasks.append(("w", (src, kc, ncol), fn))

    def call(fn):
        tasks.append(("c", None, fn))

    def run_tasks():
        for t in tasks:
            if t[0] == "w":
                wq.append(t[1])
        n = 0
        for t in tasks:
            if t[0] == "w":
                slot, rs_ = wget(n)
                n += 1
                t[2](slot, rs_)
            else:
                t[2]()

    def tap(name, ap, reads):
        if name in dbg_out:
            P.op("pool", lambda e: e.dma_start(out=dbg_out[name], in_=ap), reads=reads, writes=[Res("dbg")], dma=True)

    def rstd_from_psum(ps_ap, out_rs, Rps, Rout, div):
        P.op("act", lambda e: e.activation(out=out_rs, in_=ps_ap, func=AF.Ln, bias=EPS, scale=1.0 / div),
             reads=[Rps], writes=[Rout])
        P.op("act", lambda e: e.activation(out=out_rs, in_=out_rs, func=AF.Exp, scale=-0.5),
             reads=[Rout], writes=[Rout])

    def norm_slab(c0, s, n, goff, dst, Rdst, W=256):
        w = min(W, n - s * W)
        if W == 256:
            sl = slab[s % 2][:, :, 0:w]
            Rs = Rslab[s % 2]
        else:
            sl = slab_p[s % 4][:, :, 0:w]
            Rs = Rslab_p[s % 4]
        so_, _w = slab_off[(c0, s)]
        assert _w == w
        o = P.op("sp", lambda e: e.dma_start(out=sl, in_=xsl[:, so_:so_ + 16 * w].rearrange("p (k n) -> p k n", k=16)),
                 writes=[Rs], dma=True)
        if c0 == 0 and s == 3:
            wstate["gate"] = o
        sq = sqd[:, :, 0:w]
        P.op("act", lambda e: e.activation(out=sq, in_=sl, func=AF.Square), reads=[Rs], writes=[Rsqd])
        for k in range(16):
            P.op("pe", lambda e, k=k: e.matmul(psC[:, 0:w], lhsT=ones_bf, rhs=sqd[:, k, 0:w], start=(k == 0), stop=(k == 15)),
                 reads=[Rsqd, Rcbf], writes=[RpsC])
        rstd_from_psum(psC[:, 0:w], rs[:, 0:w], RpsC, Rrs, 2048.0)
        for k in range(16):
            if True:
                P.op("dve", lambda e, k=k: e.scalar_tensor_tensor(
                    out=dst[:, k, s * W:s * W + w], in0=sl[:, k, :], scalar=cs(goff + k), in1=rs[:, 0:w],
                    op0=ALU.mult, op1=ALU.mult), reads=[Rs, Rrs, Rcst], writes=[Rdst[k]])
            else:
                Rk = Res("slabk")
                P.op("pool", lambda e, k=k: e.tensor_scalar(out=sl[:, k, :], in0=sl[:, k, :], scalar1=cs(goff + k), scalar2=None, op0=ALU.mult),
                     reads=[Rs, Rcst], writes=[Rk])
                P.op("pool", lambda e, k=k: e.tensor_tensor(out=dst[:, k, s * 256:s * 256 + w], in0=sl[:, k, :], in1=rs[:, 0:w], op=ALU.mult),
                     reads=[Rs, Rk, Rrs], writes=[Rdst[k]])

    def prefix_slab(s):
        sl = slab_p[s % 4]
        Rs = Rslab_p[s % 4]
        so_, _w = slab_off[(0, s)]
        o = P.op("sp", lambda e: e.dma_start(out=sl, in_=xsl[:, so_:so_ + 16 * 128].rearrange("p (k n) -> p k n", k=16)),
                 writes=[Rs], dma=True)
        if s == 3:
            wstate["gate"] = o
        P.op("act", lambda e: e.activation(out=sqd[:, :, 0:128], in_=sl, func=AF.Square), reads=[Rs], writes=[Rsqd])
        for k in range(16):
            P.op("pe", lambda e, k=k: e.matmul(psC[:, 64 + s:65 + s], lhsT=sqd[:, k, 0:128], rhs=ones_bf[:, 0:1], start=(k == 0), stop=(k == 15)),
                 reads=[Rsqd, Rcbf], writes=[RpsC])
        P.op("dve", lambda e: e.tensor_tensor(out=hTp[:, :, 128 * s:128 * (s + 1)], in0=sl,
                                              in1=cst[:, C_G1:C_G1 + 16].unsqueeze(2).broadcast_to([128, 16, 128]), op=ALU.mult),
             reads=[Rs, Rcst], writes=RhTp)

    def prefix_rs():
        P.op("act", lambda e: e.activation(out=rs_col[:, 0:7], in_=psC[:, 64:71], func=AF.Ln, bias=EPS, scale=1.0 / 2048.0),
             reads=[RpsC], writes=[Rrs_col])
        P.op("act", lambda e: e.activation(out=rs_col[:, 0:7], in_=rs_col[:, 0:7], func=AF.Exp, scale=-0.5),
             reads=[Rrs_col], writes=[Rrs_col])

    def sumsq_hook(f, col0, w, first):
        if first:
            P.op("act", lambda e: e.activation(out=acc[:, 0:w], in_=xres[:, f, col0:col0 + w], func=AF.Square),
                 reads=[Rxres[f]], writes=[Racc])
        else:
            P.op("act", lambda e: e.activation(out=tmpq[:, 0:w], in_=xres[:, f, col0:col0 + w], func=AF.Square),
                 reads=[Rxres[f]], writes=[Rtmpq])
            P.op("dve", lambda e: e.tensor_tensor(out=acc[:, 0:w], in0=acc[:, 0:w], in1=tmpq[:, 0:w], op=ALU.add),
                 reads=[Racc, Rtmpq], writes=[Racc])

    def norm_finish_sliced(col0, slices, goff, dst, Rdst2, Rrs_list):
        for si, (s0, w) in enumerate(slices):
            P.op("pe", lambda e, s0=s0, w=w: e.matmul(psC[:, 0:w], lhsT=onesF, rhs=acc[:, s0:s0 + w], start=True, stop=True),
                 reads=[Racc, Rcst], writes=[RpsC])
            rstd_from_psum(psC[:, 0:w], rs2[:, s0:s0 + w], RpsC, Rrs_list[si], 2048.0)
            for k in range(16):
                P.op("dve", lambda e, k=k, s0=s0, w=w: e.scalar_tensor_tensor(
                    out=dst[:, k, s0:s0 + w], in0=xres[:, k, col0 + s0:col0 + s0 + w], scalar=cs(goff + k), in1=rs2[:, s0:s0 + w],
                    op0=ALU.mult, op1=ALU.mult), reads=[Rxres[k], Rrs_list[si], Rcst], writes=[Rdst2[k][si]])

    def norm_finish(col0, W, slices, goff, out_fn, out_res_fn, post=None):
        for (s0, w) in slices:
            P.op("pe", lambda e, s0=s0, w=w: e.matmul(psC[:, 0:w], lhsT=onesF, rhs=acc[:, s0:s0 + w], start=True, stop=True),
                 reads=[Racc, Rcst], writes=[RpsC])
            rstd_from_psum(psC[:, 0:w], rs2[:, s0:s0 + w], RpsC, Rrs2, 2048.0)
        for k in range(16):
            if True:
                P.op("dve", lambda e, k=k: e.scalar_tensor_tensor(
                    out=out_fn(k), in0=xres[:, k, col0:col0 + W], scalar=cs(goff + k), in1=rs2[:, 0:W],
                    op0=ALU.mult, op1=ALU.mult), reads=[Rxres[k], Rrs2, Rcst], writes=[out_res_fn(k)])
            else:
                P.op("pool", lambda e, k=k: e.tensor_scalar(out=tmpq[:, 0:W], in0=xres[:, k, col0:col0 + W], scalar1=cs(goff + k),
                                                            scalar2=None, op0=ALU.mult), reads=[Rxres[k], Rcst], writes=[Rtmpq])
                P.op("pool", lambda e, k=k: e.tensor_tensor(out=out_fn(k), in0=tmpq[:, 0:W], in1=rs2[:, 0:W], op=ALU.mult),
                     reads=[Rtmpq, Rrs2], writes=[out_res_fn(k)])
            if post is not None:
                post(k)

    def next_bank():
        tm = pools["tm"]
        b = tm[pools["tmi"] % len(tm)]
        pools["tmi"] += 1
        return b

    def next_group():
        fm = pools["fm"]
        g = fm[pools["fmi"] % len(fm)]
        pools["fmi"] += 1
        return g

    def tok_major_proj(slot, Rs, src, Rsrc, tile_list, ncol, evac):
        for t in tile_list:
            ps, Rp = next_bank()
            for k in range(16):
                P.op("pe", lambda e, k=k, t=t, ps=ps: e.matmul(ps[:, 0:ncol], lhsT=src[:, k, t * 128:(t + 1) * 128],
                                                                rhs=slot[:, k, 0:ncol], start=(k == 0), stop=(k == 15)),
                     reads=[Rsrc[k], Rs], writes=[Rp])
            evac(t, ps[:, 0:ncol], Rp)

    def feat_major_proj(slot, Rs, j, src, Rsrc, kc, col_slices, Rsrc_fn=None, k_outer=False):
        ps, Rb = next_group()
        if k_outer:
            order = [(si, k) for k in range(kc) for si in range(len(col_slices))]
        else:
            order = [(si, k) for si in range(len(col_slices)) for k in range(kc)]
        for si, k in order:
            c0, w = col_slices[si]
            rd = [Rs] + (Rsrc_fn(k, si) if Rsrc_fn is not None else [Rsrc[k]])
            P.op("pe", lambda e, k=k, si=si, c0=c0, w=w: e.matmul(
                ps[:, si, 0:w], lhsT=slot[:, k, j * 128:(j + 1) * 128], rhs=src[:, k, c0:c0 + w],
                start=(k == 0), stop=(k == kc - 1)),
                 reads=rd, writes=[Rb[si]])
        return ps, Rb

    def gate_math(G):
        T = G.T
        P.op("act", lambda e: e.activation(out=G.etmp, in_=G.gates[:, :, 4:8], func=AF.Exp, scale=-1.0),
             reads=[G.Rgates], writes=[G.Retmp])
        P.op("act", lambda e: e.activation(out=G.nlf, in_=G.etmp, func=AF.Ln, bias=1.0),
             reads=[G.Retmp], writes=[G.Rnlf])
        nlf_all = G.nlf.rearrange("p t n -> p (t n)")
        P.op("pe", lambda e: e.matmul(psC[:, 0:4 * T], lhsT=triM, rhs=nlf_all, start=True, stop=True),
             reads=[G.Rnlf, Rcst], writes=[RpsC])
        P.op("pe", lambda e: e.matmul(psC[:, 4 * T:8 * T], lhsT=onesF, rhs=nlf_all, start=True, stop=True),
             reads=[G.Rnlf, Rcst], writes=[RpsC])
        pcum = psC[:, 0:4 * T].rearrange("p (t n) -> p t n", t=T)
        ptot = psC[:, 4 * T:8 * T].rearrange("p (t n) -> p t n", t=T)
        P.op("dve", lambda e: e.tensor_tensor(out=G.arg, in0=pcum, in1=G.gates[:, :, 0:4], op=ALU.add),
             reads=[RpsC, G.Rgates], writes=[G.Rarg])
        P.op("act", lambda e: e.activation(out=G.wp, in_=G.arg, func=AF.Exp), reads=[G.Rarg], writes=[G.Rwp])
        P.op("dve", lambda e: e.tensor_copy(out=G.wpb, in_=G.wp), reads=[G.Rwp], writes=[G.Rwpb])
        P.op("act", lambda e: e.activation(out=G.ebt, in_=ptot, func=AF.Exp, scale=-1.0),
             reads=[RpsC], writes=[G.Rebt])
        P.op("act", lambda e: e.activation(out=G.ebs, in_=pcum, func=AF.Exp, scale=-1.0),
             reads=[RpsC], writes=[G.Rebs])

    def if_task(G, src, Rsrc, rsc=None):
        T = G.T

        def consume(slot, Rs):
            for t in range(T):
                for k in range(16):
                    P.op("pe", lambda e, k=k, t=t: e.matmul(psC[:, t * 8:t * 8 + 8], lhsT=src[:, k, t * 128:(t + 1) * 128],
                                                             rhs=slot[:, k, 0:8], start=(k == 0), stop=(k == 15)),
                         reads=[Rsrc[k], Rs], writes=[RpsC])
            pc = psC[:, 0:8 * T].rearrange("p (t n) -> p t n", t=T)
            bias_b = cst[:, C_BIF:C_BIF + 8].unsqueeze(1).broadcast_to([128, T, 8])
            if rsc is None:
                P.op("dve", lambda e: e.tensor_tensor(out=G.gates, in0=pc, in1=bias_b, op=ALU.add),
                     reads=[RpsC, Rcst], writes=[G.Rgates])
            else:
                P.op("dve", lambda e: e.tensor_tensor(out=G.gates, in0=pc, in1=rsc[:, 0:T].unsqueeze(2).broadcast_to([128, T, 8]), op=ALU.mult),
                     reads=[RpsC, Rrs_col], writes=[G.Rgates])
                P.op("dve", lambda e: e.tensor_tensor(out=G.gates, in0=G.gates, in1=bias_b, op=ALU.add),
                     reads=[G.Rgates, Rcst], writes=[G.Rgates])
            gate_math(G)
        task(w_if[:, :].rearrange("p (k n) -> p k n", k=16), 16, 8, consume)

    def v_task(G, src, Rsrc, vwb, Rvwb, h, rsc=None):
        def consume(slot, Rs):
            def evac(t, ps, Rp):
                if rsc is None:
                    P.op("dve", lambda e: e.tensor_scalar(out=vwb[:, t, h, :], in0=ps, scalar1=G.wp[:, t, h:h + 1], scalar2=None,
                                                          op0=ALU.mult), reads=[Rp, G.Rwp], writes=[Rvwb[t]])
                else:
                    P.op("dve", lambda e: e.tensor_scalar(out=vwb[:, t, h, :], in0=ps, scalar1=rsc[:, t:t + 1], scalar2=G.wp[:, t, h:h + 1],
                                                          op0=ALU.mult, op1=ALU.mult), reads=[Rp, G.Rwp, Rrs_col], writes=[Rvwb[t]])
            tok_major_proj(slot, Rs, src, Rsrc, range(G.T), 256, evac)
        task(blk(w_in_b, 16 + h), 16, 256, consume)

    qk_slices = [(0, 384), (384, 384), (768, 384)]

    def k_task(G, src, Rsrc, kb, Rkb, j, with_kT, rsc=None):
        def consume(slot, Rs):
            def evac(t, ps, Rp):
                if rsc is None:
                    P.op("act", lambda e: e.activation(out=kb[:, t, 256 * j:256 * (j + 1)], in_=ps, func=AF.Copy),
                         reads=[Rp], writes=[Rkb[t]])
                else:
                    P.op("act", lambda e: e.activation(out=kb[:, t, 256 * j:256 * (j + 1)], in_=ps, func=AF.Identity, scale=rsc[:, t:t + 1]),
                         reads=[Rp, Rrs_col], writes=[Rkb[t]])
            tok_major_proj(slot, Rs, src, Rsrc, range(G.T), 256, evac)
            if with_kT and j == 1:
                tb = [(psD[:, 0:512], RpsD), (psC[:, :].bitcast(BF16)[:, 0:512], RpsC)]
                for t in range(G.T):
                    pt, Rpt = tb[t % 2]
                    for h in range(4):
                        P.op("pe", lambda e, t=t, h=h, pt=pt: e.transpose(pt[:, h * 128:(h + 1) * 128], kb[:, t, h * 128:(h + 1) * 128], ident_bf),
                             reads=[Rkb[t], Rcbf], writes=[Rpt])
                    P.op("act", lambda e, t=t, pt=pt: e.activation(out=kT[:, :, t * 128:(t + 1) * 128],
                                                                   in_=pt.rearrange("p (h n) -> p h n", h=4), func=AF.Copy),
                         reads=[Rpt], writes=RkT)
        task(blk(w_in_b, 14 + j), 16, 256, consume)

    def q_task(j):
        def consume(slot, Rs):
            for jj in range(2):
                h = 2 * j + jj
                ps, Rb = feat_major_proj(slot, Rs, jj, hT, RhT, 16, qk_slices)
                P.op("act", lambda e, ps=ps, h=h: e.activation(out=qT[:, h, :].rearrange("p (s n) -> p s n", s=3),
                                                                in_=ps[:, :, 0:384], func=AF.Identity, scale=float(128 ** -0.5)),
                     reads=Rb, writes=[RqT[h]])
        task(blk(w_in_b, 12 + j), 16, 256, consume)

    def o_task(j):
        def consume(slot, Rs):
            def evac(t, ps, Rp):
                P.op("act", lambda e: e.activation(out=junk[:, 0:256], in_=ps, func=AF.Sigmoid), reads=[Rp], writes=[Rjunk])
                P.op("dve", lambda e: e.tensor_tensor(out=og[:, t, 256 * j:256 * (j + 1)], in0=junk[:, 0:256],
                                                      in1=cst[:, C_GMH + 256 * j:C_GMH + 256 * (j + 1)], op=ALU.mult),
                     reads=[Rjunk, Rcst], writes=[Rog[t]])
            tok_major_proj(slot, Rs, hT, RhT, range(9), 256, evac)
        task(blk(w_in_b, 20 + j), 16, 256, consume)

    psNum = [psA[:, 0, :].rearrange("p (h n) -> p h n", h=2), psA[:, 1, :].rearrange("p (h n) -> p h n", h=2)]
    psDen = psA[:, 2, 0:4]
    psNU = psA[:, 2, 4:8]
    mstate = {"first": True}

    def prev_ebt(G, T):
        if T > 0:
            return G.ebt[:, T - 1, :], G.Rebt
        return Gp.ebt[:, 6, :], Gp.Rebt

    def mlstm_part1(G, T):
        cols = slice(T * 128, (T + 1) * 128)
        for h in range(4):
            P.op("pe", lambda e, h=h: e.matmul(psC[:, h * 128:(h + 1) * 128], lhsT=kT[:, h, cols], rhs=qT[:, h, cols],
                                               start=True, stop=True), reads=[RkT[h], RqT[h]], writes=[RpsC])
        P.op("dve", lambda e: e.tensor_tensor(out=Sm[:, :, :], in0=psC[:, 0:512].rearrange("p (h n) -> p h n", h=4),
                                              in1=triM.unsqueeze(1).broadcast_to([128, 4, 128]), op=ALU.mult),
             reads=[RpsC, Rcst], writes=[RSm])
        for h in range(4):
            pn = psNum[h // 2][:, h % 2, :]
            P.op("pe", lambda e, h=h, pn=pn: e.matmul(pn, lhsT=Sm[:, h, :], rhs=vw[:, T, h, :], start=True, stop=False),
                 reads=[RSm, Rvw[T]], writes=[RpsA[h // 2]])
            P.op("pe", lambda e, h=h, pn=pn: e.matmul(pn, lhsT=qT[:, h, cols], rhs=Cbf[:, h, :], start=False, stop=True),
                 reads=[RqT[h], RCbf[h]], writes=[RpsA[h // 2]])
            P.op("pe", lambda e, h=h: e.matmul(psDen[:, h:h + 1], lhsT=Sm[:, h, :], rhs=G.wpb[:, T, h:h + 1], start=True, stop=False),
                 reads=[RSm, G.Rwpb], writes=[RpsA[2]])
            P.op("pe", lambda e, h=h: e.matmul(psDen[:, h:h + 1], lhsT=qT[:, h, cols], rhs=nbf[:, h:h + 1], start=False, stop=True),
                 reads=[RqT[h], Rnbf], writes=[RpsA[2]])
        S = Rsm
        P.op("dve", lambda e: e.tensor_tensor(out=d1, in0=psDen, in1=G.ebs[:, T, :], op=ALU.mult), reads=[RpsA[2], G.Rebs], writes=[S["d1"]])
        P.op("dve", lambda e: e.scalar_tensor_tensor(out=d2, in0=d1, scalar=-1.0, in1=d1, op0=ALU.mult, op1=ALU.max),
             reads=[S["d1"]], writes=[S["d2"]])
        P.op("dve", lambda e: e.tensor_scalar(out=d3, in0=d2, scalar1=1.0, scalar2=None, op0=ALU.max), reads=[S["d2"]], writes=[S["d3"]])
        P.op("dve", lambda e: e.reciprocal(out=rec, in_=d3), reads=[S["d3"]], writes=[S["rec"]])
        P.op("dve", lambda e: e.tensor_tensor(out=cc, in0=rec, in1=G.ebs[:, T, :], op=ALU.mult), reads=[S["rec"], G.Rebs], writes=[S["cc"]])
        P.op("dve", lambda e: e.memset(ss, 0.0), writes=[S["ss"]])
        for h in range(4):
            pn = psNum[h // 2][:, h % 2, :]
            P.op("act", lambda e, h=h, pn=pn: e.activation(out=junk[:, 0:256], in_=pn, func=AF.Square, scale=cc[:, h:h + 1],
                                                           accum_out=ss[:, h:h + 1]),
                 reads=[RpsA[h // 2], S["ss"], S["cc"]], writes=[Rjunk, S["ss"]])
        P.op("act", lambda e: e.activation(out=lnr, in_=ss, func=AF.Ln, bias=EPS, scale=1.0 / 256.0), reads=[S["ss"]], writes=[S["lnr"]])
        P.op("act", lambda e: e.activation(out=rr, in_=lnr, func=AF.Exp, scale=-0.5), reads=[S["lnr"]], writes=[S["rr"]])
        P.op("dve", lambda e: e.tensor_tensor(out=rowsc, in0=cc, in1=rr, op=ALU.mult), reads=[S["cc"], S["rr"]], writes=[S["rowsc"]])
        for h in range(4):
            pn = psNum[h // 2][:, h % 2, :]
            P.op("dve", lambda e, h=h, pn=pn: e.scalar_tensor_tensor(
                out=ym[:, h * 256:(h + 1) * 256], in0=pn, scalar=rowsc[:, h:h + 1], in1=og[:, T, h * 256:(h + 1) * 256],
                op0=ALU.mult, op1=ALU.mult), reads=[RpsA[h // 2], S["rowsc"], Rog[T]], writes=[Rym])

    def mlstm_transposes(T):
        for j in range(8):
            P.op("pe", lambda e, j=j: e.transpose(psD[:, j * 128:(j + 1) * 128], ym[:, j * 128:(j + 1) * 128], ident_bf),
                 reads=[Rym, Rcbf], writes=[RpsD])
        if T == 0:
            i0, n, src0 = 0, 5, 123
        else:
            i0, n, src0 = 5 + 128 * (T - 1), 128, 0
        P.op("act", lambda e: e.activation(out=yT[:, 8:16, i0:i0 + n],
                                           in_=psD[:, :].rearrange("p (j n) -> p j n", j=8)[:, :, src0:src0 + n], func=AF.Copy),
             reads=[RpsD], writes=RyT[8:16])

    def mlstm_update(G, T, kb, Rkb, vwb, Rvwb, need_c):
        first = mstate["first"]
        for h in range(4):
            pu = psNum[h // 2][:, h % 2, :]
            P.op("pe", lambda e, h=h, pu=pu: e.matmul(pu, lhsT=kb[:, T, h * 128:(h + 1) * 128], rhs=vwb[:, T, h, :], start=True, stop=True),
                 reads=[Rkb[T], Rvwb[T]], writes=[RpsA[h // 2]])
            P.op("pe", lambda e, h=h: e.matmul(psNU[:, h:h + 1], lhsT=kb[:, T, h * 128:(h + 1) * 128], rhs=G.wpb[:, T, h:h + 1], start=True, stop=True),
                 reads=[Rkb[T], G.Rwpb], writes=[RpsA[2]])
        pe4, Rpe = prev_ebt(G, T)
        for h in range(4):
            pu = psNum[h // 2][:, h % 2, :]
            if first:
                P.op("dve", lambda e, h=h, pu=pu: e.tensor_copy(out=E32[:, h, :], in_=pu), reads=[RpsA[h // 2]], writes=[RE32[h]])
            else:
                P.op("dve", lambda e, h=h, pu=pu: e.scalar_tensor_tensor(out=E32[:, h, :], in0=E32[:, h, :], scalar=pe4[:, h:h + 1], in1=pu,
                                                                           op0=ALU.mult, op1=ALU.add),
                     reads=[RE32[h], RpsA[h // 2], Rpe], writes=[RE32[h]])
            if need_c:
                P.op("act", lambda e, h=h: e.activation(out=Cbf[:, h, :], in_=E32[:, h, :], func=AF.Identity, scale=G.ebt[:, T, h:h + 1]),
                     reads=[RE32[h], G.Rebt], writes=[RCbf[h]])
        if first:
            P.op("dve", lambda e: e.tensor_copy(out=En, in_=psNU), reads=[RpsA[2]], writes=[REn])
        else:
            P.op("dve", lambda e: e.tensor_tensor(out=En, in0=En, in1=pe4, op=ALU.mult), reads=[REn, Rpe], writes=[REn])
            P.op("dve", lambda e: e.tensor_tensor(out=En, in0=En, in1=psNU, op=ALU.add), reads=[REn, RpsA[2]], writes=[REn])
        if need_c:
            P.op("dve", lambda e: e.tensor_tensor(out=nbf, in0=En, in1=G.ebt[:, T, :], op=ALU.mult), reads=[REn, G.Rebt], writes=[Rnbf])
        mstate["first"] = False

    cg_slices = [(121 + 343 * s, 345) for s in range(3)]

    def conv_chunk(kind, b, j, slot, Rs):
        ps, Rb = feat_major_proj(slot, Rs, j, hT, RhT, 16, cg_slices)
        if kind == "gc":
            for si in range(3):
                P.op("act", lambda e, si=si: e.activation(out=gcs[j][:, si, :], in_=ps[:, si, 0:345], func=AF.Copy),
                     reads=[Rb[si]], writes=[Rgcs[j]])
        elif kind == "u":
            for si in range(3):
                P.op("dve", lambda e, si=si: e.tensor_tensor(out=gcs[j][:, si, :], in0=ps[:, si, 0:345], in1=gcs[j][:, si, :], op=ALU.mult),
                     reads=[Rb[si], Rgcs[j]], writes=[Rgcs[j]])
        else:
            fc = 2 * b + j
            z = gcs[j]
            P.op("act", lambda e: e.activation(out=cacc[:, :, :], in_=z[:, :, 0:343], func=AF.Identity,
                                               scale=cs(C_SCW + 3 * fc + 0)), reads=[Rgcs[j], Rcst], writes=[Rcacc])
            P.op("dve", lambda e: e.scalar_tensor_tensor(out=cacc[:, :, :], in0=z[:, :, 1:344], scalar=cs(C_SCW + 3 * fc + 1),
                                                         in1=cacc[:, :, :], op0=ALU.mult, op1=ALU.add),
                 reads=[Rgcs[j], Rcacc, Rcst], writes=[Rcacc])
            P.op("dve", lambda e: e.scalar_tensor_tensor(out=cacc[:, :, :], in0=z[:, :, 2:345], scalar=cs(C_SCW + 3 * fc + 2),
                                                         in1=cacc[:, :, :], op0=ALU.mult, op1=ALU.add),
                 reads=[Rgcs[j], Rcacc, Rcst], writes=[Rcacc])
            for si in range(3):
                P.op("dve", lambda e, si=si: e.tensor_tensor(out=yT[:, fc, 343 * si:343 * (si + 1)],
                                                             in0=ps[:, si, 2:345], in1=cacc[:, si, :], op=ALU.mult),
                     reads=[Rb[si], Rcacc], writes=[RyT[fc]])

    def conv_task(kind, b, split):
        st = {}
        col = {"gc": 4, "u": 8, "gb": 0}[kind] + b

        def consume(slot, Rs):
            st["slot"], st["Rs"] = slot, Rs
            conv_chunk(kind, b, 0, slot, Rs)
            if not split:
                conv_chunk(kind, b, 1, slot, Rs)
        task(blk(w_in_b, col), 16, 256, consume)
        if split:
            return lambda: conv_chunk(kind, b, 1, st["slot"], st["Rs"])
        return None

    def setup():
        P.op("sp", lambda e: e.dma_start(out=cst[:, :], in_=cst_d[:, :]), writes=[Rcst], dma=True)
        P.op("dve", lambda e: e.tensor_copy(out=cbf[:, 0:128], in_=cst[:, C_ONE:C_ONE + 128]), reads=[Rcst], writes=[Rcbf])
        P.op("dve", lambda e: e.tensor_copy(out=cbf[:, 128:256], in_=cst[:, C_IDN:C_IDN + 128]), reads=[Rcst], writes=[Rcbf])
    call(setup)
    for s in range(7):
        call(lambda s=s: prefix_slab(s))
    call(prefix_rs)

    def ext_norm_alias():
        for r in Rslab:
            P.alias(r, Rslab_p)
    call(ext_norm_alias)
    ext_slabs = [lambda s=s: norm_slab(896, s, 1152, C_G1, hT, RhT) for s in range(5)]
    if_task(Gp, hTp, RhTp, rs_col)
    for h in range(4):
        v_task(Gp, hTp, RhTp, vw_p, Rvw_p, h, rs_col)
        call(ext_slabs[h])
    k_task(Gp, hTp, RhTp, k_tm_p, Rk_tm_p, 0, False, rs_col)
    call(ext_slabs[4])
    k_task(Gp, hTp, RhTp, k_tm_p, Rk_tm_p, 1, False, rs_col)
    call(lambda: tap("hT", hT[:, 0, 0:1152], RhT))

    def phase_c_begin():
        for r in resAB:
            P.alias(r, Rslab)
        set_pools(banksB, [(psB, RpsB)])
    call(phase_c_begin)
    sweep = [lambda T=T: mlstm_update(Gp, T, k_tm_p, Rk_tm_p, vw_p, Rvw_p, T == 6) for T in range(7)]
    call(sweep[0])
    if_task(Gx, hT, RhT)
    for h in range(4):
        call(sweep[1 + h])
        v_task(Gx, hT, RhT, vw, Rvw, h)
    for j in range(2):
        call(sweep[5 + j])
        k_task(Gx, hT, RhT, k_tm, Rk_tm, j, True)

    def phase_d_begin():
        for r in resC:
            P.alias(r, resCp)
        set_pools(banksA + banksB, [(psA, RpsA), (psB, RpsB)])
    call(phase_d_begin)
    for j in range(2):
        q_task(j)
    for j in range(4):
        o_task(j)

    def phase_e_begin():
        for r in conv_res:
            P.alias(r, [Rsqd])
        for r in RyT:
            P.alias(r, RhTp)
        set_pools(banksB, [(psB, RpsB)])
    call(phase_e_begin)
    conv_list = []
    for b in range(4):
        conv_list += [("gc", b), ("u", b), ("gb", b)]
    ci = 0
    for T in range(9):
        call(lambda T=T: mlstm_part1(Gx, T))
        second = conv_task(conv_list[ci][0], conv_list[ci][1], True); ci += 1
        if T < 8:
            call(lambda T=T: mlstm_update(Gx, T, k_tm, Rk_tm, vw, Rvw, True))
        call(lambda T=T: mlstm_transposes(T))
        call(second)
        if T < 3:
            conv_task(conv_list[ci][0], conv_list[ci][1], False); ci += 1
    assert ci == 12
    call(lambda: tap("yT", yTt[:, 0:16464], RyT))

    def xreload():
        set_pools(banksA + banksB, [(psA, RpsA), (psB, RpsB)])
        for k in range(16):
            P.alias(Rxres[k], resAB + resC + Rslab)
        for k4 in range(4):
            P.op("sp", lambda e, k4=k4: e.dma_start(out=xres[:, 4 * k4:4 * k4 + 4, :],
                                                    in_=xr[:, 4116 * k4:4116 * (k4 + 1)].rearrange("p (k n) -> p k n", k=4)),
                 writes=Rxres[4 * k4:4 * k4 + 4], dma=True)
        for r in (Racc, Rtmpq, Rrs2):
            P.alias(r, conv_res + mlstm_small_res)
    call(xreload)
    wo_slices = [(343 * s, 343) for s in range(3)]
    n2_slices = [(0, 345), (345, 342), (687, 342)]
    RhT2 = [[Res("hT2_%d_%d" % (k, si)) for si in range(3)] for k in range(16)]
    RhT3 = [[Res("hT3_%d_%d" % (k, si)) for si in range(2)] for k in range(16)]
    Rrs2_3 = [Res("rs2_%d" % si) for si in range(3)]
    ff_src = lambda k, si: [RhT2[k][si]] + ([RhT2[k][si - 1]] if si > 0 else [])

    def wout_task(b):
        def consume(slot, Rs):
            for j in range(2):
                f = 2 * b + j
                ps, Rb = feat_major_proj(slot, Rs, j, yT, RyT, 16, wo_slices, k_outer=(b == 0 and j == 0))
                P.op("dve", lambda e, ps=ps, f=f: e.tensor_tensor(out=xres[:, f, :].rearrange("p (s n) -> p s n", s=3),
                                                                  in0=ps[:, :, 0:343], in1=xres[:, f, :].rearrange("p (s n) -> p s n", s=3),
                                                                  op=ALU.add), reads=Rb + [Rxres[f]], writes=[Rxres[f]])
                sumsq_hook(f, 0, 1029, f == 0)
        task(blk(w_out_b, b), 16, 256, consume)
    for b in range(8):
        wout_task(b)

    def norm2():
        tap("x1", arena[:, 0:16464], Rxres)
        for k in range(16):
            for si in range(3):
                P.alias(RhT2[k][si], [RhT[k]])
        for si in range(3):
            P.alias(Rrs2_3[si], [Rrs2])
        norm_finish_sliced(0, n2_slices, C_G2, hT, RhT2, Rrs2_3)
        for r in RaT:
            P.alias(r, RyT)
        for r in Rupg + Rupu:
            P.alias(r, conv_res + mlstm_small_res)
    call(norm2)

    ff_slices = [(1 + 342 * s, 344) for s in range(3)]
    groups = [(0, 16), (16, 16), (32, 12)]

    def conv_evac(ps, Rb, dst, Rdst, cidx):
        w0, w1, w2 = (cs(C_FCW + 3 * cidx + i) for i in range(3))
        P.op("act", lambda e: e.activation(out=dst[:, :, :], in_=ps[:, :, 2:344], func=AF.Identity, bias=cs(C_FCB + cidx), scale=w2),
             reads=Rb + [Rcst], writes=[Rdst])
        P.op("dve", lambda e: e.scalar_tensor_tensor(out=dst[:, :, :], in0=ps[:, :, 1:343], scalar=w1, in1=dst[:, :, :], op0=ALU.mult, op1=ALU.add),
             reads=Rb + [Rdst, Rcst], writes=[Rdst])
        P.op("dve", lambda e: e.scalar_tensor_tensor(out=dst[:, :, :], in0=ps[:, :, 0:342], scalar=w0, in1=dst[:, :, :], op0=ALU.mult, op1=ALU.add),
             reads=Rb + [Rdst, Rcst], writes=[Rdst])

    def ffg_task(jj):
        def consume(slot, Rs):
            for j in range(2):
                c = 2 * jj + j
                ps, Rb = feat_major_proj(slot, Rs, j, hT, RhT, 16, ff_slices, ff_src)
                conv_evac(ps, Rb, upg[j], Rupg[j], c)
                P.op("act", lambda e, j=j: e.activation(out=upg[j][:, :, :], in_=upg[j][:, :, :], func=AF.Silu),
                     reads=[Rupg[j]], writes=[Rupg[j]])
        task(blk(w_up_b, jj), 16, 256, consume)

    def ffu_task(jj, g0):
        def consume(slot, Rs):
            for j in range(2):
                c = 2 * jj + j
                ps, Rb = feat_major_proj(slot, Rs, j, hT, RhT, 16, ff_slices, ff_src)
                conv_evac(ps, Rb, upu[j], Rupu[j], 44 + c)
                P.op("dve", lambda e, j=j, c=c: e.tensor_tensor(out=aT[:, c - g0, :].rearrange("p (s n) -> p s n", s=3),
                                                                 in0=upg[j][:, :, :], in1=upu[j][:, :, :], op=ALU.mult),
                     reads=[Rupg[j], Rupu[j]], writes=[RaT[c - g0]])
        task(blk(w_up_b, 22 + jj), 16, 256, consume)

    wd_slices = [(2, 512), (514, 512)]

    Rtl = Res("tbl")

    def table_preload():
        P.op("act", lambda e: e.activation(out=regE[:, 4110:4111], in_=cst[:, C_ONE:C_ONE + 1], func=AF.Ln), reads=[Rcst], writes=[Rtl])

    def wdown_task(g0, gn, cb, last):
        def consume(slot, Rs):
            if last and cb == 0:
                table_preload()
            for j in range(2):
                f = 2 * cb + j
                ps, Rb = feat_major_proj(slot, Rs, j, aT, RaT[0:gn], gn, wd_slices, k_outer=(cb == 0 and j == 0))
                P.op("dve", lambda e, ps=ps, f=f: e.tensor_tensor(out=xres[:, f, 5:1029].rearrange("p (s n) -> p s n", s=2),
                                                                  in0=ps[:, 0:2, 0:512], in1=xres[:, f, 5:1029].rearrange("p (s n) -> p s n", s=2),
                                                                  op=ALU.add), reads=Rb[0:2] + [Rxres[f]], writes=[Rxres[f]])
                if last:
                    sumsq_hook(f, 5, 1024, f == 0)
        task(blk(w_down_b, cb, 44)[:, g0:g0 + gn, :], gn, 256, consume)

    for gi, (g0, gn) in enumerate(groups):
        for jj in range(g0 // 2, (g0 + gn) // 2):
            ffg_task(jj)
            ffu_task(jj, g0)
        for cb in range(8):
            wdown_task(g0, gn, cb, gi == len(groups) - 1)

    own_slices = [(0, 512), (512, 512)]

    def norm3():
        tap("x2", arena[:, 0:16464], Rxres)
        for k in range(16):
            for si in range(2):
                P.alias(RhT3[k][si], RhT2[k])
        for si in range(2):
            P.alias(Rrs2_3[si], Rrs2_3)
        norm_finish_sliced(5, own_slices, C_G3, hT, RhT3, Rrs2_3)
        P.alias(Rppw, RaT)
        P.alias(RpTb, RaT)
        P.op("pool", lambda e: e.dma_start(out=ppw, in_=w_ppv[:, :, :]), writes=[Rppw], dma=True)
        P.op("pool", lambda e: e.dma_start(out=pTb, in_=pTv[:, :, :]), writes=[RpTb], dma=True)
        for r in Rsgt + Rppt:
            P.alias(r, Rupg + Rupu)
    call(norm3)
    pairs = [((psA[:, 0, :], RpsA[0]), (psA[:, 1, :], RpsA[1])),
             ((psA[:, 2, :], RpsA[2]), (psB[:, 0, :], RpsB[0])),
             ((psB[:, 1, :], RpsB[1]), (psB[:, 2, :], RpsB[2]))]
    pr = {"i": 0}
    Rx2 = [[Res("x2_%d_%d" % (f, sl_)) for sl_ in range(2)] for f in range(16)]
    Racc2 = [Res("acc2_0"), Res("acc2_1")]
    Rrs2s = [Res("rs2s0"), Res("rs2s1")]

    def ple_alias():
        for f in range(16):
            for sl_ in range(2):
                P.alias(Rx2[f][sl_], [Rxres[f]])
        for sl_ in range(2):
            P.alias(Racc2[sl_], [Racc])
            P.alias(Rrs2s[sl_], [Rrs2] + Rrs2_3)
    call(ple_alias)

    def pg_task(b, s):
        def consume(slot, Rs):
            for j in range(2):
                f = 2 * b + j
                (pg_, Rg), (pp_, Rp) = pairs[pr["i"] % 3]
                bi = pr["i"] % 2
                pr["i"] += 1
                xs = xres[:, f, 5 + 512 * s:5 + 512 * (s + 1)]
                for k in range(16):
                    P.op("pe", lambda e, k=k, pg_=pg_, j=j: e.matmul(pg_[:, 0:512], lhsT=slot[:, k, j * 128:(j + 1) * 128],
                                                                     rhs=hT[:, k, 512 * s:512 * (s + 1)], start=(k == 0), stop=(k == 15)),
                         reads=[Rs, RhT3[k][s]], writes=[Rg])
                for k in range(2):
                    P.op("pe", lambda e, k=k, pp_=pp_, f=f: e.matmul(pp_[:, 0:512], lhsT=ppw[:, k, f * 128:(f + 1) * 128],
                                                                     rhs=pTb[:, k, 512 * s:512 * (s + 1)], start=(k == 0), stop=(k == 1)),
                         reads=[Rppw, RpTb], writes=[Rp])
                P.op("act", lambda e, pg_=pg_, bi=bi: e.activation(out=sgt[bi], in_=pg_[:, 0:512], func=AF.Sigmoid), reads=[Rg], writes=[Rsgt[bi]])
                P.op("dve", lambda e, pp_=pp_, bi=bi: e.tensor_tensor(out=ppt[bi], in0=pp_[:, 0:512], in1=sgt[bi], op=ALU.mult),
                     reads=[Rp, Rsgt[bi]], writes=[Rppt[bi]])
                P.op("dve", lambda e, bi=bi, xs=xs: e.tensor_tensor(out=xs, in0=xs, in1=ppt[bi], op=ALU.add),
                     reads=[Rppt[bi], Rx2[f][s]], writes=[Rx2[f][s]])
                a_ = acc[:, 512 * s:512 * (s + 1)]
                if f == 0:
                    P.op("act", lambda e, xs=xs, a_=a_: e.activation(out=a_, in_=xs, func=AF.Square), reads=[Rx2[f][s]], writes=[Racc2[s]])
                else:
                    P.op("act", lambda e, xs=xs: e.activation(out=tmpq[:, 0:512], in_=xs, func=AF.Square), reads=[Rx2[f][s]], writes=[Rtmpq])
                    P.op("dve", lambda e, a_=a_: e.tensor_tensor(out=a_, in0=a_, in1=tmpq[:, 0:512], op=ALU.add),
                         reads=[Racc2[s], Rtmpq], writes=[Racc2[s]])
        task(blk(w_pg_b, b), 16, 256, consume)

    def final_rs(s):
        P.op("pe", lambda e: e.matmul(psC[:, 0:512], lhsT=onesF, rhs=acc[:, 512 * s:512 * (s + 1)], start=True, stop=True),
             reads=[Racc2[s], Rcst], writes=[RpsC])
        rstd_from_psum(psC[:, 0:512], rs2[:, 512 * s:512 * (s + 1)], RpsC, Rrs2s[s], 2048.0)

    def final_ops(s, ks):
        for k in ks:
            xs = xres[:, k, 5 + 512 * s:5 + 512 * (s + 1)]
            P.op("dve", lambda e, k=k, xs=xs: e.scalar_tensor_tensor(out=xs, in0=xs, scalar=cs(C_GF + k), in1=rs2[:, 512 * s:512 * (s + 1)],
                                                                      op0=ALU.mult, op1=ALU.mult),
                 reads=[Rx2[k][s], Rrs2s[s], Rcst], writes=[Rx2[k][s]])
            P.op("sp", lambda e, k=k, xs=xs: e.dma_start(out=outTv[:, k, 512 * s:512 * (s + 1)], in_=xs),
                 reads=[Rx2[k][s]], writes=[Res("o")], dma=True)

    for b in range(8):
        pg_task(b, 0)
    call(lambda: final_rs(0))
    for b in range(8):
        pg_task(b, 1)
        call(lambda b=b: final_ops(0, [2 * b, 2 * b + 1]))
    call(table_preload)
    call(lambda: final_rs(1))
    call(lambda: final_ops(1, range(16)))

    def finish():
        Rfin = Res("fin")
        for o in P.ops["sp"]:
            if o.is_dma:
                Rfin.rdma.append(o)
        P.op("sp", lambda e: e.nop(), writes=[Rfin])
    call(finish)

    run_tasks()
    P.emit()
    return nc


def _consts(norm_mix_g, norm_ffn_g, norm_ple_g, final_norm_g, short_conv_w, ffn_conv_w, ffn_conv_b,
            b_igate, b_fgate, mh_norm_g):
    c = np.zeros((128, NCST), np.float32)
    pc = lambda v: np.ascontiguousarray(np.asarray(v, np.float32).reshape(-1, 128).T)
    c[:, C_G1:C_G1 + 16] = pc(norm_mix_g)
    c[:, C_G2:C_G2 + 16] = pc(norm_ffn_g)
    c[:, C_G3:C_G3 + 16] = pc(norm_ple_g)
    c[:, C_GF:C_GF + 16] = pc(final_norm_g)
    scw = np.asarray(short_conv_w, np.float32).reshape(3, 8, 128)
    c[:, C_SCW:C_SCW + 24] = scw.transpose(2, 1, 0).reshape(128, 24)
    fcw = np.asarray(ffn_conv_w, np.float32).reshape(3, 88, 128)
    c[:, C_FCW:C_FCW + 264] = fcw.transpose(2, 1, 0).reshape(128, 264)
    c[:, C_FCB:C_FCB + 88] = pc(ffn_conv_b)
    c[:, C_BIF:C_BIF + 4] = np.asarray(b_igate, np.float32).reshape(1, 4)
    c[:, C_BIF + 4:C_BIF + 8] = np.asarray(b_fgate, np.float32).reshape(1, 4)
    c[:, C_GMH:C_GMH + 1024] = np.asarray(mh_norm_g, np.float32).reshape(1, 1024)
    c[:, C_TRI:C_TRI + 128] = np.triu(np.ones((128, 128), np.float32))
    c[:, C_ONE:C_ONE + 128] = 1.0
    c[:, C_IDN:C_IDN + 128] = np.eye(128, dtype=np.float32)
    return c


def make_in_maps(x, p, norm_mix_g, w_in, b_igate, b_fgate, short_conv_w, mh_norm_g, w_out,
                 norm_ffn_g, w_up, ffn_conv_w, ffn_conv_b, w_down, norm_ple_g, w_pg, w_pp, final_norm_g):
    x = np.asarray(x, np.float32)
    p = np.asarray(p, np.float32)
    cst = _consts(norm_mix_g[0], norm_ffn_g[0], norm_ple_g[0], final_norm_g, short_conv_w[0], ffn_conv_w[0],
                  ffn_conv_b[0], b_igate[0], b_fgate[0], mh_norm_g[0])
    def blocked(w, nb):
        w = np.asarray(w, np.float32)
        kc = w.shape[0] // 128
        return np.ascontiguousarray(w.reshape(kc, 128, nb, 256).transpose(2, 1, 0, 3)).reshape(nb, 128, kc * 256)
    w_in0 = np.asarray(w_in[0], np.float32)
    shared = {
        "w_in_b": blocked(w_in0[:, 0:6144], 24),
        "w_if": np.ascontiguousarray(w_in0[:, 6144:6152].reshape(16, 128, 8).transpose(1, 0, 2)).reshape(128, 128),
        "w_out_b": blocked(w_out[0], 8),
        "w_up_b": blocked(w_up[0], 44),
        "w_down_b": blocked(w_down[0], 8),
        "w_pg_b": blocked(w_pg[0], 8),
        "w_pp": np.ascontiguousarray(np.asarray(w_pp[0], np.float32)),
        "cst": cst,
    }
    maps = []
    for c in range(8):
        b, half = c // 2, c % 2
        xT = np.zeros((2048, 2048), np.float32)
        if half == 1:
            xT[:, 0:1024] = x[b, 0:1024].T
        xT[:, 1024:2048] = x[b, half * 1024:(half + 1) * 1024].T
        pT = np.ascontiguousarray(p[0, b, half * 1024:(half + 1) * 1024].T)
        m = dict(shared)
        x4 = xT.reshape(16, 128, 2048)
        parts = []
        for (c0_, n_, W_) in ((0, 896, 128), (896, 1152, 256)):
            for s_ in range((n_ + W_ - 1) // W_):
                w_ = min(W_, n_ - W_ * s_)
                a0 = c0_ + W_ * s_
                parts.append(x4[:, :, a0:a0 + w_].transpose(1, 0, 2).reshape(128, 16 * w_))
        m["xsl"] = np.ascontiguousarray(np.concatenate(parts, axis=1))
        m["xr"] = np.ascontiguousarray(x4[:, :, 1019:2048].transpose(1, 0, 2)).reshape(128, 16 * 1029)
        m["pT"] = pT
        maps.append(m)
    return maps


def kernel(x, p, norm_mix_g, w_in, b_igate, b_fgate, short_conv_w, mh_norm_g, w_out,
           norm_ffn_g, w_up, ffn_conv_w, ffn_conv_b, w_down, norm_ple_g, w_ple_gate,
           w_ple_proj, final_norm_g):
    maps = make_in_maps(x, p, norm_mix_g, w_in, b_igate, b_fgate, short_conv_w, mh_norm_g, w_out,
                        norm_ffn_g, w_up, ffn_conv_w, ffn_conv_b, w_down, norm_ple_g, w_ple_gate,
                        w_ple_proj, final_norm_g)
    nc = build_nc()
    res = run_bass_kernel_spmd(nc, maps, core_ids=list(range(8)))
    out = np.empty((4, 2048, 2048), np.float32)
    for c in range(8):
        b, half = c // 2, c % 2
        out[b, half * 1024:(half + 1) * 1024, :] = res.results[c]["outT"].T
    return out
```
